# Optimizing a Trainium2 kernel written in Bass

```python
import jax, jax.numpy as jnp
from jax import lax
import numpy as np

D_MODEL = 2048
BATCH = 4
SEQ = 2048
DEPTH = 4

GRID_W = 64
CTX_LEN = 256
N_MIXERS = 2
N_LRU_LAYERS = (DEPTH + 1) // 2
N_S5_LAYERS = DEPTH // 2
LRU_WIDTH = D_MODEL
LRU_HEADS = 8
LRU_BLOCK = LRU_WIDTH // LRU_HEADS
CONV_WIDTH = 4
LRU_C = 8.0
S5_GROUP = 16
S5_GROUPS = D_MODEL // S5_GROUP
S5_STATE = 64
D_FF = -(-8 * D_MODEL // (3 * 256)) * 256
EPS = 1e-6

kernel_name = 'hybrid_rglru_s5_prefix_dit'


def rmsnorm(x, g):
    xf = x.astype(jnp.float32)
    y = xf * lax.rsqrt(jnp.mean(xf * xf, axis=-1, keepdims=True) + EPS) * g.astype(jnp.float32)
    return y.astype(x.dtype)


def modulate(x, g, shift, scale):
    return rmsnorm(x, g) * (1 + scale) + shift


def swiglu(h, w13, w2):
    a, b = jnp.split(h @ w13, 2, axis=-1)
    return (jax.nn.silu(a) * b) @ w2


def depthwise_conv(x, w, b):
    y = lax.conv_general_dilated(x, w[:, None, :], window_strides=(1,), padding=[(1, 2)],
                                 dimension_numbers=('NWC', 'WIO', 'NWC'),
                                 feature_group_count=x.shape[-1])
    return y + b


def linear_scan(a, b, h0, reverse):
    if h0 is not None:
        idx = -1 if reverse else 0
        b = b.at[:, idx].add(a[:, idx] * h0)

    def comb(e1, e2):
        a1, b1 = e1
        a2, b2 = e2
        return a1 * a2, a2 * b1 + b2

    return lax.associative_scan(comb, (a, b), axis=1, reverse=reverse)[1]


def lru_gates(xc, wa, ba, wx, bx, lam):
    bsz, L, _ = xc.shape
    xh = xc.reshape(bsz, L, LRU_HEADS, LRU_BLOCK)
    r = jax.nn.sigmoid(jnp.einsum('blhi,hij->blhj', xh, wa).reshape(bsz, L, LRU_WIDTH) + ba)
    i = jax.nn.sigmoid(jnp.einsum('blhi,hij->blhj', xh, wx).reshape(bsz, L, LRU_WIDTH) + bx)
    log_a = -LRU_C * r * jax.nn.softplus(-lam)
    a = jnp.exp(log_a)
    beta = jnp.sqrt(-jnp.expm1(2.0 * log_a))
    return a, beta * (i * xc)


def rglru_mixer(h_ctx, h_lat, w_in, conv_w, conv_b, wa, ba, wx, bx, lam, w_out, with_ctx_out):
    f32 = jnp.float32

    def branches(h):
        gate, xb = jnp.split(h @ w_in, 2, axis=-1)
        return gate, depthwise_conv(xb, conv_w, conv_b).astype(f32)

    gate_c, x_c = branches(h_ctx)
    gate_l, x_l = branches(h_lat)
    ys_c, ys_l = [], []
    for d, rev in enumerate((False, True)):
        p = (wa[d].astype(f32), ba[d].astype(f32), wx[d].astype(f32), bx[d].astype(f32), lam[d].astype(f32))
        a_c, u_c = lru_gates(x_c, *p)
        s_c = linear_scan(a_c, u_c, None, rev)
        h0 = s_c[:, 0] if rev else s_c[:, -1]
        a_l, u_l = lru_gates(x_l, *p)
        ys_l.append(linear_scan(a_l, u_l, h0, rev))
        ys_c.append(s_c)
    y_l = (ys_l[0] + ys_l[1]).astype(h_lat.dtype)
    o_l = (y_l * jax.nn.gelu(gate_l)) @ w_out
    o_c = None
    if with_ctx_out:
        y_c = (ys_c[0] + ys_c[1]).astype(h_ctx.dtype)
        o_c = (y_c * jax.nn.gelu(gate_c)) @ w_out
    return o_c, o_l


def s5_discretise(lam_re, lam_im, log_step, b_re, b_im):
    lr = jnp.minimum(lam_re, -1e-4)
    li = lam_im
    step = jnp.exp(log_step)[:, None]
    er = jnp.exp(lr * step)
    ar = er * jnp.cos(li * step)
    ai = er * jnp.sin(li * step)
    den = lr * lr + li * li
    cr = ((ar - 1.0) * lr + ai * li) / den
    ci = (ai * lr - (ar - 1.0) * li) / den
    bbr = cr[..., None] * b_re - ci[..., None] * b_im
    bbi = cr[..., None] * b_im + ci[..., None] * b_re
    return ar, ai, bbr, bbi


def complex_scan(ar, ai, ur, ui, h0, reverse):
    L = ur.shape[1]
    a_r = jnp.broadcast_to(ar, (1, L) + ar.shape)
    a_i = jnp.broadcast_to(ai, (1, L) + ai.shape)
    if h0 is not None:
        hr, hi = h0
        idx = -1 if reverse else 0
        ur = ur.at[:, idx].add(ar * hr - ai * hi)
        ui = ui.at[:, idx].add(ar * hi + ai * hr)

    def comb(e1, e2):
        ar1, ai1, br1, bi1 = e1
        ar2, ai2, br2, bi2 = e2
        return (ar1 * ar2 - ai1 * ai2, ar1 * ai2 + ai1 * ar2,
                ar2 * br1 - ai2 * bi1 + br2, ar2 * bi1 + ai2 * br1 + bi2)

    _, _, sr, si = lax.associative_scan(comb, (a_r, a_i, ur, ui), axis=1, reverse=reverse)
    return sr, si


def s5_mixer(h_ctx, h_lat, lam_re, lam_im, log_step, b_re, b_im, c_re, c_im, d_skip, w_glu, with_ctx_out):
    f32 = jnp.float32
    bsz, L, D = h_lat.shape
    rows = L // GRID_W
    u_l = h_lat.reshape(bsz, rows, GRID_W, D).transpose(0, 2, 1, 3).reshape(bsz, L, D)

    def drive(u, bbr, bbi):
        ug = u.astype(f32).reshape(u.shape[0], u.shape[1], S5_GROUPS, S5_GROUP)
        return (jnp.einsum('blgh,gph->blgp', ug, bbr), jnp.einsum('blgh,gph->blgp', ug, bbi))

    def readout(sr, si, cr, ci):
        y = jnp.einsum('blgp,ghp->blgh', sr, cr) - jnp.einsum('blgp,ghp->blgh', si, ci)
        return y.reshape(y.shape[0], y.shape[1], D_MODEL)

    ys_c, ys_l = [], []
    for d, rev in enumerate((False, True)):
        ar, ai, bbr, bbi = s5_discretise(lam_re[d].astype(f32), lam_im[d].astype(f32), log_step[d].astype(f32),
                                         b_re[d].astype(f32), b_im[d].astype(f32))
        cr, ci = c_re[d].astype(f32), c_im[d].astype(f32)
        ur_c, ui_c = drive(h_ctx, bbr, bbi)
        sr_c, si_c = complex_scan(ar, ai, ur_c, ui_c, None, rev)
        idx = 0 if rev else -1
        ur_l, ui_l = drive(u_l, bbr, bbi)
        sr_l, si_l = complex_scan(ar, ai, ur_l, ui_l, (sr_c[:, idx], si_c[:, idx]), rev)
        ys_l.append(readout(sr_l, si_l, cr, ci))
        if with_ctx_out:
            ys_c.append(readout(sr_c, si_c, cr, ci))

    def glu_out(y, u):
        y = jax.nn.gelu(y + d_skip.astype(f32) * u.astype(f32)).astype(u.dtype)
        a, g = jnp.split(y @ w_glu, 2, axis=-1)
        return a * jax.nn.sigmoid(g)

    o_l = glu_out(ys_l[0] + ys_l[1], u_l)
    o_l = o_l.reshape(bsz, GRID_W, rows, D).transpose(0, 2, 1, 3).reshape(bsz, L, D)
    o_c = glu_out(ys_c[0] + ys_c[1], h_ctx) if with_ctx_out else None
    return o_c, o_l


def setup_inputs(seed: int = 0) -> dict:
    key = jax.random.key(seed)
    ks = jax.random.split(key, 32)
    f32 = jnp.float32

    def nrm(k, shape, scale):
        return jax.random.normal(k, shape, f32) * scale

    D = D_MODEL
    a0 = jax.random.uniform(ks[20], (N_LRU_LAYERS, 2, LRU_WIDTH), f32, 0.9, 0.999)
    s = a0 ** (1.0 / LRU_C)
    lru_lam = jnp.log(s) - jnp.log1p(-s)
    s5_shape = (N_S5_LAYERS, 2, S5_GROUPS, S5_STATE)
    n_idx = jnp.arange(S5_STATE, dtype=f32)
    return {
        'x': nrm(ks[0], (BATCH, SEQ, D), 1.0),
        'c': nrm(ks[1], (BATCH, D), 1.0),
        'ctx': nrm(ks[2], (BATCH, CTX_LEN, D), 1.0),
        'c_ctx': nrm(ks[3], (D,), 1.0),
        'ada_w': nrm(ks[4], (DEPTH, D, 6 * D), 0.5 * D ** -0.5),
        'ada_b': nrm(ks[5], (DEPTH, 6 * D), 0.02),
        'norm1_g': 1.0 + nrm(ks[6], (DEPTH, D), 0.02),
        'norm2_g': 1.0 + nrm(ks[7], (DEPTH, D), 0.02),
        'final_g': 1.0 + nrm(ks[8], (D,), 0.02),
        'ffn_w13': nrm(ks[9], (DEPTH, D, 2 * D_FF), D ** -0.5),
        'ffn_w2': nrm(ks[10], (DEPTH, D_FF, D), D_FF ** -0.5),
        'lru_w_in': nrm(ks[11], (N_LRU_LAYERS, D, 2 * LRU_WIDTH), D ** -0.5),
        'lru_conv_w': nrm(ks[12], (N_LRU_LAYERS, CONV_WIDTH, LRU_WIDTH), CONV_WIDTH ** -0.5),
        'lru_conv_b': nrm(ks[13], (N_LRU_LAYERS, LRU_WIDTH), 0.01),
        'lru_wa': nrm(ks[14], (N_LRU_LAYERS, 2, LRU_HEADS, LRU_BLOCK, LRU_BLOCK), LRU_BLOCK ** -0.5),
        'lru_ba': nrm(ks[15], (N_LRU_LAYERS, 2, LRU_WIDTH), 0.01),
        'lru_wx': nrm(ks[16], (N_LRU_LAYERS, 2, LRU_HEADS, LRU_BLOCK, LRU_BLOCK), LRU_BLOCK ** -0.5),
        'lru_bx': nrm(ks[17], (N_LRU_LAYERS, 2, LRU_WIDTH), 0.01),
        'lru_lam': lru_lam,
        'lru_w_out': nrm(ks[18], (N_LRU_LAYERS, LRU_WIDTH, D), LRU_WIDTH ** -0.5),
        's5_lam_re': -0.5 + nrm(ks[19], s5_shape, 0.01),
        's5_lam_im': jnp.pi * n_idx + nrm(ks[21], s5_shape, 0.01),
        's5_log_step': jax.random.uniform(ks[22], (N_S5_LAYERS, 2, S5_GROUPS), f32,
                                          float(np.log(1e-3)), float(np.log(1e-1))),
        's5_b_re': nrm(ks[23], (N_S5_LAYERS, 2, S5_GROUPS, S5_STATE, S5_GROUP), (2 * S5_GROUP) ** -0.5),
        's5_b_im': nrm(ks[24], (N_S5_LAYERS, 2, S5_GROUPS, S5_STATE, S5_GROUP), (2 * S5_GROUP) ** -0.5),
        's5_c_re': nrm(ks[25], (N_S5_LAYERS, 2, S5_GROUPS, S5_GROUP, S5_STATE), 2 ** -0.5),
        's5_c_im': nrm(ks[26], (N_S5_LAYERS, 2, S5_GROUPS, S5_GROUP, S5_STATE), 2 ** -0.5),
        's5_d': nrm(ks[27], (N_S5_LAYERS, D), 1.0),
        's5_w_glu': nrm(ks[28], (N_S5_LAYERS, D, 2 * D), D ** -0.5),
    }


def reference(x, c, ctx, c_ctx, ada_w, ada_b, norm1_g, norm2_g, final_g, ffn_w13, ffn_w2,
              lru_w_in, lru_conv_w, lru_conv_b, lru_wa, lru_ba, lru_wx, lru_bx, lru_lam, lru_w_out,
              s5_lam_re, s5_lam_im, s5_log_step, s5_b_re, s5_b_im, s5_c_re, s5_c_im, s5_d, s5_w_glu):
    silu_c = jax.nn.silu(c)
    silu_cc = jax.nn.silu(c_ctx)
    for layer in range(DEPTH):
        last = layer == DEPTH - 1
        mod_l = (silu_c @ ada_w[layer] + ada_b[layer])[:, None, :]
        mod_c = silu_cc @ ada_w[layer] + ada_b[layer]
        sh1_l, sc1_l, g1_l, sh2_l, sc2_l, g2_l = jnp.split(mod_l, 6, axis=-1)
        sh1_c, sc1_c, g1_c, sh2_c, sc2_c, g2_c = jnp.split(mod_c, 6, axis=-1)
        h_l = modulate(x, norm1_g[layer], sh1_l, sc1_l)
        h_c = modulate(ctx, norm1_g[layer], sh1_c, sc1_c)
        j = layer // N_MIXERS
        if layer % N_MIXERS == 0:
            o_c, o_l = rglru_mixer(h_c, h_l, lru_w_in[j], lru_conv_w[j], lru_conv_b[j], lru_wa[j], lru_ba[j],
                                   lru_wx[j], lru_bx[j], lru_lam[j], lru_w_out[j], not last)
        else:
            o_c, o_l = s5_mixer(h_c, h_l, s5_lam_re[j], s5_lam_im[j], s5_log_step[j], s5_b_re[j], s5_b_im[j],
                                s5_c_re[j], s5_c_im[j], s5_d[j], s5_w_glu[j], not last)
        x = x + g1_l * o_l
        x = x + g2_l * swiglu(modulate(x, norm2_g[layer], sh2_l, sc2_l), ffn_w13[layer], ffn_w2[layer])
        if not last:
            ctx = ctx + g1_c * o_c
            ctx = ctx + g2_c * swiglu(modulate(ctx, norm2_g[layer], sh2_c, sc2_c), ffn_w13[layer], ffn_w2[layer])
    return rmsnorm(x, final_g)
```

```python
import contextlib
import math
import numpy as np
import concourse.bass as bass
import concourse.mybir as mybir
from concourse.bass_utils import run_bass_kernel_spmd

F32 = mybir.dt.float32
BF16 = mybir.dt.bfloat16
I32 = mybir.dt.int32
AF = mybir.ActivationFunctionType
ALU = mybir.AluOpType

D = 2048
T = 2304
NCTX = 256
DFF = 5632
NL = 4
EPS = 1e-6


class Dep:
    __slots__ = ("w", "r")

    def __init__(self):
        self.w = None
        self.r = []


class P:
    NDMA = 40

    def __init__(self):
        nc = self.nc = bass.Bass("TRN2", target_bir_lowering=False)
        self.es = contextlib.ExitStack()
        self.eng = {"pe": nc.tensor, "act": nc.scalar, "dve": nc.vector, "pool": nc.gpsimd, "sp": nc.sync}
        self.sem = {e: self.es.enter_context(nc.semaphore("s_" + e)) for e in self.eng}
        self.cnt = {e: 0 for e in self.eng}
        self.known = {e: {} for e in self.eng}
        self.dsem = [self.es.enter_context(nc.semaphore("d%d" % i)) for i in range(self.NDMA)]
        self.duse = [0] * self.NDMA
        self.di = 0
        self.ninst = 0
        self.uid = 0

    def sb(self, name, shape, dt, st=None):
        self.uid += 1
        return (st or self.es).enter_context(self.nc.sbuf_tensor("%s_%d" % (name, self.uid), list(shape), dt))

    def ps(self, name, shape, dt):
        return self.es.enter_context(self.nc.psum_tensor(name, list(shape), dt))

    def _wait(self, e, ev):
        if ev is None:
            return
        sem, val = ev
        k = self.known[e]
        if k.get(sem.num, 0) >= val:
            return
        self.eng[e].wait_ge(sem, val)
        k[sem.num] = val

    def _deps(self, e, reads, writes):
        for d in reads:
            self._wait(e, d.w)
        for d in writes:
            self._wait(e, d.w)
            for ev in d.r:
                self._wait(e, ev)

    def _commit(self, ev, reads, writes):
        for d in reads:
            d.r.append(ev)
            if len(d.r) > 16:
                best = {}
                for s, v in d.r:
                    if best.get(s.num, (None, -1))[1] < v:
                        best[s.num] = (s, v)
                d.r = list(best.values())
        for d in writes:
            d.w = ev
            d.r = []

    def op(self, e, make, reads=(), writes=()):
        self._deps(e, reads, writes)
        inst = make(self.eng[e])
        self.cnt[e] += 1
        ev = (self.sem[e], self.cnt[e])
        inst.then_inc(self.sem[e], 1)
        self._commit(ev, reads, writes)
        self.ninst += 1
        return ev

    def mm(self, steps, reads=(), writes=()):
        self._deps("pe", reads, writes)
        inst = None
        for mk in steps:
            inst = mk(self.eng["pe"])
        self.cnt["pe"] += 1
        ev = (self.sem["pe"], self.cnt["pe"])
        inst.then_inc(self.sem["pe"], 1)
        self._commit(ev, reads, writes)
        self.ninst += len(steps)
        return ev

    def dma(self, out, in_, reads=(), writes=(), q="sp", **kw):
        i = self.di
        self.di = (self.di + 1) % self.NDMA
        if self.duse[i] > 0:
            self._wait(q, (self.dsem[i], 16 * self.duse[i]))
        self._deps(q, reads, writes)
        inst = self.eng[q].dma_start(out=out, in_=in_, **kw)
        self.duse[i] += 1
        ev = (self.dsem[i], 16 * self.duse[i])
        inst.then_inc(self.dsem[i], 16)
        self._commit(ev, reads, writes)
        self.ninst += 1
        return ev

    def barrier(self):
        evs = [(self.sem[e], self.cnt[e]) for e in self.eng if self.cnt[e] > 0]
        evs += [(self.dsem[i], 16 * self.duse[i]) for i in range(self.NDMA) if self.duse[i] > 0]
        for e in self.eng:
            for ev in evs:
                self._wait(e, ev)

    @contextlib.contextmanager
    def phase(self):
        st = contextlib.ExitStack()
        try:
            yield st
        finally:
            self.barrier()
            st.close()

    def finish(self):
        self.barrier()
        self.es.close()


def build(debug=None, nlayers=NL):
    p = P()
    nc = p.nc

    def din(name, shape):
        return nc.dram_tensor(name, list(shape), F32, kind="ExternalInput").ap()

    xin = din("xin", [T, D])
    cvec = din("cvec", [2, D])
    ada_w = din("ada_w", [NL, D, 6 * D])
    ada_b = din("ada_b", [NL, 6 * D])
    norm1_g = din("norm1_g", [NL, D])
    norm2_g = din("norm2_g", [NL, D])
    final_g = din("final_g", [D])
    ffn_w13 = din("ffn_w13", [NL, D, 2 * DFF])
    ffn_w2 = din("ffn_w2", [NL, DFF, D])
    lru_w_in = din("lru_w_in", [2, D, 2 * D])
    lru_conv_w = din("lru_conv_w", [2, 4, D])
    lru_conv_b = din("lru_conv_b", [2, D])
    lru_wa = din("lru_wa", [2, 2, 8, 256, 256])
    lru_ba = din("lru_ba", [2, 2, D])
    lru_wx = din("lru_wx", [2, 2, 8, 256, 256])
    lru_bx = din("lru_bx", [2, 2, D])
    lru_lam = din("lru_lam", [2, 2, D])
    lru_w_out = din("lru_w_out", [2, D, D])
    s5_lam_re = din("s5_lam_re", [2, 2, 128, 64])
    s5_lam_im = din("s5_lam_im", [2, 2, 128, 64])
    s5_log_step = din("s5_log_step", [2, 2, 128])
    s5_b_re = din("s5_b_re", [2, 2, 128, 64, 16])
    s5_b_im = din("s5_b_im", [2, 2, 128, 64, 16])
    s5_c_re = din("s5_c_re", [2, 2, 128, 16, 64])
    s5_c_im = din("s5_c_im", [2, 2, 128, 16, 64])
    s5_d = din("s5_d", [2, D])
    s5_w_glu = din("s5_w_glu", [2, D, 2 * D])
    identin = din("identin", [128, 128])
    maskin = din("maskin", [2, 128, 128])
    out = nc.dram_tensor("out", [T - NCTX, D], F32, kind="ExternalOutput").ap()
    dbg = None
    if debug:
        dbg = nc.dram_tensor("dbg", [T, D], F32, kind="ExternalOutput").ap()
        dbgmod = nc.dram_tensor("dbgmod", [NL, 2, 6 * D], F32, kind="ExternalOutput").ap()

    XS = nc.dram_tensor("XS", [T, D], F32, kind="Internal").ap()
    MOD = nc.dram_tensor("MOD", [NL, 2, 6 * D], F32, kind="Internal").ap()
    ZTD = nc.dram_tensor("ZTD", [128, 16, T], BF16, kind="Internal").ap()
    VD = nc.dram_tensor("VD", [128, 128, 288], BF16, kind="Internal").ap()
    YD = nc.dram_tensor("YD", [128, 128, 288], BF16, kind="Internal").ap()
    class _Fresh:
        def __getitem__(self, i):
            return Dep()
    XD = _Fresh()
    MODD = Dep()
    ZTDD = Dep()
    VDD = Dep()
    YDD = Dep()

    ident = p.sb("ident", [128, 128], BF16)
    identf = p.sb("identf", [128, 128], F32)
    ones1 = p.sb("ones1", [1, 128], F32)
    cD = Dep()
    p.dma(ident[:], identin[:, :], writes=[cD], q="pool")
    p.dma(identf[:], identin[:, :], writes=[cD])
    p.op("dve", lambda e: e.memset(ones1[:], 1.0), writes=[cD])
    psA = [(p.ps("psA%d" % i, [128, 512], F32), Dep()) for i in range(6)]
    psT = [(p.ps("psT%d" % i, [128, 8, 128], BF16), Dep()) for i in range(2)]
    rot = {"A": 0, "T": 0}

    def bankA():
        rot["A"] = (rot["A"] + 1) % 6
        return psA[rot["A"]]

    def bankT():
        rot["T"] = (rot["T"] + 1) % 2
        return psT[rot["T"]]

    for t in range(18):
        p.dma(XS[t * 128:(t + 1) * 128, :], xin[t * 128:(t + 1) * 128, :], writes=[XD[t]])

    class Rot:
        def __init__(self, st, name, shape, dt, n):
            self.b = [(p.sb(name, shape, dt, st), Dep()) for _ in range(n)]
            self.i = 0

        def get(self):
            self.i = (self.i + 1) % len(self.b)
            return self.b[self.i]

    def bcast_rows(st, specs):
        res = []
        dsts = [(p.sb("bc", [128, D], F32, st), Dep()) for _ in specs]
        tmp_st = contextlib.ExitStack()
        HD = D // 2
        r0 = Rot(tmp_st, "r0", [1, HD], F32, 1)
        r1 = Rot(tmp_st, "r1", [1, HD], F32, 1)
        for si, sp in enumerate(specs):
            dst, dd = dsts[si]
            for hf in range(2):
                hsl = slice(hf * HD, (hf + 1) * HD)
                a, ad = r0.get()
                p.dma(a[:], sp[1][:, hsl], reads=[MODD], writes=[ad])
                if sp[0] == "g1p":
                    b, bd = r1.get()
                    p.dma(b[:], sp[2][:, hsl], reads=[MODD], writes=[bd])
                    p.op("dve", lambda e: e.scalar_tensor_tensor(out=a[:], in0=b[:], scalar=1.0, in1=a[:], op0=ALU.add, op1=ALU.mult),
                         reads=[ad, bd], writes=[ad])
                for k in range(2):
                    bk, bkd = bankA()
                    p.op("pe", lambda e: e.matmul(bk[:], lhsT=ones1[:], rhs=a[0:1, k * 512:(k + 1) * 512], start=True, stop=True),
                         reads=[ad, cD], writes=[bkd])
                    c0 = hf * HD + k * 512
                    p.op("act", lambda e: e.activation(out=dst[:, c0:c0 + 512], in_=bk[:], func=AF.Copy), reads=[bkd], writes=[dd])
            res.append((dst, dd))
        p.barrier()
        tmp_st.close()
        return res

    def modrow(L, v, idx):
        return MOD[L, v:v + 1, idx * D:(idx + 1) * D]

    def norm_mod(st_bufs, xt, xd, npart, Mt, Md, SHt, SHd, outap, outd, v3=False):
        jr, ssr, rsr = st_bufs
        junk, jd = jr.get()
        ss, ssd = ssr.get()
        rs, rsd = rsr.get()
        p.op("act", lambda e: e.activation(out=junk[:npart], in_=xt, func=AF.Square, accum_out=ss[:npart]), reads=[xd], writes=[jd, ssd])
        p.op("dve", lambda e: e.tensor_scalar(out=rs[:npart], in0=ss[:npart], scalar1=1.0 / D, scalar2=EPS, op0=ALU.mult, op1=ALU.add),
             reads=[ssd], writes=[rsd])
        p.op("act", lambda e: e.activation(out=rs[:npart], in_=rs[:npart], func=AF.Sqrt), reads=[rsd], writes=[rsd])
        p.op("dve", lambda e: e.reciprocal(out=rs[:npart], in_=rs[:npart]), reads=[rsd], writes=[rsd])
        p.op("dve", lambda e: e.scalar_tensor_tensor(out=junk[:npart], in0=xt, scalar=rs[:npart, 0:1], in1=Mt[:npart],
                                                     op0=ALU.mult, op1=ALU.mult), reads=[xd, rsd, Md], writes=[jd])
        i0, i1 = junk[:npart], SHt[:npart]
        if v3:
            i0 = i0.rearrange("p (g h) -> p g h", h=16)
            i1 = i1.rearrange("p (g h) -> p g h", h=16)
        p.op("dve", lambda e: e.tensor_tensor(out=outap, in0=i0, in1=i1, op=ALU.add), reads=[jd, SHd], writes=[outd])

    def nm_bufs(st, nj=2):
        return (Rot(st, "junk", [128, D], F32, nj), Rot(st, "ss", [128, 1], F32, 3), Rot(st, "rs", [128, 1], F32, 3))

    def to_fm(hb, hbd, ntok, HT, HTd, tok0):
        for kb in range(2):
            bk, bkd = bankT()
            for kk in range(8):
                k = kb * 8 + kk
                p.op("pe", lambda e: e.transpose(bk[:, kk, 0:ntok], hb[0:ntok, k * 128:(k + 1) * 128], ident[0:ntok, 0:ntok]),
                     reads=[hbd, cD], writes=[bkd])
            eng = "act" if kb == 0 else "dve"
            if eng == "act":
                p.op("act", lambda e: e.activation(out=HT[:, kb * 8:(kb + 1) * 8, tok0:tok0 + ntok], in_=bk[:, :, 0:ntok], func=AF.Copy),
                     reads=[bkd], writes=[HTd])
            else:
                p.op("dve", lambda e: e.tensor_copy(HT[:, kb * 8:(kb + 1) * 8, tok0:tok0 + ntok], bk[:, :, 0:ntok]), reads=[bkd], writes=[HTd])

    sT = p.sb("sT", [128, 32], BF16)
    sTd = Dep()
    with p.phase() as st:
        c32 = p.sb("c32", [32, 128], F32, st)
        c32d = Dep()
        p.dma(c32[:], cvec.rearrange("v (k q) -> (v k) q", q=128), writes=[c32d])
        bk, bkd = bankA()
        p.op("pe", lambda e: e.transpose(bk[:, 0:32], c32[:, :], identf[0:32, 0:32]), reads=[c32d, cD], writes=[bkd])
        p.op("act", lambda e: e.activation(out=sT[:], in_=bk[:, 0:32], func=AF.Silu), reads=[bkd], writes=[sTd])

    def adaln_tile(L, nt, w3, wd, br, mr):
        cs = slice(nt * 512, (nt + 1) * 512)
        p.dma(w3, ada_w[L, :, cs].rearrange("(k q) n -> q k n", q=128), writes=[wd], q="pool")
        b_, bd_ = br.get()
        p.dma(b_[:], ada_b[L:L + 1, cs].broadcast_to([2, 512]), writes=[bd_])
        bk, bkd = bankA()
        p.mm([(lambda e, k=k: e.matmul(bk[0:2, :], lhsT=sT[:, k:32:16], rhs=w3[:, k, :], start=(k == 0), stop=(k == 15))) for k in range(16)],
             reads=[sTd, wd], writes=[bkd])
        m_, md_ = mr.get()
        p.op("dve", lambda e: e.tensor_tensor(out=m_[:], in0=bk[0:2, :], in1=b_[:], op=ALU.add), reads=[bkd, bd_], writes=[md_])
        p.dma(MOD[L, :, cs], m_[:], reads=[md_], writes=[MODD])
        if debug:
            p.dma(dbgmod[L, :, cs], m_[:], reads=[md_])

    with p.phase() as st:
        wr = Rot(st, "adaw", [128, 16, 512], BF16, 3)
        br = Rot(st, "adab", [2, 512], F32, 3)
        mr = Rot(st, "adam", [2, 512], F32, 3)
        for nt in range(24):
            w, wd = wr.get()
            adaln_tile(0, nt, w[:], wd, br, mr)

    def ffn(L, last):
        tl = list(range(2, 18)) if last else list(range(18))
        half = len(tl) // 2
        ada_pending = list(range(24)) if (L + 1 < nlayers) else []
        for part in (tl[:half], tl[half:]):
            nt_ = len(part) * 128
            subs = []
            o_ = 0
            while o_ < nt_:
                n_ = min(384, nt_ - o_)
                subs.append((o_, n_))
                o_ += n_
            kinds = sorted(set(1 if ti < 2 else 0 for ti in part))
            with p.phase() as stG:
                GT = p.sb("GT", [128, 44, nt_], BF16, stG)
                GTd = [Dep() for _ in range(44)]
                with p.phase() as stH:
                    HT = p.sb("HT", [128, 16, nt_], BF16, stH)
                    HTd = Dep()
                    with p.phase() as st2:
                        nb = nm_bufs(st2, 1)
                        xr = Rot(st2, "xt", [128, D], F32, 3)
                        hr = Rot(st2, "hb", [128, D], BF16, 2)
                        for v in kinds:
                            with p.phase() as stk:
                                (Mt, Md), (SHt, SHd) = bcast_rows(stk, [("g1p", norm2_g[L:L + 1, :], modrow(L, v, 4)), ("row", modrow(L, v, 3))])
                                for i, ti in enumerate(part):
                                    if (1 if ti < 2 else 0) != v:
                                        continue
                                    xt, xd = xr.get()
                                    p.dma(xt[:], XS[ti * 128:(ti + 1) * 128, :], reads=[XD[ti]], writes=[xd])
                                    hb, hbd = hr.get()
                                    norm_mod(nb, xt[:], xd, 128, Mt, Md, SHt, SHd, hb[:], hbd)
                                    to_fm(hb, hbd, 128, HT, HTd, i * 128)
                    with p.phase() as st2:
                        wr = Rot(st2, "w13", [128, 16, 2, 256], BF16, 3)
                        sr = Rot(st2, "sa", [128, 384], F32, 3)
                        abr = Rot(st2, "adab", [2, 512], F32, 2)
                        amr = Rot(st2, "adam", [2, 512], F32, 2)
                        for fg in range(22):
                            if ada_pending:
                                aw, awd = wr.get()
                                adaln_tile(L + 1, ada_pending.pop(0), aw[:, :, :, :].rearrange("p k a n -> p k (a n)"), awd, abr, amr)
                            w, wd = wr.get()
                            p.dma(w[:, :, 0, :], ffn_w13[L, :, fg * 256:(fg + 1) * 256].rearrange("(k q) n -> q k n", q=128), writes=[wd], q="pool")
                            p.dma(w[:, :, 1, :], ffn_w13[L, :, DFF + fg * 256:DFF + (fg + 1) * 256].rearrange("(k q) n -> q k n", q=128),
                                  writes=[wd], q="pool")
                            for c2 in range(2):
                                f = fg * 2 + c2
                                for (o_, n_) in subs:
                                    ba_, bad = bankA()
                                    bb_, bbd = bankA()
                                    p.mm([(lambda e, k=k: e.matmul(ba_[:, 0:n_], lhsT=w[:, k, 0, c2 * 128:(c2 + 1) * 128], rhs=HT[:, k, o_:o_ + n_],
                                                                  start=(k == 0), stop=(k == 15))) for k in range(16)], reads=[wd, HTd], writes=[bad])
                                    p.mm([(lambda e, k=k: e.matmul(bb_[:, 0:n_], lhsT=w[:, k, 1, c2 * 128:(c2 + 1) * 128], rhs=HT[:, k, o_:o_ + n_],
                                                                  start=(k == 0), stop=(k == 15))) for k in range(16)], reads=[wd, HTd], writes=[bbd])
                                    sa, sad = sr.get()
                                    p.op("act", lambda e: e.activation(out=sa[:, 0:n_], in_=ba_[:, 0:n_], func=AF.Silu), reads=[bad], writes=[sad])
                                    p.op("dve", lambda e: e.tensor_tensor(out=GT[:, f, o_:o_ + n_], in0=sa[:, 0:n_], in1=bb_[:, 0:n_], op=ALU.mult),
                                         reads=[sad, bbd], writes=[GTd[f]])
                with p.phase() as st2:
                    g2 = {}
                    for v in kinds:
                        (g2[v],) = bcast_rows(st2, [("row", modrow(L, v, 5))])
                    wr = Rot(st2, "w2", [128, 22, 512], BF16, 3)
                    xr = Rot(st2, "xp", [128, 512], F32, 2)
                    tr = Rot(st2, "tp", [128, 512], F32, 2)
                    for nt in range(4):
                        wh = [wr.get(), wr.get()]
                        for kq in range(4):
                            w_, wd_ = wh[kq // 2]
                            p.dma(w_[:, (kq % 2) * 11:(kq % 2 + 1) * 11, :],
                                  ffn_w2[L, kq * 1408:(kq + 1) * 1408, nt * 512:(nt + 1) * 512].rearrange("(k q) n -> q k n", q=128), writes=[wd_], q="pool")
                        for i, ti in enumerate(part):
                            G2t, G2d = g2[1 if ti < 2 else 0]
                            bk, bkd = bankA()
                            p.mm([(lambda e, k=k: e.matmul(bk[:, :], lhsT=GT[:, k, i * 128:(i + 1) * 128], rhs=wh[k // 22][0][:, k % 22, :],
                                                          start=(k == 0), stop=(k == 43))) for k in range(44)], reads=[wh[0][1], wh[1][1]] + GTd, writes=[bkd])
                            xp, xpd = xr.get()
                            p.dma(xp[:], XS[ti * 128:(ti + 1) * 128, nt * 512:(nt + 1) * 512], reads=[XD[ti]], writes=[xpd])
                            tp, tpd = tr.get()
                            p.op("dve", lambda e: e.tensor_tensor(out=tp[:], in0=bk[:, :], in1=G2t[:, nt * 512:(nt + 1) * 512], op=ALU.mult),
                                 reads=[bkd, G2d], writes=[tpd])
                            p.op("dve", lambda e: e.tensor_tensor(out=tp[:], in0=tp[:], in1=xp[:], op=ALU.add), reads=[tpd, xpd], writes=[tpd])
                            p.dma(XS[ti * 128:(ti + 1) * 128, nt * 512:(nt + 1) * 512], tp[:], reads=[tpd], writes=[XD[ti]])

    def out_proj(L, last, wsrc, glu, tiles):
        OW = 256 if glu else 512
        with p.phase() as st:
            ZT = p.sb("ZT", [128, 16, T], BF16, st)
            ZTd = Dep()
            for k in range(16):
                p.dma(ZT[:, k, :], ZTD[:, k, :], reads=[Dep()], writes=[ZTd])
            g1 = {}
            for v in ((0, 1) if not last else (0,)):
                (g1[v],) = bcast_rows(st, [("row", modrow(L, v, 2))])
            wr = Rot(st, "wo", [128, 16, 512], BF16, 2)
            xr = Rot(st, "xp", [128, OW], F32, 3)
            tr = Rot(st, "tp", [128, OW], F32, 3)
            sr = Rot(st, "sg", [128, 256], F32, 3)
            def wload(nt):
                w, wd = wr.get()
                if glu:
                    p.dma(w[:, :, 0:256], wsrc[:, nt * 256:(nt + 1) * 256].rearrange("(k q) n -> q k n", q=128), writes=[wd], q="pool")
                    p.dma(w[:, :, 256:512], wsrc[:, D + nt * 256:D + (nt + 1) * 256].rearrange("(k q) n -> q k n", q=128), writes=[wd], q="pool")
                else:
                    p.dma(w[:, :, :], wsrc[:, nt * 512:(nt + 1) * 512].rearrange("(k q) n -> q k n", q=128), writes=[wd], q="pool")
                return w, wd
            nxt = wload(0)
            for nt in range(D // OW):
                w, wd = nxt
                if nt + 1 < D // OW:
                    nxt = wload(nt + 1)
                for (z0, ntok, isctx, rowfn) in tiles:
                    if last and isctx:
                        continue
                    G1t, G1d = g1[isctx]
                    bk, bkd = bankA()
                    p.mm([(lambda e, k=k: e.matmul(bk[0:ntok, :], lhsT=ZT[:, k, z0:z0 + ntok], rhs=w[:, k, :],
                                                  start=(k == 0), stop=(k == 15))) for k in range(16)], reads=[wd, ZTd], writes=[bkd])
                    cs = slice(nt * OW, (nt + 1) * OW)
                    rows = rowfn(cs)
                    xp, xpd = xr.get()
                    for (p0, np_, ap_, dep_) in rows:
                        p.dma(xp[p0:p0 + np_, :], ap_, reads=[dep_], writes=[xpd])
                    tp, tpd = tr.get()
                    if glu:
                        sg, sgd = sr.get()
                        p.op("act", lambda e: e.activation(out=sg[0:ntok], in_=bk[0:ntok, 256:512], func=AF.Sigmoid), reads=[bkd], writes=[sgd])
                        p.op("dve", lambda e: e.tensor_tensor(out=tp[0:ntok], in0=bk[0:ntok, 0:256], in1=sg[0:ntok], op=ALU.mult),
                             reads=[bkd, sgd], writes=[tpd])
                        p.op("dve", lambda e: e.tensor_tensor(out=tp[0:ntok], in0=tp[0:ntok], in1=G1t[0:ntok, cs], op=ALU.mult),
                             reads=[tpd, G1d], writes=[tpd])
                    else:
                        p.op("dve", lambda e: e.tensor_tensor(out=tp[0:ntok], in0=bk[0:ntok, :], in1=G1t[0:ntok, cs], op=ALU.mult),
                             reads=[bkd, G1d], writes=[tpd])
                    p.op("dve", lambda e: e.tensor_tensor(out=tp[0:ntok], in0=tp[0:ntok], in1=xp[0:ntok], op=ALU.add), reads=[tpd, xpd], writes=[tpd])
                    for (p0, np_, ap_, dep_) in rows:
                        p.dma(ap_, tp[p0:p0 + np_, :], reads=[tpd], writes=[dep_], q="pool")

    def lru_layer(L, last):
        j = L // 2
        with p.phase() as st:
            HT = p.sb("HT", [128, 16, T], BF16, st)
            HTd = Dep()
            with p.phase() as st2:
                nb = nm_bufs(st2)
                xr = Rot(st2, "xt", [128, D], F32, 4)
                hr = Rot(st2, "hb", [128, D], BF16, 2)
                for isctx in (1, 0):
                    with p.phase() as st3:
                        (Mt, Md), (SHt, SHd) = bcast_rows(st3, [("g1p", norm1_g[L:L + 1, :], modrow(L, isctx, 1)), ("row", modrow(L, isctx, 0))])
                        for ti in (range(0, 2) if isctx else range(2, 18)):
                            xt, xd = xr.get()
                            p.dma(xt[:], XS[ti * 128:(ti + 1) * 128, :], reads=[XD[ti]], writes=[xd])
                            hb, hbd = hr.get()
                            norm_mod(nb, xt[:], xd, 128, Mt, Md, SHt, SHd, hb[:], hbd)
                            to_fm(hb, hbd, 128, HT, HTd, ti * 128)
            with p.phase() as st2:
                colT = p.sb("colT", [128, 16 * 16], F32, st2)
                colN = [0]
                stage = Rot(st2, "colstage", [16, 128], F32, 2)

                def colload(name, src_row):
                    t_, td = stage.get()
                    p.dma(t_[:], src_row.rearrange("(k q) -> k q", q=128), writes=[td])
                    bk, bkd = bankA()
                    p.op("pe", lambda e: e.transpose(bk[:, 0:16], t_[:, :], identf[0:16, 0:16]), reads=[td, cD], writes=[bkd])
                    o_ = colT[:, colN[0] * 16:(colN[0] + 1) * 16]
                    colN[0] += 1
                    od = Dep()
                    p.op("act", lambda e: e.activation(out=o_[:], in_=bk[:, 0:16], func=AF.Copy), reads=[bkd], writes=[od])
                    return o_, od
                cw = [colload("cw%d" % k, lru_conv_w[j, k, :]) for k in range(4)]
                cb = colload("cb", lru_conv_b[j, :])
                ba = [colload("ba%d" % d, lru_ba[j, d, :]) for d in range(2)]
                bx = [colload("bx%d" % d, lru_bx[j, d, :]) for d in range(2)]
                cl = []
                for d in range(2):
                    lm, lmd = colload("lam%d" % d, lru_lam[j, d, :])
                    p.op("act", lambda e: e.activation(out=lm[:], in_=lm[:], func=AF.Exp, scale=-1.0), reads=[lmd], writes=[lmd])
                    p.op("dve", lambda e: e.tensor_scalar(out=lm[:], in0=lm[:], scalar1=1.0, scalar2=None, op0=ALU.add), reads=[lmd], writes=[lmd])
                    p.op("act", lambda e: e.activation(out=lm[:], in_=lm[:], func=AF.Ln), reads=[lmd], writes=[lmd])
                    p.op("dve", lambda e: e.tensor_scalar(out=lm[:], in0=lm[:], scalar1=-8.0, scalar2=None, op0=ALU.mult), reads=[lmd], writes=[lmd])
                    cl.append((lm, lmd))
                W = T + 6
                OFFC, OFFL = 1, 260
                XB = p.sb("XB", [128, 2, W], F32, st2)
                XBd = [Dep(), Dep()]
                p.op("dve", lambda e: e.memset(XB[:, :, :], 0.0), writes=XBd)
                XC = p.sb("XC", [128, 2, T], F32, st2)
                XCd = [Dep(), Dep()]
                XCb = p.sb("XCb", [128, 2, T], BF16, st2)
                XCbd = [Dep(), Dep()]
                GG = p.sb("GG", [128, 2, T], BF16, st2)
                GGd = [Dep(), Dep()]
                YA = p.sb("YA", [128, 2, T], F32, st2)
                YAd = [Dep(), Dep()]
                wr = Rot(st2, "win", [128, 16, 128], BF16, 2)
                gwr = Rot(st2, "gw", [128, 2, 256], BF16, 4)
                RFr = Rot(st2, "RF", [128, T], F32, 2)
                IFr = Rot(st2, "IF", [128, T], F32, 2)
                AF2 = p.sb("AF2", [128, T], F32, st2)
                AF2d = Dep()
                subt = [(0, 256)] + [(256 + 512 * i, 512) for i in range(4)]

                def xboff(t0):
                    return (OFFC + t0) if t0 < 256 else (OFFL + t0 - 256)
                for hd in range(8):
                    for cc in range(2):
                        ch = hd * 2 + cc
                        for which in range(2):
                            w, wd = wr.get()
                            p.dma(w[:], lru_w_in[j, :, which * D + ch * 128: which * D + (ch + 1) * 128].rearrange("(k q) n -> q k n", q=128),
                                  writes=[wd], q="pool")
                            for (t0, n_) in subt:
                                bk, bkd = bankA()
                                p.mm([(lambda e, k=k: e.matmul(bk[:, 0:n_], lhsT=w[:, k, :], rhs=HT[:, k, t0:t0 + n_], start=(k == 0), stop=(k == 15)))
                                      for k in range(16)], reads=[wd, HTd], writes=[bkd])
                                if which == 0:
                                    p.op("act", lambda e: e.activation(out=GG[:, cc, t0:t0 + n_], in_=bk[:, 0:n_], func=AF.Gelu_apprx_tanh),
                                         reads=[bkd], writes=[GGd[cc]])
                                else:
                                    o0 = xboff(t0)
                                    p.op("act", lambda e: e.activation(out=XB[:, cc, o0:o0 + n_], in_=bk[:, 0:n_], func=AF.Copy),
                                         reads=[bkd], writes=[XBd[cc]])
                        for (s0, sn, off) in ((0, 256, OFFC), (256, 2048, OFFL)):
                            p.op("dve", lambda e: e.tensor_scalar(out=XC[:, cc, s0:s0 + sn], in0=XB[:, cc, off:off + sn], scalar1=cw[1][0][:, ch:ch + 1],
                                                                  scalar2=cb[0][:, ch:ch + 1], op0=ALU.mult, op1=ALU.add),
                                 reads=[XBd[cc], cw[1][1], cb[1]], writes=[XCd[cc]])
                            for (k, sh) in ((0, -1), (2, 1), (3, 2)):
                                p.op("dve", lambda e: e.scalar_tensor_tensor(out=XC[:, cc, s0:s0 + sn], in0=XB[:, cc, off + sh:off + sh + sn],
                                                                             scalar=cw[k][0][:, ch:ch + 1], in1=XC[:, cc, s0:s0 + sn],
                                                                             op0=ALU.mult, op1=ALU.add),
                                     reads=[XBd[cc], cw[k][1], XCd[cc]], writes=[XCd[cc]])
                        p.op("act", lambda e: e.activation(out=XCb[:, cc, :], in_=XC[:, cc, :], func=AF.Copy), reads=[XCd[cc]], writes=[XCbd[cc]])
                    for d in range(2):
                        gwa, gwad = gwr.get()
                        gwx, gwxd = gwr.get()
                        p.dma(gwa[:], lru_wa[j, d, hd].rearrange("(k q) n -> q k n", q=128), writes=[gwad], q="pool")
                        p.dma(gwx[:], lru_wx[j, d, hd].rearrange("(k q) n -> q k n", q=128), writes=[gwxd], q="pool")
                        for oc in range(2):
                            ch = hd * 2 + oc
                            RF, RFd = RFr.get()
                            IF, IFd = IFr.get()
                            for (t0, n_) in subt:
                                bR, bRd = bankA()
                                bI, bId = bankA()
                                p.mm([(lambda e, k=k: e.matmul(bR[:, 0:n_], lhsT=gwa[:, k, oc * 128:(oc + 1) * 128], rhs=XCb[:, k, t0:t0 + n_],
                                                              start=(k == 0), stop=(k == 1))) for k in range(2)], reads=[gwad, XCbd[0], XCbd[1]], writes=[bRd])
                                p.mm([(lambda e, k=k: e.matmul(bI[:, 0:n_], lhsT=gwx[:, k, oc * 128:(oc + 1) * 128], rhs=XCb[:, k, t0:t0 + n_],
                                                              start=(k == 0), stop=(k == 1))) for k in range(2)], reads=[gwxd, XCbd[0], XCbd[1]], writes=[bId])
                                p.op("act", lambda e: e.activation(out=RF[:, t0:t0 + n_], in_=bR[:, 0:n_], func=AF.Sigmoid, bias=ba[d][0][:, ch:ch + 1]),
                                     reads=[bRd, ba[d][1]], writes=[RFd])
                                p.op("act", lambda e: e.activation(out=IF[:, t0:t0 + n_], in_=bI[:, 0:n_], func=AF.Sigmoid, bias=bx[d][0][:, ch:ch + 1]),
                                     reads=[bId, bx[d][1]], writes=[IFd])
                            p.op("act", lambda e: e.activation(out=RF[:, :], in_=RF[:, :], func=AF.Exp, scale=cl[d][0][:, ch:ch + 1]), reads=[RFd, cl[d][1]], writes=[RFd])
                            p.op("dve", lambda e: e.tensor_tensor(out=AF2[:, :], in0=RF[:, :], in1=RF[:, :], op=ALU.mult), reads=[RFd], writes=[AF2d])
                            p.op("act", lambda e: e.activation(out=AF2[:, :], in_=AF2[:, :], func=AF.Sqrt, scale=-1.0, bias=1.0), reads=[AF2d], writes=[AF2d])
                            p.op("dve", lambda e: e.tensor_tensor(out=IF[:, :], in0=IF[:, :], in1=XC[:, oc, :], op=ALU.mult), reads=[IFd, XCd[oc]], writes=[IFd])
                            p.op("dve", lambda e: e.tensor_tensor(out=IF[:, :], in0=IF[:, :], in1=AF2[:, :], op=ALU.mult), reads=[IFd, AF2d], writes=[IFd])
                            if d == 0:
                                p.op("dve", lambda e: e.tensor_tensor_scan(out=YA[:, oc, :], data0=RF[:, :], data1=IF[:, :], initial=0.0,
                                                                           op0=ALU.mult, op1=ALU.add), reads=[RFd, IFd, YAd[oc]], writes=[YAd[oc]])
                            else:
                                p.op("dve", lambda e: e.tensor_tensor_scan(out=AF2[:, 0:256][:, ::-1], data0=RF[:, 0:256][:, ::-1], data1=IF[:, 0:256][:, ::-1],
                                                                           initial=0.0, op0=ALU.mult, op1=ALU.add), reads=[RFd, IFd, AF2d], writes=[AF2d])
                                p.op("dve", lambda e: e.tensor_tensor_scan(out=AF2[:, 256:T][:, ::-1], data0=RF[:, 256:T][:, ::-1], data1=IF[:, 256:T][:, ::-1],
                                                                           initial=AF2[:, 0:1], op0=ALU.mult, op1=ALU.add), reads=[RFd, IFd, AF2d], writes=[AF2d])
                                p.op("dve", lambda e: e.tensor_tensor(out=YA[:, oc, :], in0=YA[:, oc, :], in1=AF2[:, :], op=ALU.add),
                                     reads=[YAd[oc], AF2d], writes=[YAd[oc]])
                    for oc in range(2):
                        ch = hd * 2 + oc
                        p.op("dve", lambda e: e.tensor_tensor(out=GG[:, oc, :], in0=YA[:, oc, :], in1=GG[:, oc, :], op=ALU.mult),
                             reads=[YAd[oc], GGd[oc]], writes=[GGd[oc]])
                        p.dma(ZTD[:, ch, :], GG[:, oc, :], reads=[GGd[oc]], writes=[Dep()])
        tiles = []
        for ti in range(18):
            def rowfn(cs, ti=ti):
                return [(0, 128, XS[ti * 128:(ti + 1) * 128, cs], XD[ti])]
            tiles.append((ti * 128, 128, 1 if ti < 2 else 0, rowfn))
        out_proj(L, last, lru_w_out[j], False, tiles)

    def s5_col(c):
        w_ = c // 4
        return 32 + 128 * (w_ // 32) + 32 * (c % 4) + (w_ % 32)

    def s5_layer(L, last):
        j = L // 2
        TWO_PI = 2.0 * math.pi
        with p.phase() as st:
            nb = nm_bufs(st)
            xr = Rot(st, "xt", [128, D], F32, 4)
            Ab = p.sb("Ab", [128, 128, 8, 16], BF16, st)
            Abd = Dep()
            vst = Rot(st, "vst", [128, 8, 128], BF16, 3)
            for isctx in (1, 0):
                with p.phase() as st3:
                    (Mt, Md), (SHt, SHd) = bcast_rows(st3, [("g1p", norm1_g[L:L + 1, :], modrow(L, isctx, 1)), ("row", modrow(L, isctx, 0))])
                    for lt in ((0,) if isctx else (0, 1)):
                        npart = 32 if isctx else 128
                        col0 = 0 if isctx else 32 + 128 * lt
                        for jj in range(8):
                            xt, xd = xr.get()
                            if isctx:
                                src = XS[0:256, :].rearrange("(c j) d -> j c d", j=8)
                                p.dma(xt[0:16, :], src[jj, 0:16, :], reads=[XD[0]], writes=[xd])
                                p.dma(xt[16:32, :], src[jj, 16:32, :], reads=[XD[1]], writes=[xd])
                            else:
                                for q in range(4):
                                    r0 = 256 + (8 * q + jj) * 64 + 32 * lt
                                    p.dma(xt[32 * q:32 * q + 32, :], XS[r0:r0 + 32, :], reads=[XD[2 + (8 * q + jj) // 2]], writes=[xd])
                            norm_mod(nb, xt[0:npart], xd, npart, Mt, Md, SHt, SHd, Ab[0:npart, :, jj, :], Abd, v3=True)
                        for g8 in range(16):
                            bk, bkd = bankT()
                            for gg in range(8):
                                g = g8 * 8 + gg
                                p.op("pe", lambda e: e.transpose(bk[:, gg, 0:npart], Ab[0:npart, g, :, :].rearrange("p j h -> p (j h)"),
                                                                 ident[0:npart, 0:npart]), reads=[Abd, cD], writes=[bkd])
                            vs, vsd = vst.get()
                            if isctx:
                                vo, vi = vs[:, :, 0:npart], bk[:, :, 0:npart]
                            else:
                                vo = vs[:, :, :].rearrange("p g (wl q) -> p g q wl", q=4)
                                vi = bk[:, :, :].rearrange("p g (q wl) -> p g q wl", q=4)
                            if g8 % 2 == 0:
                                p.op("act", lambda e: e.activation(out=vo, in_=vi, func=AF.Copy), reads=[bkd], writes=[vsd])
                            else:
                                p.op("dve", lambda e: e.tensor_copy(vo, vi), reads=[bkd], writes=[vsd])
                            p.dma(VD[:, g8 * 8:(g8 + 1) * 8, col0:col0 + npart], vs[:, :, 0:npart], reads=[vsd], writes=[Dep()])

        GB = 32
        NS = 36
        TWO_PI = 2.0 * math.pi
        with p.phase() as st0:
            dcol = p.sb("dcol", [128, 128], F32, st0)
            dcd = Dep()
            for jj in range(8):
                p.dma(dcol[16 * jj:16 * jj + 16, :], s5_d[j, :].rearrange("(g h) -> h g", h=16), writes=[dcd], allow_slow_non_contiguous=True)
            mask = p.sb("mask", [128, 2, 128], F32, st0)
            mkd = Dep()
            for d in range(2):
                p.dma(mask[:, d, :], maskin[d], writes=[mkd])
            for gb in range(128 // GB):
                g0 = gb * GB
                with p.phase() as st:
                    Vb = p.sb("Vb", [128, GB, 288], BF16, st)
                    Vbd = [Dep() for _ in range(GB)]
                    p.dma(Vb[:], VD[:, g0:g0 + GB, :], reads=[Dep()], writes=Vbd)
                    T0b = [p.sb("T0b", [128, GB, 128], BF16, st) for _ in range(2)]
                    T0d = [[Dep() for _ in range(GB)] for _ in range(2)]
                    Bcb = [p.sb("Bcb", [128, GB, 128], BF16, st) for _ in range(2)]
                    Bcd = [[Dep() for _ in range(GB)] for _ in range(2)]
                    OR = p.sb("OR", [128, GB, 128], F32, st)
                    OI = p.sb("OI", [128, GB, 128], F32, st)
                    Od = Dep()
                    LRP = p.sb("LRP", [128, 9, 2, GB], F32, st)
                    LIP = p.sb("LIP", [128, 9, 2, GB], F32, st)
                    Ld = Dep()
                    with p.phase() as sp_:
                        def t2(name):
                            return p.sb(name, [128, GB], F32, sp_), Dep()

                        def dv(fn, reads, writes, eng="dve"):
                            p.op(eng, fn, reads=reads, writes=writes)
                        lre, lred = t2("lre")
                        lim, limd = t2("lim")
                        stp, stpd = t2("stp")
                        for d in range(2):
                            hs = slice(64 * d, 64 * d + 64)
                            p.dma(lre[hs, :], s5_lam_re[j, d, g0:g0 + GB, :].rearrange("g q -> q g"), writes=[lred], allow_slow_non_contiguous=True)
                            p.dma(lim[hs, :], s5_lam_im[j, d, g0:g0 + GB, :].rearrange("g q -> q g"), writes=[limd], allow_slow_non_contiguous=True)
                            p.dma(stp[hs, :], s5_log_step[j, d, g0:g0 + GB].partition_broadcast(64), writes=[stpd])
                        dv(lambda e: e.activation(out=stp[:], in_=stp[:], func=AF.Exp), [stpd], [stpd], "act")
                        dv(lambda e: e.tensor_scalar(out=lre[:], in0=lre[:], scalar1=-1e-4, scalar2=None, op0=ALU.min), [lred], [lred])
                        er, erd = t2("er")
                        ang, angd = t2("ang")
                        dv(lambda e: e.tensor_tensor(out=er[:], in0=lre[:], in1=stp[:], op=ALU.mult), [lred, stpd], [erd])
                        dv(lambda e: e.activation(out=er[:], in_=er[:], func=AF.Exp), [erd], [erd], "act")
                        dv(lambda e: e.tensor_tensor(out=ang[:], in0=lim[:], in1=stp[:], op=ALU.mult), [limd, stpd], [angd])
                        cs = []
                        for shift in (math.pi / 2.0, 0.0):
                            a_, ad_ = t2("a")
                            ki = p.sb("ki", [128, GB], I32, sp_)
                            kid = Dep()
                            kf, kfd = t2("kf")
                            m_, md_ = t2("m")
                            dv(lambda e: e.tensor_scalar(out=a_[:], in0=ang[:], scalar1=shift, scalar2=None, op0=ALU.add), [angd], [ad_])
                            dv(lambda e: e.tensor_scalar(out=ki[:], in0=a_[:], scalar1=1.0 / TWO_PI, scalar2=None, op0=ALU.mult), [ad_], [kid])
                            dv(lambda e: e.tensor_copy(kf[:], ki[:]), [kid], [kfd])
                            dv(lambda e: e.scalar_tensor_tensor(out=a_[:], in0=kf[:], scalar=-TWO_PI, in1=a_[:], op0=ALU.mult, op1=ALU.add), [kfd, ad_], [ad_])
                            dv(lambda e: e.tensor_scalar(out=m_[:], in0=a_[:], scalar1=math.pi, scalar2=None, op0=ALU.is_gt), [ad_], [md_])
                            dv(lambda e: e.scalar_tensor_tensor(out=a_[:], in0=m_[:], scalar=-TWO_PI, in1=a_[:], op0=ALU.mult, op1=ALU.add), [md_, ad_], [ad_])
                            dv(lambda e: e.tensor_scalar(out=m_[:], in0=a_[:], scalar1=-math.pi, scalar2=None, op0=ALU.is_lt), [ad_], [md_])
                            dv(lambda e: e.scalar_tensor_tensor(out=a_[:], in0=m_[:], scalar=TWO_PI, in1=a_[:], op0=ALU.mult, op1=ALU.add), [md_, ad_], [ad_])
                            dv(lambda e: e.activation(out=a_[:], in_=a_[:], func=AF.Sin), [ad_], [ad_], "act")
                            cs.append((a_, ad_))
                        (cosv, cosd), (sinv, sind) = cs
                        PW = p.sb("PW", [128, 9, 2, GB], F32, sp_)
                        MW = p.sb("MW", [128, 8, 2, GB], F32, sp_)
                        PWd = Dep()
                        dv(lambda e: e.memset(PW[:, 0, 0, :], 1.0), [], [PWd])
                        dv(lambda e: e.memset(PW[:, 0, 1, :], 0.0), [], [PWd])
                        dv(lambda e: e.memset(MW[:, 0, 0, :], 1.0), [], [PWd])
                        dv(lambda e: e.memset(MW[:, 0, 1, :], 0.0), [], [PWd])
                        dv(lambda e: e.tensor_tensor(out=PW[:, 1, 0, :], in0=er[:], in1=cosv[:], op=ALU.mult), [erd, cosd], [PWd])
                        dv(lambda e: e.tensor_tensor(out=PW[:, 1, 1, :], in0=er[:], in1=sinv[:], op=ALU.mult), [erd, sind], [PWd])
                        ar, ai = PW[:, 1, 0, :], PW[:, 1, 1, :]
                        q1, q1d = t2("q1")
                        q2, q2d = t2("q2")
                        dv(lambda e: e.tensor_tensor(out=q1[:], in0=ar, in1=ar, op=ALU.mult), [PWd], [q1d])
                        dv(lambda e: e.tensor_tensor(out=q2[:], in0=ai, in1=ai, op=ALU.mult), [PWd], [q2d])
                        dv(lambda e: e.tensor_tensor(out=q1[:], in0=q1[:], in1=q2[:], op=ALU.add), [q1d, q2d], [q1d])
                        dv(lambda e: e.reciprocal(out=q1[:], in_=q1[:]), [q1d], [q1d])
                        dv(lambda e: e.tensor_tensor(out=MW[:, 1, 0, :], in0=ar, in1=q1[:], op=ALU.mult), [PWd, q1d], [PWd])
                        dv(lambda e: e.scalar_tensor_tensor(out=MW[:, 1, 1, :], in0=ai, scalar=-1.0, in1=q1[:], op0=ALU.mult, op1=ALU.mult), [PWd, q1d], [PWd])

                        def cmul_s(dst, k, src, b):
                            xr_, xi_ = src[:, k - 1, 0, :], src[:, k - 1, 1, :]
                            br_, bi_ = b
                            dv(lambda e: e.tensor_tensor(out=q1[:], in0=xr_, in1=br_, op=ALU.mult), [PWd, q1d], [q1d])
                            dv(lambda e: e.tensor_tensor(out=q2[:], in0=xi_, in1=bi_, op=ALU.mult), [PWd, q2d], [q2d])
                            dv(lambda e: e.tensor_tensor(out=dst[:, k, 0, :], in0=q1[:], in1=q2[:], op=ALU.subtract), [q1d, q2d], [PWd])
                            dv(lambda e: e.tensor_tensor(out=q1[:], in0=xr_, in1=bi_, op=ALU.mult), [PWd, q1d], [q1d])
                            dv(lambda e: e.tensor_tensor(out=q2[:], in0=xi_, in1=br_, op=ALU.mult), [PWd, q2d], [q2d])
                            dv(lambda e: e.tensor_tensor(out=dst[:, k, 1, :], in0=q1[:], in1=q2[:], op=ALU.add), [q1d, q2d], [PWd])
                        for k in range(2, 9):
                            cmul_s(PW, k, PW, (ar, ai))
                        for k in range(2, 8):
                            cmul_s(MW, k, MW, (MW[:, 1, 0, :], MW[:, 1, 1, :]))
                        LW = p.sb("LW", [128, 9, 2, GB], F32, sp_)
                        dv(lambda e: e.memset(LW[:, 0, 0, :], 1.0), [PWd], [PWd])
                        dv(lambda e: e.memset(LW[:, 0, 1, :], 0.0), [PWd], [PWd])
                        dv(lambda e: e.tensor_copy(LW[:, 1, :, :], PW[:, 8, :, :]), [PWd], [PWd])
                        for k in range(2, 9):
                            cmul_s(LW, k, LW, (PW[:, 8, 0, :], PW[:, 8, 1, :]))
                        dv(lambda e: e.tensor_copy(LRP[:, :, 0, :], LW[:, :, 0, :]), [PWd], [Ld])
                        dv(lambda e: e.tensor_copy(LRP[:, :, 1, :], LW[:, :, 0, :]), [PWd], [Ld])
                        dv(lambda e: e.tensor_scalar(out=LIP[:, :, 0, :], in0=LW[:, :, 1, :], scalar1=-1.0, scalar2=None, op0=ALU.mult), [PWd], [Ld])
                        dv(lambda e: e.tensor_copy(LIP[:, :, 1, :], LW[:, :, 1, :]), [PWd], [Ld])
                        MWs = p.sb("MWs", [128, 8, 2, GB], F32, sp_)
                        PWy = p.sb("PWy", [128, 9, 2, GB], F32, sp_)
                        PWb = p.sb("PWb", [128, 8, 2, GB], F32, sp_)
                        dv(lambda e: e.tensor_copy(MWs[0:64], MW[0:64]), [PWd], [PWd])
                        dv(lambda e: e.tensor_copy(MWs[64:128], MW[64:128][:, ::-1, :, :]), [PWd], [PWd])
                        dv(lambda e: e.tensor_copy(PWy[0:64], PW[0:64]), [PWd], [PWd])
                        dv(lambda e: e.tensor_copy(PWy[64:128], PW[64:128][:, ::-1, :, :]), [PWd], [PWd])
                        dv(lambda e: e.tensor_copy(PWb[0:64], PW[0:64, 0:8][:, ::-1, :, :]), [PWd], [PWd])
                        dv(lambda e: e.tensor_copy(PWb[64:128], PW[64:128, 0:8]), [PWd], [PWd])
                        den, dend = t2("den")
                        am1, am1d = t2("am1")
                        cr, crd = t2("cr")
                        ci, cid = t2("ci")
                        dv(lambda e: e.tensor_tensor(out=den[:], in0=lre[:], in1=lre[:], op=ALU.mult), [lred], [dend])
                        dv(lambda e: e.tensor_tensor(out=q1[:], in0=lim[:], in1=lim[:], op=ALU.mult), [limd, q1d], [q1d])
                        dv(lambda e: e.tensor_tensor(out=den[:], in0=den[:], in1=q1[:], op=ALU.add), [dend, q1d], [dend])
                        dv(lambda e: e.reciprocal(out=den[:], in_=den[:]), [dend], [dend])
                        dv(lambda e: e.tensor_scalar(out=am1[:], in0=ar, scalar1=-1.0, scalar2=None, op0=ALU.add), [PWd], [am1d])
                        dv(lambda e: e.tensor_tensor(out=q1[:], in0=am1[:], in1=lre[:], op=ALU.mult), [am1d, lred, q1d], [q1d])
                        dv(lambda e: e.tensor_tensor(out=q2[:], in0=ai, in1=lim[:], op=ALU.mult), [PWd, limd, q2d], [q2d])
                        dv(lambda e: e.tensor_tensor(out=q1[:], in0=q1[:], in1=q2[:], op=ALU.add), [q1d, q2d], [q1d])
                        dv(lambda e: e.tensor_tensor(out=cr[:], in0=q1[:], in1=den[:], op=ALU.mult), [q1d, dend], [crd])
                        dv(lambda e: e.tensor_tensor(out=q1[:], in0=ai, in1=lre[:], op=ALU.mult), [PWd, lred, q1d], [q1d])
                        dv(lambda e: e.tensor_tensor(out=q2[:], in0=am1[:], in1=lim[:], op=ALU.mult), [am1d, limd, q2d], [q2d])
                        dv(lambda e: e.tensor_tensor(out=q1[:], in0=q1[:], in1=q2[:], op=ALU.subtract), [q1d, q2d], [q1d])
                        dv(lambda e: e.tensor_tensor(out=ci[:], in0=q1[:], in1=den[:], op=ALU.mult), [q1d, dend], [cid])

                        def t3(name):
                            return p.sb(name, [128, GB, 16], F32, sp_), Dep()
                        Br, Brd = t3("Br")
                        Bi, Bid = t3("Bi")
                        Cr, Crd = t3("Cr")
                        Ci, Cid = t3("Ci")
                        for d in range(2):
                            hs = slice(64 * d, 64 * d + 64)
                            p.dma(Br[hs], s5_b_re[j, d, g0:g0 + GB].rearrange("g q h -> q g h"), writes=[Brd])
                            p.dma(Bi[hs], s5_b_im[j, d, g0:g0 + GB].rearrange("g q h -> q g h"), writes=[Bid])
                        ctr = Rot(sp_, "ct", [128, 128], F32, 2)
                        for (Cdst, Cdd, csrc) in ((Cr, Crd, s5_c_re), (Ci, Cid, s5_c_im)):
                            for g8 in range(GB // 8):
                                ct, ctd = ctr.get()
                                for d in range(2):
                                    p.dma(ct[:, 64 * d:64 * d + 64], csrc[j, d, g0 + 8 * g8:g0 + 8 * g8 + 8].rearrange("g h q -> (g h) q"), writes=[ctd])
                                bk, bkd = bankA()
                                p.op("pe", lambda e: e.transpose(bk[:, 0:128], ct[:, :], identf[:, :]), reads=[ctd, cD], writes=[bkd])
                                p.op("act", lambda e: e.activation(out=Cdst[:, 8 * g8:8 * g8 + 8, :], in_=bk[:, 0:128].rearrange("p (g h) -> p g h", h=16),
                                                                   func=AF.Copy), reads=[bkd], writes=[Cdd])
                        T1 = p.sb("T1", [128, GB, 9, 16], F32, sp_)
                        T1d = Dep()

                        def cmul(dR, dI, dd, Pr, Pi, Pd, Xr_, Xi_, Xd, T1v, negI=False):
                            p.op("dve", lambda e: e.tensor_tensor(out=dR, in0=Xr_, in1=Pr, op=ALU.mult), reads=Pd + Xd, writes=dd)
                            p.op("pool", lambda e: e.tensor_tensor(out=T1v, in0=Xi_, in1=Pi, op=ALU.mult), reads=Pd + Xd, writes=[T1d])
                            p.op("dve", lambda e: e.tensor_tensor(out=dI, in0=Xi_, in1=Pr, op=ALU.mult), reads=Pd + Xd, writes=dd)
                            p.op("dve", lambda e: e.tensor_tensor(out=dR, in0=dR, in1=T1v, op=ALU.subtract), reads=dd + [T1d], writes=dd)
                            p.op("pool", lambda e: e.tensor_tensor(out=T1v, in0=Xr_, in1=Pi, op=ALU.mult), reads=Pd + Xd, writes=[T1d])
                            if negI:
                                p.op("dve", lambda e: e.scalar_tensor_tensor(out=dI, in0=dI, scalar=-1.0, in1=T1v, op0=ALU.mult, op1=ALU.subtract),
                                     reads=dd + [T1d], writes=dd)
                            else:
                                p.op("dve", lambda e: e.tensor_tensor(out=dI, in0=dI, in1=T1v, op=ALU.add), reads=dd + [T1d], writes=dd)

                        def slots(tab, S, ri):
                            return tab[:, :, ri, :].rearrange("p s g -> p g s").unsqueeze(3).to_broadcast([128, GB, S, 16])

                        def overs(x3, S):
                            return x3.unsqueeze(2).to_broadcast([128, GB, S, 16])
                        BbR, BbRd = t3("BbR")
                        BbI, BbId = t3("BbI")
                        cmul(BbR[:], BbI[:], [BbRd, BbId], cr[:].unsqueeze(2).to_broadcast([128, GB, 16]), ci[:].unsqueeze(2).to_broadcast([128, GB, 16]),
                             [crd, cid], Br[:], Bi[:], [Brd, Bid], T1[:, :, 0, :])
                        Bbd = [BbRd, BbId]
                        with p.phase() as sq:
                            XR = p.sb("XR", [128, GB, 8, 16], F32, sq)
                            XI = p.sb("XI", [128, GB, 8, 16], F32, sq)
                            Xd_ = Dep()
                            YR = p.sb("YR", [128, GB, 9, 16], F32, sq)
                            YI = p.sb("YI", [128, GB, 9, 16], F32, sq)
                            Yd_ = Dep()
                            cmul(XR[:], XI[:], [Xd_], slots(MWs, 8, 0), slots(MWs, 8, 1), [PWd], overs(BbR[:], 8), overs(BbI[:], 8), Bbd, T1[:, :, 0:8, :])
                            cmul(YR[:], YI[:], [Yd_], slots(PWy, 9, 0), slots(PWy, 9, 1), [PWd], overs(Cr[:], 9), overs(Ci[:], 9), [Crd, Cid], T1[:], negI=True)
                            tmr = Rot(sq, "tm", [128, 128], F32, 3)
                            for d in range(2):
                                hs = slice(64 * d, 64 * d + 64)
                                y0 = 0 if d == 0 else 1
                                for g in range(GB):
                                    bk, bkd = bankA()
                                    p.mm([lambda e: e.matmul(bk[:, 0:128], lhsT=XR[hs, g, :, :].rearrange("q j h -> q (j h)"),
                                                             rhs=YR[hs, g, y0:y0 + 8, :].rearrange("q j h -> q (j h)"), start=True, stop=False),
                                          lambda e: e.matmul(bk[:, 0:128], lhsT=XI[hs, g, :, :].rearrange("q j h -> q (j h)"),
                                                             rhs=YI[hs, g, y0:y0 + 8, :].rearrange("q j h -> q (j h)"), start=False, stop=True)],
                                         reads=[Xd_, Yd_], writes=[bkd])
                                    if d == 0:
                                        tm, tmd = tmr.get()
                                        p.op("dve", lambda e: e.tensor_tensor(out=tm[:], in0=bk[:, 0:128], in1=mask[:, d, :], op=ALU.mult),
                                             reads=[bkd, mkd], writes=[tmd])
                                        p.op("dve", lambda e: e.scalar_tensor_tensor(out=T0b[d][:, g, :], in0=identf[:, :], scalar=dcol[:, g0 + g:g0 + g + 1],
                                                                                     in1=tm[:], op0=ALU.mult, op1=ALU.add),
                                             reads=[tmd, dcd, cD], writes=[T0d[d][g]])
                                    else:
                                        p.op("dve", lambda e: e.tensor_tensor(out=T0b[d][:, g, :], in0=bk[:, 0:128], in1=mask[:, d, :], op=ALU.mult),
                                             reads=[bkd, mkd], writes=[T0d[d][g]])
                            for (hs, o0) in ((slice(0, 64), 1), (slice(64, 128), 0)):
                                p.op("act", lambda e: e.activation(out=OR[hs].rearrange("q g (j h) -> q g j h", h=16), in_=YR[hs, :, o0:o0 + 8, :], func=AF.Copy),
                                     reads=[Yd_], writes=[Od])
                                p.op("act", lambda e: e.activation(out=OI[hs].rearrange("q g (j h) -> q g j h", h=16), in_=YI[hs, :, o0:o0 + 8, :], func=AF.Copy),
                                     reads=[Yd_], writes=[Od])
                        with p.phase() as sq:
                            BR = p.sb("BR", [128, GB, 8, 16], F32, sq)
                            BI = p.sb("BI", [128, GB, 8, 16], F32, sq)
                            Bd_ = Dep()
                            cmul(BR[:], BI[:], [Bd_], slots(PWb, 8, 0), slots(PWb, 8, 1), [PWd], overs(BbR[:], 8), overs(BbI[:], 8), Bbd, T1[:, :, 0:8, :])
                            for d in range(2):
                                hs = slice(64 * d, 64 * d + 64)
                                for g in range(GB):
                                    bk, bkd = bankA()
                                    p.mm([lambda e: e.transpose(bk[:, 0:64], BR[hs, g, :, :].rearrange("q j h -> q (j h)"), identf[hs, hs]),
                                          lambda e: e.transpose(bk[:, 64:128], BI[hs, g, :, :].rearrange("q j h -> q (j h)"), identf[hs, hs])],
                                         reads=[Bd_, cD], writes=[bkd])
                                    p.op("act", lambda e: e.activation(out=Bcb[d][:, g, :], in_=bk[:, 0:128], func=AF.Copy), reads=[bkd], writes=[Bcd[d][g]])
                    XC = p.sb("XC", [128, 2, GB, 288], F32, st)
                    with p.phase() as sq:
                        xcd = Dep()
                        for g in range(GB):
                            for ri in range(2):
                                bk, bkd = bankA()
                                p.mm([lambda e: e.matmul(bk[0:64, 0:288], lhsT=Bcb[0][:, g, ri * 64:(ri + 1) * 64], rhs=Vb[:, g, :], start=True, stop=True),
                                      lambda e: e.matmul(bk[64:128, 0:32], lhsT=Bcb[1][:, g, ri * 64:(ri + 1) * 64], rhs=Vb[:, g, 0:32][:, ::-1],
                                                         start=True, stop=True),
                                      lambda e: e.matmul(bk[64:128, 32:288], lhsT=Bcb[1][:, g, ri * 64:(ri + 1) * 64], rhs=Vb[:, g, 32:288][:, ::-1],
                                                         start=True, stop=True)],
                                     reads=[Bcd[0][g], Bcd[1][g], Vbd[g]], writes=[bkd])
                                if ri == 0:
                                    p.op("act", lambda e: e.activation(out=XC[:, ri, g, :], in_=bk[:, 0:288], func=AF.Copy), reads=[bkd], writes=[xcd])
                                else:
                                    p.op("dve", lambda e: e.tensor_copy(XC[:, ri, g, :], bk[:, 0:288]), reads=[bkd], writes=[xcd])
                    with p.phase() as sq:
                        shp = [128, 2, GB, NS]
                        Tr = Rot(sq, "T", shp, F32, 2)
                        Bb_ = p.sb("Bq", shp, F32, sq)
                        Bqd = Dep()
                        SIN = p.sb("SIN", shp, F32, sq)
                        SINd = [Dep() for _ in range(NS)]
                        _r3 = {}

                        def Rot_get3(st_):
                            if "r" not in _r3:
                                _r3["r"] = Rot(st_, "m6", [128, 2, GB], F32, 4)
                            return _r3["r"].get()
                        slotD = [Dep() for _ in range(8)]

                        def lr(k):
                            return LRP[:, k, :, :].unsqueeze(3).to_broadcast(shp)

                        def li(k):
                            return LIP[:, k, :, :].unsqueeze(3).to_broadcast(shp)

                        def slot(k):
                            return XC[:, :, :, k::8]
                        tp_, tpd_ = None, None
                        for k in range(1, 9):
                            tk, tkd = Tr.get()
                            if k == 1:
                                p.op("act", lambda e: e.activation(out=tk[:], in_=slot(0), func=AF.Copy), reads=[slotD[0]], writes=[tkd])
                                p.op("dve", lambda e: e.memset(slot(0), 0.0), writes=[slotD[0]])
                            else:
                                p.op("pool", lambda e: e.tensor_tensor(out=tk[:], in0=tp_[:], in1=lr(1), op=ALU.mult), reads=[tpd_, Ld], writes=[tkd])
                                p.op("dve", lambda e: e.tensor_tensor(out=Bb_[:], in0=tp_[:, ::-1, :, :], in1=li(1), op=ALU.mult), reads=[tpd_, Ld], writes=[Bqd])
                                p.op("dve", lambda e: e.tensor_tensor(out=Bb_[:], in0=Bb_[:], in1=slot(k - 1), op=ALU.add), reads=[Bqd, slotD[k - 1]], writes=[Bqd])
                                p.op("dve", lambda e: e.tensor_tensor(out=tk[:], in0=tk[:], in1=Bb_[:], op=ALU.add), reads=[tkd, Bqd], writes=[tkd])
                                p.op("act", lambda e: e.activation(out=slot(k - 1), in_=tp_[:], func=AF.Copy), reads=[tpd_], writes=[slotD[k - 1]])
                            tp_, tpd_ = tk, tkd
                        Z, Zd = tp_, tpd_
                        NB6 = 6
                        sh6 = [128, 2, GB, NB6]
                        L8r = LRP[:, 8, :, :].unsqueeze(3).to_broadcast(sh6)
                        L8i = LIP[:, 8, :, :].unsqueeze(3).to_broadcast(sh6)
                        LLR = p.sb("LLR", [128, 7, 2, GB], F32, sq)
                        LLI = p.sb("LLI", [128, 7, 2, GB], F32, sq)
                        LLd = Dep()
                        llst = contextlib.ExitStack()
                        LL = p.sb("LL", [128, 7, 2, GB], F32, llst)
                        u1 = p.sb("u1", [128, GB], F32, llst)
                        u2 = p.sb("u2", [128, GB], F32, llst)
                        u1d, u2d = Dep(), Dep()
                        p.op("dve", lambda e: e.memset(LL[:, 0, 0, :], 1.0), writes=[LLd])
                        p.op("dve", lambda e: e.memset(LL[:, 0, 1, :], 0.0), writes=[LLd])
                        p.op("dve", lambda e: e.tensor_copy(LL[:, 1, 0, :], LRP[:, 8, 0, :]), reads=[Ld], writes=[LLd])
                        p.op("dve", lambda e: e.tensor_copy(LL[:, 1, 1, :], LIP[:, 8, 1, :]), reads=[Ld], writes=[LLd])
                        for k in range(2, 7):
                            xr_, xi_ = LL[:, k - 1, 0, :], LL[:, k - 1, 1, :]
                            br_, bi_ = LL[:, 1, 0, :], LL[:, 1, 1, :]
                            p.op("dve", lambda e: e.tensor_tensor(out=u1[:], in0=xr_, in1=br_, op=ALU.mult), reads=[LLd, u1d], writes=[u1d])
                            p.op("dve", lambda e: e.tensor_tensor(out=u2[:], in0=xi_, in1=bi_, op=ALU.mult), reads=[LLd, u2d], writes=[u2d])
                            p.op("dve", lambda e: e.tensor_tensor(out=LL[:, k, 0, :], in0=u1[:], in1=u2[:], op=ALU.subtract), reads=[u1d, u2d], writes=[LLd])
                            p.op("dve", lambda e: e.tensor_tensor(out=u1[:], in0=xr_, in1=bi_, op=ALU.mult), reads=[LLd, u1d], writes=[u1d])
                            p.op("dve", lambda e: e.tensor_tensor(out=u2[:], in0=xi_, in1=br_, op=ALU.mult), reads=[LLd, u2d], writes=[u2d])
                            p.op("dve", lambda e: e.tensor_tensor(out=LL[:, k, 1, :], in0=u1[:], in1=u2[:], op=ALU.add), reads=[u1d, u2d], writes=[LLd])
                        p.op("dve", lambda e: e.tensor_copy(LLR[:, :, 0, :], LL[:, :, 0, :]), reads=[LLd], writes=[LLd])
                        p.op("dve", lambda e: e.tensor_copy(LLR[:, :, 1, :], LL[:, :, 0, :]), reads=[LLd], writes=[LLd])
                        p.op("dve", lambda e: e.tensor_scalar(out=LLI[:, :, 0, :], in0=LL[:, :, 1, :], scalar1=-1.0, scalar2=None, op0=ALU.mult), reads=[LLd], writes=[LLd])
                        p.op("dve", lambda e: e.tensor_copy(LLI[:, :, 1, :], LL[:, :, 1, :]), reads=[LLd], writes=[LLd])
                        p.barrier()
                        llst.close()
                        SINall = Dep()

                        def sslot(k):
                            return SIN[:, :, :, k::6]

                        def zslot(k):
                            return Z[:, :, :, k::6]
                        w6 = [(p.sb("w6_%d" % i, sh6, F32, sq), Dep()) for i in range(3)]
                        p.op("dve", lambda e: e.memset(sslot(0), 0.0), writes=[SINall])
                        for k in range(1, NB6 + 1):
                            (ta, tad), (tb, tbd) = w6[0], w6[1]
                            if k == 1:
                                p.op("dve", lambda e: e.tensor_copy(ta[:], zslot(0)), reads=[Zd], writes=[tad])
                            else:
                                p.op("dve", lambda e: e.tensor_tensor(out=ta[:], in0=prev6, in1=L8r, op=ALU.mult), reads=[SINall, Ld, w6[2][1]], writes=[tad])
                                p.op("dve", lambda e: e.tensor_tensor(out=tb[:], in0=prev6[:, ::-1, :, :], in1=L8i, op=ALU.mult), reads=[SINall, Ld, w6[2][1]], writes=[tbd])
                                p.op("dve", lambda e: e.tensor_tensor(out=ta[:], in0=ta[:], in1=zslot(k - 1), op=ALU.add), reads=[tad, Zd], writes=[tad])
                                p.op("dve", lambda e: e.tensor_tensor(out=ta[:], in0=ta[:], in1=tb[:], op=ALU.add), reads=[tad, tbd], writes=[tad])
                            if k < NB6:
                                p.op("dve", lambda e: e.tensor_copy(sslot(k), ta[:]), reads=[tad], writes=[SINall])
                                prev6 = sslot(k)
                            else:
                                p.op("dve", lambda e: e.tensor_copy(w6[2][0][:], ta[:]), reads=[tad], writes=[w6[2][1]])
                        tot, totd = w6[2]
                        Eb = p.sb("Eb", sh6, F32, sq)
                        Ebd = Dep()
                        p.op("dve", lambda e: e.memset(Eb[:, :, :, 0], 0.0), writes=[Ebd])
                        for b_ in range(1, NB6):
                            m1, m1d = Rot_get3(sq)
                            m2, m2d = Rot_get3(sq)
                            p.op("dve", lambda e: e.tensor_tensor(out=m1[:], in0=Eb[:, :, :, b_ - 1], in1=LLR[:, 6, :, :], op=ALU.mult), reads=[Ebd, LLd], writes=[m1d])
                            p.op("dve", lambda e: e.tensor_tensor(out=m2[:], in0=Eb[:, ::-1, :, b_ - 1], in1=LLI[:, 6, :, :], op=ALU.mult), reads=[Ebd, LLd], writes=[m2d])
                            p.op("dve", lambda e: e.tensor_tensor(out=m1[:], in0=m1[:], in1=tot[:, :, :, b_ - 1], op=ALU.add), reads=[m1d, totd], writes=[m1d])
                            p.op("dve", lambda e: e.tensor_tensor(out=Eb[:, :, :, b_], in0=m1[:], in1=m2[:], op=ALU.add), reads=[m1d, m2d, Ebd], writes=[Ebd])
                        for k in range(NB6):
                            (ta, tad), (tb, tbd) = w6[0], w6[1]
                            lr6 = LLR[:, k, :, :].unsqueeze(3).to_broadcast(sh6)
                            li6 = LLI[:, k, :, :].unsqueeze(3).to_broadcast(sh6)
                            p.op("dve", lambda e: e.tensor_tensor(out=ta[:], in0=Eb[:], in1=lr6, op=ALU.mult), reads=[Ebd, LLd], writes=[tad])
                            p.op("dve", lambda e: e.tensor_tensor(out=tb[:], in0=Eb[:, ::-1, :, :], in1=li6, op=ALU.mult), reads=[Ebd, LLd], writes=[tbd])
                            p.op("dve", lambda e: e.tensor_tensor(out=ta[:], in0=ta[:], in1=tb[:], op=ALU.add), reads=[tad, tbd], writes=[tad])
                            p.op("dve", lambda e: e.tensor_tensor(out=sslot(k), in0=sslot(k), in1=ta[:], op=ALU.add), reads=[SINall, tad], writes=[SINall])
                        SINd = [SINall]
                        p.op("act", lambda e: e.activation(out=slot(0), in_=SIN[:], func=AF.Copy), reads=SINd, writes=[slotD[0]])
                        Ab_, Aqd = Tr.get()
                        for k in range(1, 8):
                            p.op("pool", lambda e: e.tensor_tensor(out=Ab_[:], in0=SIN[:], in1=lr(k), op=ALU.mult), reads=SINd + [Ld], writes=[Aqd])
                            p.op("dve", lambda e: e.tensor_tensor(out=Bb_[:], in0=SIN[:, ::-1, :, :], in1=li(k), op=ALU.mult), reads=SINd + [Ld], writes=[Bqd])
                            p.op("dve", lambda e: e.tensor_tensor(out=Bb_[:], in0=Bb_[:], in1=slot(k), op=ALU.add), reads=[Bqd, slotD[k]], writes=[Bqd])
                            p.op("dve", lambda e: e.tensor_tensor(out=slot(k), in0=Bb_[:], in1=Ab_[:], op=ALU.add), reads=[slotD[k], Aqd, Bqd], writes=[slotD[k]])
                    with p.phase() as sq:
                        tr_ = Rot(sq, "ty", [128, 288], F32, 3)
                        ur_ = Rot(sq, "tu", [128, 288], F32, 3)
                        for g in range(GB):
                            bk, bkd = bankA()
                            b2_, b2d = bankA()
                            p.mm([lambda e: e.matmul(bk[:, 0:288], lhsT=T0b[0][:, g, :], rhs=Vb[:, g, :], start=True, stop=False),
                                  lambda e: e.matmul(bk[:, 0:288], lhsT=T0b[1][:, g, :], rhs=Vb[:, g, :], start=False, stop=False),
                                  lambda e: e.matmul(bk[:, 0:288], lhsT=OR[0:64, g, :], rhs=XC[0:64, 0, g, :], start=False, stop=False),
                                  lambda e: e.matmul(bk[:, 0:288], lhsT=OI[0:64, g, :], rhs=XC[0:64, 1, g, :], start=False, stop=True)],
                                 reads=[T0d[0][g], T0d[1][g], Vbd[g], Od], writes=[bkd])
                            p.mm([lambda e: e.matmul(b2_[:, 0:32], lhsT=OR[64:128, g, :], rhs=XC[64:128, 0, g, 0:32][:, ::-1], start=True, stop=False),
                                  lambda e: e.matmul(b2_[:, 0:32], lhsT=OI[64:128, g, :], rhs=XC[64:128, 1, g, 0:32][:, ::-1], start=False, stop=False),
                                  lambda e: e.matmul(b2_[:, 32:288], lhsT=OR[64:128, g, :], rhs=XC[64:128, 0, g, 32:288][:, ::-1], start=True, stop=False),
                                  lambda e: e.matmul(b2_[:, 32:288], lhsT=OI[64:128, g, :], rhs=XC[64:128, 1, g, 32:288][:, ::-1], start=False, stop=True)],
                                 reads=[Od], writes=[b2d])
                            tu, tud = ur_.get()
                            ty, tyd = tr_.get()
                            p.op("act", lambda e: e.activation(out=tu[:], in_=b2_[:, 0:288], func=AF.Copy), reads=[b2d], writes=[tud])
                            p.op("dve", lambda e: e.tensor_tensor(out=ty[:], in0=bk[:, 0:288], in1=tu[:], op=ALU.add), reads=[bkd, tud], writes=[tyd])
                            p.op("act", lambda e: e.activation(out=Vb[:, g, 0:32], in_=ty[:, 0:32], func=AF.Gelu_apprx_tanh), reads=[tyd], writes=[Vbd[g]])
                            p.op("act", lambda e: e.activation(out=Vb[:, g, 32:288].rearrange("p (lt q wl) -> p lt wl q", lt=2, q=4),
                                                               in_=ty[:, 32:288].rearrange("p (lt wl q) -> p lt wl q", lt=2, q=4),
                                                               func=AF.Gelu_apprx_tanh), reads=[tyd], writes=[Vbd[g]])
                    p.dma(YD[:, g0:g0 + GB, :], Vb[:], reads=Vbd, writes=[Dep()])
        YDv = YD.rearrange("(j h) g c -> h g j c", h=16)
        for k in range(16):
            for gl in range(8):
                p.dma(ZTD[16 * gl:16 * gl + 16, k, :].rearrange("h (j c) -> h j c", j=8), YDv[:, 8 * k + gl, :, :], reads=[Dep()], writes=[Dep()])
        p.barrier()
        tiles = []
        for jj in range(8):
            def rowfn_c(cs, jj=jj):
                src = XS[0:256, :].rearrange("(c j) d -> j c d", j=8)
                return [(0, 16, src[jj, 0:16, cs], XD[0]), (16, 16, src[jj, 16:32, cs], XD[1])]
            tiles.append((jj * 288, 32, 1, rowfn_c))
        for lt in range(2):
            for jj in range(8):
                def rowfn_l(cs, jj=jj, lt=lt):
                    res = []
                    for q in range(4):
                        r0 = 256 + (8 * q + jj) * 64 + 32 * lt
                        res.append((32 * q, 32, XS[r0:r0 + 32, cs], XD[2 + (8 * q + jj) // 2]))
                    return res
                tiles.append((jj * 288 + 32 + 128 * lt, 128, 0, rowfn_l))
        out_proj(L, last, s5_w_glu[j], True, tiles)

    for L in range(nlayers):
        last = (L == NL - 1)
        if L % 2 == 0:
            lru_layer(L, last)
        else:
            s5_layer(L, last)
        ffn(L, last)

    with p.phase() as st:
        if debug:
            dxr = Rot(st, "dx", [128, D], F32, 2)
            for t in range(18):
                xt, xd = dxr.get()
                p.dma(xt[:], XS[t * 128:(t + 1) * 128, :], reads=[XD[t]], writes=[xd])
                p.dma(dbg[t * 128:(t + 1) * 128, :], xt[:], reads=[xd])
        fg_ = p.sb("fg", [1, D], F32, st)
        fgd = Dep()
        p.dma(fg_[:], final_g.rearrange("(o n) -> o n", o=1), writes=[fgd])
        Gt = p.sb("Gt", [128, D], F32, st)
        Gd = Dep()
        for k in range(4):
            bk, bkd = bankA()
            p.op("pe", lambda e: e.matmul(bk[:], lhsT=ones1[:], rhs=fg_[0:1, k * 512:(k + 1) * 512], start=True, stop=True), reads=[fgd, cD], writes=[bkd])
            p.op("act", lambda e: e.activation(out=Gt[:, k * 512:(k + 1) * 512], in_=bk[:], func=AF.Copy), reads=[bkd], writes=[Gd])
        nb = nm_bufs(st)
        jr, ssr, rsr = nb
        xr = Rot(st, "xt", [128, D], F32, 4)
        orr = Rot(st, "ot", [128, D], F32, 2)
        for ti in range(2, 18):
            junk, jd = jr.get()
            xt, xd = xr.get()
            p.dma(xt[:], XS[ti * 128:(ti + 1) * 128, :], reads=[XD[ti]], writes=[xd])
            ss, ssd = ssr.get()
            rs, rsd = rsr.get()
            ot, otd = orr.get()
            p.op("act", lambda e: e.activation(out=junk[:], in_=xt[:], func=AF.Square, accum_out=ss[:]), reads=[xd], writes=[jd, ssd])
            p.op("dve", lambda e: e.tensor_scalar(out=rs[:], in0=ss[:], scalar1=1.0 / D, scalar2=EPS, op0=ALU.mult, op1=ALU.add), reads=[ssd], writes=[rsd])
            p.op("act", lambda e: e.activation(out=rs[:], in_=rs[:], func=AF.Sqrt), reads=[rsd], writes=[rsd])
            p.op("dve", lambda e: e.reciprocal(out=rs[:], in_=rs[:]), reads=[rsd], writes=[rsd])
            p.op("dve", lambda e: e.scalar_tensor_tensor(out=ot[:], in0=xt[:], scalar=rs[:, 0:1], in1=Gt[:], op0=ALU.mult, op1=ALU.mult),
                 reads=[xd, rsd, Gd], writes=[otd])
            p.dma(out[(ti - 2) * 128:(ti - 1) * 128, :], ot[:], reads=[otd])
    p.finish()
    return p


def make_inputs(inputs):
    ident = np.eye(128, dtype=np.float32)
    jj = np.arange(128) // 16
    maskf = (jj[None, :] >= jj[:, None]).astype(np.float32)
    maskb = (jj[None, :] <= jj[:, None]).astype(np.float32)
    masks = np.stack([maskf, maskb]).astype(np.float32)
    shared = {k: np.ascontiguousarray(np.asarray(v, dtype=np.float32)) for k, v in inputs.items() if k not in ("x", "c", "ctx", "c_ctx")}
    maps = []
    for b in range(4):
        m = dict(shared)
        m["xin"] = np.ascontiguousarray(np.concatenate([inputs["ctx"][b], inputs["x"][b]], axis=0).astype(np.float32))
        m["cvec"] = np.ascontiguousarray(np.stack([inputs["c"][b], inputs["c_ctx"]]).astype(np.float32))
        m["identin"] = ident
        m["maskin"] = masks
        maps.append(m)
    return maps


def kernel(**inputs):
    inputs = {k: np.asarray(v) for k, v in inputs.items()}
    p = build()
    maps = make_inputs(inputs)
    res = run_bass_kernel_spmd(p.nc, maps, core_ids=list(range(4)))
    return np.stack([np.asarray(r["out"], dtype=np.float32) for r in res.results], axis=0)
```

```python
import contextlib
import math
import numpy as np
import concourse.bass as bass
import concourse.mybir as mybir
from concourse.bass_utils import run_bass_kernel_spmd

F32 = mybir.dt.float32
BF16 = mybir.dt.bfloat16
I32 = mybir.dt.int32
AF = mybir.ActivationFunctionType
ALU = mybir.AluOpType

D = 2048
T = 2304
NCTX = 256
DFF = 5632
NL = 4
EPS = 1e-6


class Dep:
    __slots__ = ("w", "r")

    def __init__(self):
        self.w = None
        self.r = []


class P:
    NDMA = 40

    def __init__(self):
        nc = self.nc = bass.Bass("TRN2", target_bir_lowering=False)
        self.es = contextlib.ExitStack()
        self.eng = {"pe": nc.tensor, "act": nc.scalar, "dve": nc.vector, "pool": nc.gpsimd, "sp": nc.sync}
        self.sem = {e: self.es.enter_context(nc.semaphore("s_" + e)) for e in self.eng}
        self.cnt = {e: 0 for e in self.eng}
        self.known = {e: {} for e in self.eng}
        self.dsem = [self.es.enter_context(nc.semaphore("d%d" % i)) for i in range(self.NDMA)]
        self.duse = [0] * self.NDMA
        self.di = 0
        self.ninst = 0
        self.uid = 0

    def sb(self, name, shape, dt, st=None):
        self.uid += 1
        return (st or self.es).enter_context(self.nc.sbuf_tensor("%s_%d" % (name, self.uid), list(shape), dt))

    def ps(self, name, shape, dt):
        return self.es.enter_context(self.nc.psum_tensor(name, list(shape), dt))

    def _wait(self, e, ev):
        if ev is None:
            return
        sem, val = ev
        k = self.known[e]
        if k.get(sem.num, 0) >= val:
            return
        self.eng[e].wait_ge(sem, val)
        k[sem.num] = val

    def _deps(self, e, reads, writes):
        for d in reads:
            self._wait(e, d.w)
        for d in writes:
            self._wait(e, d.w)
            for ev in d.r:
                self._wait(e, ev)

    def _commit(self, ev, reads, writes):
        for d in reads:
            d.r.append(ev)
            if len(d.r) > 16:
                best = {}
                for s, v in d.r:
                    if best.get(s.num, (None, -1))[1] < v:
                        best[s.num] = (s, v)
                d.r = list(best.values())
        for d in writes:
            d.w = ev
            d.r = []

    def op(self, e, make, reads=(), writes=()):
        self._deps(e, reads, writes)
        inst = make(self.eng[e])
        self.cnt[e] += 1
        ev = (self.sem[e], self.cnt[e])
        inst.then_inc(self.sem[e], 1)
        self._commit(ev, reads, writes)
        self.ninst += 1
        return ev

    def mm(self, steps, reads=(), writes=()):
        self._deps("pe", reads, writes)
        inst = None
        for mk in steps:
            inst = mk(self.eng["pe"])
        self.cnt["pe"] += 1
        ev = (self.sem["pe"], self.cnt["pe"])
        inst.then_inc(self.sem["pe"], 1)
        self._commit(ev, reads, writes)
        self.ninst += len(steps)
        return ev

    def dma(self, out, in_, reads=(), writes=(), q="sp", **kw):
        i = self.di
        self.di = (self.di + 1) % self.NDMA
        if self.duse[i] > 0:
            self._wait(q, (self.dsem[i], 16 * self.duse[i]))
        self._deps(q, reads, writes)
        inst = self.eng[q].dma_start(out=out, in_=in_, **kw)
        self.duse[i] += 1
        ev = (self.dsem[i], 16 * self.duse[i])
        inst.then_inc(self.dsem[i], 16)
        self._commit(ev, reads, writes)
        self.ninst += 1
        return ev

    def barrier(self):
        evs = [(self.sem[e], self.cnt[e]) for e in self.eng if self.cnt[e] > 0]
        evs += [(self.dsem[i], 16 * self.duse[i]) for i in range(self.NDMA) if self.duse[i] > 0]
        for e in self.eng:
            for ev in evs:
                self._wait(e, ev)

    @contextlib.contextmanager
    def phase(self):
        st = contextlib.ExitStack()
        try:
            yield st
        finally:
            self.barrier()
            st.close()

    def finish(self):
        self.barrier()
        self.es.close()


def build(debug=None, nlayers=NL):
    p = P()
    nc = p.nc

    def din(name, shape):
        return nc.dram_tensor(name, list(shape), F32, kind="ExternalInput").ap()

    xin = din("xin", [T, D])
    cvec = din("cvec", [2, D])
    ada_w = din("ada_w", [NL, D, 6 * D])
    ada_b = din("ada_b", [NL, 6 * D])
    norm1_g = din("norm1_g", [NL, D])
    norm2_g = din("norm2_g", [NL, D])
    final_g = din("final_g", [D])
    ffn_w13 = din("ffn_w13", [NL, D, 2 * DFF])
    ffn_w2 = din("ffn_w2", [NL, DFF, D])
    lru_w_in = din("lru_w_in", [2, D, 2 * D])
    lru_conv_w = din("lru_conv_w", [2, 4, D])
    lru_conv_b = din("lru_conv_b", [2, D])
    lru_wa = din("lru_wa", [2, 2, 8, 256, 256])
    lru_ba = din("lru_ba", [2, 2, D])
    lru_wx = din("lru_wx", [2, 2, 8, 256, 256])
    lru_bx = din("lru_bx", [2, 2, D])
    lru_lam = din("lru_lam", [2, 2, D])
    lru_w_out = din("lru_w_out", [2, D, D])
    s5_lam_re = din("s5_lam_re", [2, 2, 128, 64])
    s5_lam_im = din("s5_lam_im", [2, 2, 128, 64])
    s5_log_step = din("s5_log_step", [2, 2, 128])
    s5_b_re = din("s5_b_re", [2, 2, 128, 64, 16])
    s5_b_im = din("s5_b_im", [2, 2, 128, 64, 16])
    s5_c_re = din("s5_c_re", [2, 2, 128, 16, 64])
    s5_c_im = din("s5_c_im", [2, 2, 128, 16, 64])
    s5_d = din("s5_d", [2, D])
    s5_w_glu = din("s5_w_glu", [2, D, 2 * D])
    identin = din("identin", [128, 128])
    maskin = din("maskin", [2, 128, 128])
    out = nc.dram_tensor("out", [T - NCTX, D], F32, kind="ExternalOutput").ap()
    dbg = None
    if debug:
        dbg = nc.dram_tensor("dbg", [T, D], F32, kind="ExternalOutput").ap()
        dbgmod = nc.dram_tensor("dbgmod", [NL, 2, 6 * D], F32, kind="ExternalOutput").ap()

    XS = nc.dram_tensor("XS", [T, D], F32, kind="Internal").ap()
    MOD = nc.dram_tensor("MOD", [NL, 2, 6 * D], F32, kind="Internal").ap()
    ZTD = nc.dram_tensor("ZTD", [128, 16, T], BF16, kind="Internal").ap()
    VD = nc.dram_tensor("VD", [128, 128, 288], BF16, kind="Internal").ap()
    YD = nc.dram_tensor("YD", [128, 128, 288], BF16, kind="Internal").ap()
    class _Fresh:
        def __getitem__(self, i):
            return Dep()
    XD = _Fresh()
    MODD = Dep()
    ZTDD = Dep()
    VDD = Dep()
    YDD = Dep()

    ident = p.sb("ident", [128, 128], BF16)
    identf = p.sb("identf", [128, 128], F32)
    ones1 = p.sb("ones1", [1, 128], F32)
    cD = Dep()
    p.dma(ident[:], identin[:, :], writes=[cD], q="pool")
    p.dma(identf[:], identin[:, :], writes=[cD])
    p.op("dve", lambda e: e.memset(ones1[:], 1.0), writes=[cD])
    psA = [(p.ps("psA%d" % i, [128, 512], F32), Dep()) for i in range(6)]
    psT = [(p.ps("psT%d" % i, [128, 8, 128], BF16), Dep()) for i in range(2)]
    rot = {"A": 0, "T": 0}

    def bankA():
        rot["A"] = (rot["A"] + 1) % 6
        return psA[rot["A"]]

    def bankT():
        rot["T"] = (rot["T"] + 1) % 2
        return psT[rot["T"]]

    for t in range(18):
        p.dma(XS[t * 128:(t + 1) * 128, :], xin[t * 128:(t + 1) * 128, :], writes=[XD[t]])

    class Rot:
        def __init__(self, st, name, shape, dt, n):
            self.b = [(p.sb(name, shape, dt, st), Dep()) for _ in range(n)]
            self.i = 0

        def get(self):
            self.i = (self.i + 1) % len(self.b)
            return self.b[self.i]

    def bcast_rows(st, specs):
        res = []
        dsts = [(p.sb("bc", [128, D], F32, st), Dep()) for _ in specs]
        tmp_st = contextlib.ExitStack()
        r0 = Rot(tmp_st, "r0", [1, D], F32, 1)
        r1 = Rot(tmp_st, "r1", [1, D], F32, 1)
        for si, sp in enumerate(specs):
            dst, dd = dsts[si]
            a, ad = r0.get()
            p.dma(a[:], sp[1], reads=[MODD], writes=[ad])
            if sp[0] == "g1p":
                b, bd = r1.get()
                p.dma(b[:], sp[2], reads=[MODD], writes=[bd])
                p.op("dve", lambda e: e.scalar_tensor_tensor(out=a[:], in0=b[:], scalar=1.0, in1=a[:], op0=ALU.add, op1=ALU.mult),
                     reads=[ad, bd], writes=[ad])
            for k in range(4):
                bk, bkd = bankA()
                p.op("pe", lambda e: e.matmul(bk[:], lhsT=ones1[:], rhs=a[0:1, k * 512:(k + 1) * 512], start=True, stop=True),
                     reads=[ad, cD], writes=[bkd])
                p.op("act", lambda e: e.activation(out=dst[:, k * 512:(k + 1) * 512], in_=bk[:], func=AF.Copy), reads=[bkd], writes=[dd])
            res.append((dst, dd))
        p.barrier()
        tmp_st.close()
        return res

    def modrow(L, v, idx):
        return MOD[L, v:v + 1, idx * D:(idx + 1) * D]

    def norm_mod(st_bufs, xt, xd, npart, Mt, Md, SHt, SHd, outap, outd, v3=False):
        jr, ssr, rsr = st_bufs
        junk, jd = jr.get()
        ss, ssd = ssr.get()
        rs, rsd = rsr.get()
        p.op("act", lambda e: e.activation(out=junk[:npart], in_=xt, func=AF.Square, accum_out=ss[:npart]), reads=[xd], writes=[jd, ssd])
        p.op("dve", lambda e: e.tensor_scalar(out=rs[:npart], in0=ss[:npart], scalar1=1.0 / D, scalar2=EPS, op0=ALU.mult, op1=ALU.add),
             reads=[ssd], writes=[rsd])
        p.op("act", lambda e: e.activation(out=rs[:npart], in_=rs[:npart], func=AF.Sqrt), reads=[rsd], writes=[rsd])
        p.op("dve", lambda e: e.reciprocal(out=rs[:npart], in_=rs[:npart]), reads=[rsd], writes=[rsd])
        p.op("dve", lambda e: e.scalar_tensor_tensor(out=junk[:npart], in0=xt, scalar=rs[:npart, 0:1], in1=Mt[:npart],
                                                     op0=ALU.mult, op1=ALU.mult), reads=[xd, rsd, Md], writes=[jd])
        i0, i1 = junk[:npart], SHt[:npart]
        if v3:
            i0 = i0.rearrange("p (g h) -> p g h", h=16)
            i1 = i1.rearrange("p (g h) -> p g h", h=16)
        p.op("dve", lambda e: e.tensor_tensor(out=outap, in0=i0, in1=i1, op=ALU.add), reads=[jd, SHd], writes=[outd])

    def nm_bufs(st, nj=2):
        return (Rot(st, "junk", [128, D], F32, nj), Rot(st, "ss", [128, 1], F32, 3), Rot(st, "rs", [128, 1], F32, 3))

    def to_fm(hb, hbd, ntok, HT, HTd, tok0):
        for kb in range(2):
            bk, bkd = bankT()
            for kk in range(8):
                k = kb * 8 + kk
                p.op("pe", lambda e: e.transpose(bk[:, kk, 0:ntok], hb[0:ntok, k * 128:(k + 1) * 128], ident[0:ntok, 0:ntok]),
                     reads=[hbd, cD], writes=[bkd])
            eng = "act" if kb == 0 else "dve"
            if eng == "act":
                p.op("act", lambda e: e.activation(out=HT[:, kb * 8:(kb + 1) * 8, tok0:tok0 + ntok], in_=bk[:, :, 0:ntok], func=AF.Copy),
                     reads=[bkd], writes=[HTd])
            else:
                p.op("dve", lambda e: e.tensor_copy(HT[:, kb * 8:(kb + 1) * 8, tok0:tok0 + ntok], bk[:, :, 0:ntok]), reads=[bkd], writes=[HTd])

    sT = p.sb("sT", [128, 32], BF16)
    sTd = Dep()
    with p.phase() as st:
        c32 = p.sb("c32", [32, 128], F32, st)
        c32d = Dep()
        p.dma(c32[:], cvec.rearrange("v (k q) -> (v k) q", q=128), writes=[c32d])
        bk, bkd = bankA()
        p.op("pe", lambda e: e.transpose(bk[:, 0:32], c32[:, :], identf[0:32, 0:32]), reads=[c32d, cD], writes=[bkd])
        p.op("act", lambda e: e.activation(out=sT[:], in_=bk[:, 0:32], func=AF.Silu), reads=[bkd], writes=[sTd])

    def adaln_tile(L, nt, w3, wd, br, mr):
        cs = slice(nt * 512, (nt + 1) * 512)
        p.dma(w3, ada_w[L, :, cs].rearrange("(k q) n -> q k n", q=128), writes=[wd], q="pool")
        b_, bd_ = br.get()
        p.dma(b_[:], ada_b[L:L + 1, cs].broadcast_to([2, 512]), writes=[bd_])
        bk, bkd = bankA()
        p.mm([(lambda e, k=k: e.matmul(bk[0:2, :], lhsT=sT[:, k:32:16], rhs=w3[:, k, :], start=(k == 0), stop=(k == 15))) for k in range(16)],
             reads=[sTd, wd], writes=[bkd])
        m_, md_ = mr.get()
        p.op("dve", lambda e: e.tensor_tensor(out=m_[:], in0=bk[0:2, :], in1=b_[:], op=ALU.add), reads=[bkd, bd_], writes=[md_])
        p.dma(MOD[L, :, cs], m_[:], reads=[md_], writes=[MODD])
        if debug:
            p.dma(dbgmod[L, :, cs], m_[:], reads=[md_])

    with p.phase() as st:
        wr = Rot(st, "adaw", [128, 16, 512], BF16, 3)
        br = Rot(st, "adab", [2, 512], F32, 3)
        mr = Rot(st, "adam", [2, 512], F32, 3)
        for nt in range(24):
            w, wd = wr.get()
            adaln_tile(0, nt, w[:], wd, br, mr)

    def ffn(L, last):
        tl = list(range(2, 18)) if last else list(range(18))
        half = len(tl) // 2
        ada_pending = list(range(24)) if (L + 1 < nlayers) else []
        for part in (tl[:half], tl[half:]):
            nt_ = len(part) * 128
            subs = []
            o_ = 0
            while o_ < nt_:
                n_ = min(384, nt_ - o_)
                subs.append((o_, n_))
                o_ += n_
            kinds = sorted(set(1 if ti < 2 else 0 for ti in part))
            with p.phase() as stG:
                GT = p.sb("GT", [128, 44, nt_], BF16, stG)
                GTd = [Dep() for _ in range(44)]
                with p.phase() as stH:
                    HT = p.sb("HT", [128, 16, nt_], BF16, stH)
                    HTd = Dep()
                    with p.phase() as st2:
                        bc = {}
                        for v in kinds:
                            bc[v] = bcast_rows(st2, [("g1p", norm2_g[L:L + 1, :], modrow(L, v, 4)), ("row", modrow(L, v, 3))])
                        nb = nm_bufs(st2, 1)
                        xr = Rot(st2, "xt", [128, D], F32, 2)
                        hr = Rot(st2, "hb", [128, D], BF16, 2)
                        for i, ti in enumerate(part):
                            (Mt, Md), (SHt, SHd) = bc[1 if ti < 2 else 0]
                            xt, xd = xr.get()
                            p.dma(xt[:], XS[ti * 128:(ti + 1) * 128, :], reads=[XD[ti]], writes=[xd])
                            hb, hbd = hr.get()
                            norm_mod(nb, xt[:], xd, 128, Mt, Md, SHt, SHd, hb[:], hbd)
                            to_fm(hb, hbd, 128, HT, HTd, i * 128)
                    with p.phase() as st2:
                        wr = Rot(st2, "w13", [128, 16, 2, 256], BF16, 3)
                        sr = Rot(st2, "sa", [128, 384], F32, 3)
                        abr = Rot(st2, "adab", [2, 512], F32, 2)
                        amr = Rot(st2, "adam", [2, 512], F32, 2)
                        for fg in range(22):
                            if ada_pending:
                                aw, awd = wr.get()
                                adaln_tile(L + 1, ada_pending.pop(0), aw[:, :, :, :].rearrange("p k a n -> p k (a n)"), awd, abr, amr)
                            w, wd = wr.get()
                            p.dma(w[:, :, 0, :], ffn_w13[L, :, fg * 256:(fg + 1) * 256].rearrange("(k q) n -> q k n", q=128), writes=[wd], q="pool")
                            p.dma(w[:, :, 1, :], ffn_w13[L, :, DFF + fg * 256:DFF + (fg + 1) * 256].rearrange("(k q) n -> q k n", q=128),
                                  writes=[wd], q="pool")
                            for c2 in range(2):
                                f = fg * 2 + c2
                                for (o_, n_) in subs:
                                    ba_, bad = bankA()
                                    bb_, bbd = bankA()
                                    p.mm([(lambda e, k=k: e.matmul(ba_[:, 0:n_], lhsT=w[:, k, 0, c2 * 128:(c2 + 1) * 128], rhs=HT[:, k, o_:o_ + n_],
                                                                  start=(k == 0), stop=(k == 15))) for k in range(16)], reads=[wd, HTd], writes=[bad])
                                    p.mm([(lambda e, k=k: e.matmul(bb_[:, 0:n_], lhsT=w[:, k, 1, c2 * 128:(c2 + 1) * 128], rhs=HT[:, k, o_:o_ + n_],
                                                                  start=(k == 0), stop=(k == 15))) for k in range(16)], reads=[wd, HTd], writes=[bbd])
                                    sa, sad = sr.get()
                                    p.op("act", lambda e: e.activation(out=sa[:, 0:n_], in_=ba_[:, 0:n_], func=AF.Silu), reads=[bad], writes=[sad])
                                    p.op("dve", lambda e: e.tensor_tensor(out=GT[:, f, o_:o_ + n_], in0=sa[:, 0:n_], in1=bb_[:, 0:n_], op=ALU.mult),
                                         reads=[sad, bbd], writes=[GTd[f]])
                with p.phase() as st2:
                    g2 = {}
                    for v in kinds:
                        (g2[v],) = bcast_rows(st2, [("row", modrow(L, v, 5))])
                    wr = Rot(st2, "w2", [128, 22, 512], BF16, 3)
                    xr = Rot(st2, "xp", [128, 512], F32, 4)
                    tr = Rot(st2, "tp", [128, 512], F32, 3)
                    for nt in range(4):
                        wh = [wr.get(), wr.get()]
                        for kq in range(4):
                            w_, wd_ = wh[kq // 2]
                            p.dma(w_[:, (kq % 2) * 11:(kq % 2 + 1) * 11, :],
                                  ffn_w2[L, kq * 1408:(kq + 1) * 1408, nt * 512:(nt + 1) * 512].rearrange("(k q) n -> q k n", q=128), writes=[wd_], q="pool")
                        for i, ti in enumerate(part):
                            G2t, G2d = g2[1 if ti < 2 else 0]
                            bk, bkd = bankA()
                            p.mm([(lambda e, k=k: e.matmul(bk[:, :], lhsT=GT[:, k, i * 128:(i + 1) * 128], rhs=wh[k // 22][0][:, k % 22, :],
                                                          start=(k == 0), stop=(k == 43))) for k in range(44)], reads=[wh[0][1], wh[1][1]] + GTd, writes=[bkd])
                            xp, xpd = xr.get()
                            p.dma(xp[:], XS[ti * 128:(ti + 1) * 128, nt * 512:(nt + 1) * 512], reads=[XD[ti]], writes=[xpd])
                            tp, tpd = tr.get()
                            p.op("dve", lambda e: e.tensor_tensor(out=tp[:], in0=bk[:, :], in1=G2t[:, nt * 512:(nt + 1) * 512], op=ALU.mult),
                                 reads=[bkd, G2d], writes=[tpd])
                            p.op("dve", lambda e: e.tensor_tensor(out=tp[:], in0=tp[:], in1=xp[:], op=ALU.add), reads=[tpd, xpd], writes=[tpd])
                            p.dma(XS[ti * 128:(ti + 1) * 128, nt * 512:(nt + 1) * 512], tp[:], reads=[tpd], writes=[XD[ti]])

    def out_proj(L, last, wsrc, glu, tiles):
        OW = 256 if glu else 512
        with p.phase() as st:
            ZT = p.sb("ZT", [128, 16, T], BF16, st)
            ZTd = Dep()
            for k in range(16):
                p.dma(ZT[:, k, :], ZTD[:, k, :], reads=[Dep()], writes=[ZTd])
            g1 = {}
            for v in ((0, 1) if not last else (0,)):
                (g1[v],) = bcast_rows(st, [("row", modrow(L, v, 2))])
            wr = Rot(st, "wo", [128, 16, 512], BF16, 2)
            xr = Rot(st, "xp", [128, OW], F32, 3)
            tr = Rot(st, "tp", [128, OW], F32, 3)
            sr = Rot(st, "sg", [128, 256], F32, 3)
            def wload(nt):
                w, wd = wr.get()
                if glu:
                    p.dma(w[:, :, 0:256], wsrc[:, nt * 256:(nt + 1) * 256].rearrange("(k q) n -> q k n", q=128), writes=[wd], q="pool")
                    p.dma(w[:, :, 256:512], wsrc[:, D + nt * 256:D + (nt + 1) * 256].rearrange("(k q) n -> q k n", q=128), writes=[wd], q="pool")
                else:
                    p.dma(w[:, :, :], wsrc[:, nt * 512:(nt + 1) * 512].rearrange("(k q) n -> q k n", q=128), writes=[wd], q="pool")
                return w, wd
            nxt = wload(0)
            for nt in range(D // OW):
                w, wd = nxt
                if nt + 1 < D // OW:
                    nxt = wload(nt + 1)
                for (z0, ntok, isctx, rowfn) in tiles:
                    if last and isctx:
                        continue
                    G1t, G1d = g1[isctx]
                    bk, bkd = bankA()
                    p.mm([(lambda e, k=k: e.matmul(bk[0:ntok, :], lhsT=ZT[:, k, z0:z0 + ntok], rhs=w[:, k, :],
                                                  start=(k == 0), stop=(k == 15))) for k in range(16)], reads=[wd, ZTd], writes=[bkd])
                    cs = slice(nt * OW, (nt + 1) * OW)
                    rows = rowfn(cs)
                    xp, xpd = xr.get()
                    for (p0, np_, ap_, dep_) in rows:
                        p.dma(xp[p0:p0 + np_, :], ap_, reads=[dep_], writes=[xpd])
                    tp, tpd = tr.get()
                    if glu:
                        sg, sgd = sr.get()
                        p.op("act", lambda e: e.activation(out=sg[0:ntok], in_=bk[0:ntok, 256:512], func=AF.Sigmoid), reads=[bkd], writes=[sgd])
                        p.op("dve", lambda e: e.tensor_tensor(out=tp[0:ntok], in0=bk[0:ntok, 0:256], in1=sg[0:ntok], op=ALU.mult),
                             reads=[bkd, sgd], writes=[tpd])
                        p.op("dve", lambda e: e.tensor_tensor(out=tp[0:ntok], in0=tp[0:ntok], in1=G1t[0:ntok, cs], op=ALU.mult),
                             reads=[tpd, G1d], writes=[tpd])
                    else:
                        p.op("dve", lambda e: e.tensor_tensor(out=tp[0:ntok], in0=bk[0:ntok, :], in1=G1t[0:ntok, cs], op=ALU.mult),
                             reads=[bkd, G1d], writes=[tpd])
                    p.op("dve", lambda e: e.tensor_tensor(out=tp[0:ntok], in0=tp[0:ntok], in1=xp[0:ntok], op=ALU.add), reads=[tpd, xpd], writes=[tpd])
                    for (p0, np_, ap_, dep_) in rows:
                        p.dma(ap_, tp[p0:p0 + np_, :], reads=[tpd], writes=[dep_], q="pool")

    def lru_layer(L, last):
        j = L // 2
        with p.phase() as st:
            HT = p.sb("HT", [128, 16, T], BF16, st)
            HTd = Dep()
            with p.phase() as st2:
                nb = nm_bufs(st2)
                xr = Rot(st2, "xt", [128, D], F32, 4)
                hr = Rot(st2, "hb", [128, D], BF16, 2)
                for isctx in (1, 0):
                    with p.phase() as st3:
                        (Mt, Md), (SHt, SHd) = bcast_rows(st3, [("g1p", norm1_g[L:L + 1, :], modrow(L, isctx, 1)), ("row", modrow(L, isctx, 0))])
                        for ti in (range(0, 2) if isctx else range(2, 18)):
                            xt, xd = xr.get()
                            p.dma(xt[:], XS[ti * 128:(ti + 1) * 128, :], reads=[XD[ti]], writes=[xd])
                            hb, hbd = hr.get()
                            norm_mod(nb, xt[:], xd, 128, Mt, Md, SHt, SHd, hb[:], hbd)
                            to_fm(hb, hbd, 128, HT, HTd, ti * 128)
            with p.phase() as st2:
                colT = p.sb("colT", [128, 16 * 16], F32, st2)
                colN = [0]
                stage = Rot(st2, "colstage", [16, 128], F32, 2)

                def colload(name, src_row):
                    t_, td = stage.get()
                    p.dma(t_[:], src_row.rearrange("(k q) -> k q", q=128), writes=[td])
                    bk, bkd = bankA()
                    p.op("pe", lambda e: e.transpose(bk[:, 0:16], t_[:, :], identf[0:16, 0:16]), reads=[td, cD], writes=[bkd])
                    o_ = colT[:, colN[0] * 16:(colN[0] + 1) * 16]
                    colN[0] += 1
                    od = Dep()
                    p.op("act", lambda e: e.activation(out=o_[:], in_=bk[:, 0:16], func=AF.Copy), reads=[bkd], writes=[od])
                    return o_, od
                cw = [colload("cw%d" % k, lru_conv_w[j, k, :]) for k in range(4)]
                cb = colload("cb", lru_conv_b[j, :])
                ba = [colload("ba%d" % d, lru_ba[j, d, :]) for d in range(2)]
                bx = [colload("bx%d" % d, lru_bx[j, d, :]) for d in range(2)]
                cl = []
                for d in range(2):
                    lm, lmd = colload("lam%d" % d, lru_lam[j, d, :])
                    p.op("act", lambda e: e.activation(out=lm[:], in_=lm[:], func=AF.Exp, scale=-1.0), reads=[lmd], writes=[lmd])
                    p.op("dve", lambda e: e.tensor_scalar(out=lm[:], in0=lm[:], scalar1=1.0, scalar2=None, op0=ALU.add), reads=[lmd], writes=[lmd])
                    p.op("act", lambda e: e.activation(out=lm[:], in_=lm[:], func=AF.Ln), reads=[lmd], writes=[lmd])
                    p.op("dve", lambda e: e.tensor_scalar(out=lm[:], in0=lm[:], scalar1=-8.0, scalar2=None, op0=ALU.mult), reads=[lmd], writes=[lmd])
                    cl.append((lm, lmd))
                W = T + 6
                OFFC, OFFL = 1, 260
                XB = p.sb("XB", [128, 2, W], F32, st2)
                XBd = [Dep(), Dep()]
                p.op("dve", lambda e: e.memset(XB[:, :, :], 0.0), writes=XBd)
                XC = p.sb("XC", [128, 2, T], F32, st2)
                XCd = [Dep(), Dep()]
                XCb = p.sb("XCb", [128, 2, T], BF16, st2)
                XCbd = [Dep(), Dep()]
                GG = p.sb("GG", [128, 2, T], BF16, st2)
                GGd = [Dep(), Dep()]
                YA = p.sb("YA", [128, 2, T], F32, st2)
                YAd = [Dep(), Dep()]
                wr = Rot(st2, "win", [128, 16, 128], BF16, 2)
                gwr = Rot(st2, "gw", [128, 2, 256], BF16, 4)
                RFr = Rot(st2, "RF", [128, T], F32, 2)
                IFr = Rot(st2, "IF", [128, T], F32, 2)
                AF2 = p.sb("AF2", [128, T], F32, st2)
                AF2d = Dep()
                subt = [(0, 256)] + [(256 + 512 * i, 512) for i in range(4)]

                def xboff(t0):
                    return (OFFC + t0) if t0 < 256 else (OFFL + t0 - 256)
                for hd in range(8):
                    for cc in range(2):
                        ch = hd * 2 + cc
                        for which in range(2):
                            w, wd = wr.get()
                            p.dma(w[:], lru_w_in[j, :, which * D + ch * 128: which * D + (ch + 1) * 128].rearrange("(k q) n -> q k n", q=128),
                                  writes=[wd], q="pool")
                            for (t0, n_) in subt:
                                bk, bkd = bankA()
                                p.mm([(lambda e, k=k: e.matmul(bk[:, 0:n_], lhsT=w[:, k, :], rhs=HT[:, k, t0:t0 + n_], start=(k == 0), stop=(k == 15)))
                                      for k in range(16)], reads=[wd, HTd], writes=[bkd])
                                if which == 0:
                                    p.op("act", lambda e: e.activation(out=GG[:, cc, t0:t0 + n_], in_=bk[:, 0:n_], func=AF.Gelu_apprx_tanh),
                                         reads=[bkd], writes=[GGd[cc]])
                                else:
                                    o0 = xboff(t0)
                                    p.op("act", lambda e: e.activation(out=XB[:, cc, o0:o0 + n_], in_=bk[:, 0:n_], func=AF.Copy),
                                         reads=[bkd], writes=[XBd[cc]])
                        for (s0, sn, off) in ((0, 256, OFFC), (256, 2048, OFFL)):
                            p.op("dve", lambda e: e.tensor_scalar(out=XC[:, cc, s0:s0 + sn], in0=XB[:, cc, off:off + sn], scalar1=cw[1][0][:, ch:ch + 1],
                                                                  scalar2=cb[0][:, ch:ch + 1], op0=ALU.mult, op1=ALU.add),
                                 reads=[XBd[cc], cw[1][1], cb[1]], writes=[XCd[cc]])
                            for (k, sh) in ((0, -1), (2, 1), (3, 2)):
                                p.op("dve", lambda e: e.scalar_tensor_tensor(out=XC[:, cc, s0:s0 + sn], in0=XB[:, cc, off + sh:off + sh + sn],
                                                                             scalar=cw[k][0][:, ch:ch + 1], in1=XC[:, cc, s0:s0 + sn],
                                                                             op0=ALU.mult, op1=ALU.add),
                                     reads=[XBd[cc], cw[k][1], XCd[cc]], writes=[XCd[cc]])
                        p.op("act", lambda e: e.activation(out=XCb[:, cc, :], in_=XC[:, cc, :], func=AF.Copy), reads=[XCd[cc]], writes=[XCbd[cc]])
                    for d in range(2):
                        gwa, gwad = gwr.get()
                        gwx, gwxd = gwr.get()
                        p.dma(gwa[:], lru_wa[j, d, hd].rearrange("(k q) n -> q k n", q=128), writes=[gwad], q="pool")
                        p.dma(gwx[:], lru_wx[j, d, hd].rearrange("(k q) n -> q k n", q=128), writes=[gwxd], q="pool")
                        for oc in range(2):
                            ch = hd * 2 + oc
                            RF, RFd = RFr.get()
                            IF, IFd = IFr.get()
                            for (t0, n_) in subt:
                                bR, bRd = bankA()
                                bI, bId = bankA()
                                p.mm([(lambda e, k=k: e.matmul(bR[:, 0:n_], lhsT=gwa[:, k, oc * 128:(oc + 1) * 128], rhs=XCb[:, k, t0:t0 + n_],
                                                              start=(k == 0), stop=(k == 1))) for k in range(2)], reads=[gwad, XCbd[0], XCbd[1]], writes=[bRd])
                                p.mm([(lambda e, k=k: e.matmul(bI[:, 0:n_], lhsT=gwx[:, k, oc * 128:(oc + 1) * 128], rhs=XCb[:, k, t0:t0 + n_],
                                                              start=(k == 0), stop=(k == 1))) for k in range(2)], reads=[gwxd, XCbd[0], XCbd[1]], writes=[bId])
                                p.op("act", lambda e: e.activation(out=RF[:, t0:t0 + n_], in_=bR[:, 0:n_], func=AF.Sigmoid, bias=ba[d][0][:, ch:ch + 1]),
                                     reads=[bRd, ba[d][1]], writes=[RFd])
                                p.op("act", lambda e: e.activation(out=IF[:, t0:t0 + n_], in_=bI[:, 0:n_], func=AF.Sigmoid, bias=bx[d][0][:, ch:ch + 1]),
                                     reads=[bId, bx[d][1]], writes=[IFd])
                            p.op("act", lambda e: e.activation(out=RF[:, :], in_=RF[:, :], func=AF.Exp, scale=cl[d][0][:, ch:ch + 1]), reads=[RFd, cl[d][1]], writes=[RFd])
                            p.op("dve", lambda e: e.tensor_tensor(out=AF2[:, :], in0=RF[:, :], in1=RF[:, :], op=ALU.mult), reads=[RFd], writes=[AF2d])
                            p.op("act", lambda e: e.activation(out=AF2[:, :], in_=AF2[:, :], func=AF.Sqrt, scale=-1.0, bias=1.0), reads=[AF2d], writes=[AF2d])
                            p.op("dve", lambda e: e.tensor_tensor(out=IF[:, :], in0=IF[:, :], in1=XC[:, oc, :], op=ALU.mult), reads=[IFd, XCd[oc]], writes=[IFd])
                            p.op("dve", lambda e: e.tensor_tensor(out=IF[:, :], in0=IF[:, :], in1=AF2[:, :], op=ALU.mult), reads=[IFd, AF2d], writes=[IFd])
                            if d == 0:
                                p.op("dve", lambda e: e.tensor_tensor_scan(out=YA[:, oc, :], data0=RF[:, :], data1=IF[:, :], initial=0.0,
                                                                           op0=ALU.mult, op1=ALU.add), reads=[RFd, IFd, YAd[oc]], writes=[YAd[oc]])
                            else:
                                p.op("dve", lambda e: e.tensor_tensor_scan(out=AF2[:, 0:256][:, ::-1], data0=RF[:, 0:256][:, ::-1], data1=IF[:, 0:256][:, ::-1],
                                                                           initial=0.0, op0=ALU.mult, op1=ALU.add), reads=[RFd, IFd, AF2d], writes=[AF2d])
                                p.op("dve", lambda e: e.tensor_tensor_scan(out=AF2[:, 256:T][:, ::-1], data0=RF[:, 256:T][:, ::-1], data1=IF[:, 256:T][:, ::-1],
                                                                           initial=AF2[:, 0:1], op0=ALU.mult, op1=ALU.add), reads=[RFd, IFd, AF2d], writes=[AF2d])
                                p.op("dve", lambda e: e.tensor_tensor(out=YA[:, oc, :], in0=YA[:, oc, :], in1=AF2[:, :], op=ALU.add),
                                     reads=[YAd[oc], AF2d], writes=[YAd[oc]])
                    for oc in range(2):
                        ch = hd * 2 + oc
                        p.op("dve", lambda e: e.tensor_tensor(out=GG[:, oc, :], in0=YA[:, oc, :], in1=GG[:, oc, :], op=ALU.mult),
                             reads=[YAd[oc], GGd[oc]], writes=[GGd[oc]])
                        p.dma(ZTD[:, ch, :], GG[:, oc, :], reads=[GGd[oc]], writes=[Dep()])
        tiles = []
        for ti in range(18):
            def rowfn(cs, ti=ti):
                return [(0, 128, XS[ti * 128:(ti + 1) * 128, cs], XD[ti])]
            tiles.append((ti * 128, 128, 1 if ti < 2 else 0, rowfn))
        out_proj(L, last, lru_w_out[j], False, tiles)

    def s5_col(c):
        w_ = c // 4
        return 32 + 128 * (w_ // 32) + 32 * (c % 4) + (w_ % 32)

    def s5_layer(L, last):
        j = L // 2
        TWO_PI = 2.0 * math.pi
        with p.phase() as st:
            nb = nm_bufs(st)
            xr = Rot(st, "xt", [128, D], F32, 4)
            Ab = p.sb("Ab", [128, 128, 8, 16], BF16, st)
            Abd = Dep()
            vst = Rot(st, "vst", [128, 8, 128], BF16, 3)
            for isctx in (1, 0):
                with p.phase() as st3:
                    (Mt, Md), (SHt, SHd) = bcast_rows(st3, [("g1p", norm1_g[L:L + 1, :], modrow(L, isctx, 1)), ("row", modrow(L, isctx, 0))])
                    for lt in ((0,) if isctx else (0, 1)):
                        npart = 32 if isctx else 128
                        col0 = 0 if isctx else 32 + 128 * lt
                        for jj in range(8):
                            xt, xd = xr.get()
                            if isctx:
                                src = XS[0:256, :].rearrange("(c j) d -> j c d", j=8)
                                p.dma(xt[0:16, :], src[jj, 0:16, :], reads=[XD[0]], writes=[xd])
                                p.dma(xt[16:32, :], src[jj, 16:32, :], reads=[XD[1]], writes=[xd])
                            else:
                                for q in range(4):
                                    r0 = 256 + (8 * q + jj) * 64 + 32 * lt
                                    p.dma(xt[32 * q:32 * q + 32, :], XS[r0:r0 + 32, :], reads=[XD[2 + (8 * q + jj) // 2]], writes=[xd])
                            norm_mod(nb, xt[0:npart], xd, npart, Mt, Md, SHt, SHd, Ab[0:npart, :, jj, :], Abd, v3=True)
                        for g8 in range(16):
                            bk, bkd = bankT()
                            for gg in range(8):
                                g = g8 * 8 + gg
                                p.op("pe", lambda e: e.transpose(bk[:, gg, 0:npart], Ab[0:npart, g, :, :].rearrange("p j h -> p (j h)"),
                                                                 ident[0:npart, 0:npart]), reads=[Abd, cD], writes=[bkd])
                            vs, vsd = vst.get()
                            if isctx:
                                vo, vi = vs[:, :, 0:npart], bk[:, :, 0:npart]
                            else:
                                vo = vs[:, :, :].rearrange("p g (wl q) -> p g q wl", q=4)
                                vi = bk[:, :, :].rearrange("p g (q wl) -> p g q wl", q=4)
                            if g8 % 2 == 0:
                                p.op("act", lambda e: e.activation(out=vo, in_=vi, func=AF.Copy), reads=[bkd], writes=[vsd])
                            else:
                                p.op("dve", lambda e: e.tensor_copy(vo, vi), reads=[bkd], writes=[vsd])
                            p.dma(VD[:, g8 * 8:(g8 + 1) * 8, col0:col0 + npart], vs[:, :, 0:npart], reads=[vsd], writes=[Dep()])

        GB = 32
        NS = 36
        TWO_PI = 2.0 * math.pi
        with p.phase() as st0:
            dcol = p.sb("dcol", [128, 128], F32, st0)
            dcd = Dep()
            for jj in range(8):
                p.dma(dcol[16 * jj:16 * jj + 16, :], s5_d[j, :].rearrange("(g h) -> h g", h=16), writes=[dcd], allow_slow_non_contiguous=True)
            mask = p.sb("mask", [128, 2, 128], F32, st0)
            mkd = Dep()
            for d in range(2):
                p.dma(mask[:, d, :], maskin[d], writes=[mkd])
            for gb in range(128 // GB):
                g0 = gb * GB
                with p.phase() as st:
                    Vb = p.sb("Vb", [128, GB, 288], BF16, st)
                    Vbd = [Dep() for _ in range(GB)]
                    p.dma(Vb[:], VD[:, g0:g0 + GB, :], reads=[Dep()], writes=Vbd)
                    T0b = [p.sb("T0b", [128, GB, 128], BF16, st) for _ in range(2)]
                    T0d = [[Dep() for _ in range(GB)] for _ in range(2)]
                    Bcb = [p.sb("Bcb", [128, GB, 128], BF16, st) for _ in range(2)]
                    Bcd = [[Dep() for _ in range(GB)] for _ in range(2)]
                    OR = p.sb("OR", [128, GB, 128], F32, st)
                    OI = p.sb("OI", [128, GB, 128], F32, st)
                    Od = Dep()
                    LRP = p.sb("LRP", [128, 9, 2, GB], F32, st)
                    LIP = p.sb("LIP", [128, 9, 2, GB], F32, st)
                    Ld = Dep()
                    with p.phase() as sp_:
                        def t2(name):
                            return p.sb(name, [128, GB], F32, sp_), Dep()

                        def dv(fn, reads, writes, eng="dve"):
                            p.op(eng, fn, reads=reads, writes=writes)
                        lre, lred = t2("lre")
                        lim, limd = t2("lim")
                        stp, stpd = t2("stp")
                        for d in range(2):
                            hs = slice(64 * d, 64 * d + 64)
                            p.dma(lre[hs, :], s5_lam_re[j, d, g0:g0 + GB, :].rearrange("g q -> q g"), writes=[lred], allow_slow_non_contiguous=True)
                            p.dma(lim[hs, :], s5_lam_im[j, d, g0:g0 + GB, :].rearrange("g q -> q g"), writes=[limd], allow_slow_non_contiguous=True)
                            p.dma(stp[hs, :], s5_log_step[j, d, g0:g0 + GB].partition_broadcast(64), writes=[stpd])
                        dv(lambda e: e.activation(out=stp[:], in_=stp[:], func=AF.Exp), [stpd], [stpd], "act")
                        dv(lambda e: e.tensor_scalar(out=lre[:], in0=lre[:], scalar1=-1e-4, scalar2=None, op0=ALU.min), [lred], [lred])
                        er, erd = t2("er")
                        ang, angd = t2("ang")
                        dv(lambda e: e.tensor_tensor(out=er[:], in0=lre[:], in1=stp[:], op=ALU.mult), [lred, stpd], [erd])
                        dv(lambda e: e.activation(out=er[:], in_=er[:], func=AF.Exp), [erd], [erd], "act")
                        dv(lambda e: e.tensor_tensor(out=ang[:], in0=lim[:], in1=stp[:], op=ALU.mult), [limd, stpd], [angd])
                        cs = []
                        for shift in (math.pi / 2.0, 0.0):
                            a_, ad_ = t2("a")
                            ki = p.sb("ki", [128, GB], I32, sp_)
                            kid = Dep()
                            kf, kfd = t2("kf")
                            m_, md_ = t2("m")
                            dv(lambda e: e.tensor_scalar(out=a_[:], in0=ang[:], scalar1=shift, scalar2=None, op0=ALU.add), [angd], [ad_])
                            dv(lambda e: e.tensor_scalar(out=ki[:], in0=a_[:], scalar1=1.0 / TWO_PI, scalar2=None, op0=ALU.mult), [ad_], [kid])
                            dv(lambda e: e.tensor_copy(kf[:], ki[:]), [kid], [kfd])
                            dv(lambda e: e.scalar_tensor_tensor(out=a_[:], in0=kf[:], scalar=-TWO_PI, in1=a_[:], op0=ALU.mult, op1=ALU.add), [kfd, ad_], [ad_])
                            dv(lambda e: e.tensor_scalar(out=m_[:], in0=a_[:], scalar1=math.pi, scalar2=None, op0=ALU.is_gt), [ad_], [md_])
                            dv(lambda e: e.scalar_tensor_tensor(out=a_[:], in0=m_[:], scalar=-TWO_PI, in1=a_[:], op0=ALU.mult, op1=ALU.add), [md_, ad_], [ad_])
                            dv(lambda e: e.tensor_scalar(out=m_[:], in0=a_[:], scalar1=-math.pi, scalar2=None, op0=ALU.is_lt), [ad_], [md_])
                            dv(lambda e: e.scalar_tensor_tensor(out=a_[:], in0=m_[:], scalar=TWO_PI, in1=a_[:], op0=ALU.mult, op1=ALU.add), [md_, ad_], [ad_])
                            dv(lambda e: e.activation(out=a_[:], in_=a_[:], func=AF.Sin), [ad_], [ad_], "act")
                            cs.append((a_, ad_))
                        (cosv, cosd), (sinv, sind) = cs
                        PW = p.sb("PW", [128, 9, 2, GB], F32, sp_)
                        MW = p.sb("MW", [128, 8, 2, GB], F32, sp_)
                        PWd = Dep()
                        dv(lambda e: e.memset(PW[:, 0, 0, :], 1.0), [], [PWd])
                        dv(lambda e: e.memset(PW[:, 0, 1, :], 0.0), [], [PWd])
                        dv(lambda e: e.memset(MW[:, 0, 0, :], 1.0), [], [PWd])
                        dv(lambda e: e.memset(MW[:, 0, 1, :], 0.0), [], [PWd])
                        dv(lambda e: e.tensor_tensor(out=PW[:, 1, 0, :], in0=er[:], in1=cosv[:], op=ALU.mult), [erd, cosd], [PWd])
                        dv(lambda e: e.tensor_tensor(out=PW[:, 1, 1, :], in0=er[:], in1=sinv[:], op=ALU.mult), [erd, sind], [PWd])
                        ar, ai = PW[:, 1, 0, :], PW[:, 1, 1, :]
                        q1, q1d = t2("q1")
                        q2, q2d = t2("q2")
                        dv(lambda e: e.tensor_tensor(out=q1[:], in0=ar, in1=ar, op=ALU.mult), [PWd], [q1d])
                        dv(lambda e: e.tensor_tensor(out=q2[:], in0=ai, in1=ai, op=ALU.mult), [PWd], [q2d])
                        dv(lambda e: e.tensor_tensor(out=q1[:], in0=q1[:], in1=q2[:], op=ALU.add), [q1d, q2d], [q1d])
                        dv(lambda e: e.reciprocal(out=q1[:], in_=q1[:]), [q1d], [q1d])
                        dv(lambda e: e.tensor_tensor(out=MW[:, 1, 0, :], in0=ar, in1=q1[:], op=ALU.mult), [PWd, q1d], [PWd])
                        dv(lambda e: e.scalar_tensor_tensor(out=MW[:, 1, 1, :], in0=ai, scalar=-1.0, in1=q1[:], op0=ALU.mult, op1=ALU.mult), [PWd, q1d], [PWd])

                        def cmul_s(dst, k, src, b):
                            xr_, xi_ = src[:, k - 1, 0, :], src[:, k - 1, 1, :]
                            br_, bi_ = b
                            dv(lambda e: e.tensor_tensor(out=q1[:], in0=xr_, in1=br_, op=ALU.mult), [PWd, q1d], [q1d])
                            dv(lambda e: e.tensor_tensor(out=q2[:], in0=xi_, in1=bi_, op=ALU.mult), [PWd, q2d], [q2d])
                            dv(lambda e: e.tensor_tensor(out=dst[:, k, 0, :], in0=q1[:], in1=q2[:], op=ALU.subtract), [q1d, q2d], [PWd])
                            dv(lambda e: e.tensor_tensor(out=q1[:], in0=xr_, in1=bi_, op=ALU.mult), [PWd, q1d], [q1d])
                            dv(lambda e: e.tensor_tensor(out=q2[:], in0=xi_, in1=br_, op=ALU.mult), [PWd, q2d], [q2d])
                            dv(lambda e: e.tensor_tensor(out=dst[:, k, 1, :], in0=q1[:], in1=q2[:], op=ALU.add), [q1d, q2d], [PWd])
                        for k in range(2, 9):
                            cmul_s(PW, k, PW, (ar, ai))
                        for k in range(2, 8):
                            cmul_s(MW, k, MW, (MW[:, 1, 0, :], MW[:, 1, 1, :]))
                        LW = p.sb("LW", [128, 9, 2, GB], F32, sp_)
                        dv(lambda e: e.memset(LW[:, 0, 0, :], 1.0), [PWd], [PWd])
                        dv(lambda e: e.memset(LW[:, 0, 1, :], 0.0), [PWd], [PWd])
                        dv(lambda e: e.tensor_copy(LW[:, 1, :, :], PW[:, 8, :, :]), [PWd], [PWd])
                        for k in range(2, 9):
                            cmul_s(LW, k, LW, (PW[:, 8, 0, :], PW[:, 8, 1, :]))
                        dv(lambda e: e.tensor_copy(LRP[:, :, 0, :], LW[:, :, 0, :]), [PWd], [Ld])
                        dv(lambda e: e.tensor_copy(LRP[:, :, 1, :], LW[:, :, 0, :]), [PWd], [Ld])
                        dv(lambda e: e.tensor_scalar(out=LIP[:, :, 0, :], in0=LW[:, :, 1, :], scalar1=-1.0, scalar2=None, op0=ALU.mult), [PWd], [Ld])
                        dv(lambda e: e.tensor_copy(LIP[:, :, 1, :], LW[:, :, 1, :]), [PWd], [Ld])
                        MWs = p.sb("MWs", [128, 8, 2, GB], F32, sp_)
                        PWy = p.sb("PWy", [128, 9, 2, GB], F32, sp_)
                        PWb = p.sb("PWb", [128, 8, 2, GB], F32, sp_)
                        dv(lambda e: e.tensor_copy(MWs[0:64], MW[0:64]), [PWd], [PWd])
                        dv(lambda e: e.tensor_copy(MWs[64:128], MW[64:128][:, ::-1, :, :]), [PWd], [PWd])
                        dv(lambda e: e.tensor_copy(PWy[0:64], PW[0:64]), [PWd], [PWd])
                        dv(lambda e: e.tensor_copy(PWy[64:128], PW[64:128][:, ::-1, :, :]), [PWd], [PWd])
                        dv(lambda e: e.tensor_copy(PWb[0:64], PW[0:64, 0:8][:, ::-1, :, :]), [PWd], [PWd])
                        dv(lambda e: e.tensor_copy(PWb[64:128], PW[64:128, 0:8]), [PWd], [PWd])
                        den, dend = t2("den")
                        am1, am1d = t2("am1")
                        cr, crd = t2("cr")
                        ci, cid = t2("ci")
                        dv(lambda e: e.tensor_tensor(out=den[:], in0=lre[:], in1=lre[:], op=ALU.mult), [lred], [dend])
                        dv(lambda e: e.tensor_tensor(out=q1[:], in0=lim[:], in1=lim[:], op=ALU.mult), [limd, q1d], [q1d])
                        dv(lambda e: e.tensor_tensor(out=den[:], in0=den[:], in1=q1[:], op=ALU.add), [dend, q1d], [dend])
                        dv(lambda e: e.reciprocal(out=den[:], in_=den[:]), [dend], [dend])
                        dv(lambda e: e.tensor_scalar(out=am1[:], in0=ar, scalar1=-1.0, scalar2=None, op0=ALU.add), [PWd], [am1d])
                        dv(lambda e: e.tensor_tensor(out=q1[:], in0=am1[:], in1=lre[:], op=ALU.mult), [am1d, lred, q1d], [q1d])
                        dv(lambda e: e.tensor_tensor(out=q2[:], in0=ai, in1=lim[:], op=ALU.mult), [PWd, limd, q2d], [q2d])
                        dv(lambda e: e.tensor_tensor(out=q1[:], in0=q1[:], in1=q2[:], op=ALU.add), [q1d, q2d], [q1d])
                        dv(lambda e: e.tensor_tensor(out=cr[:], in0=q1[:], in1=den[:], op=ALU.mult), [q1d, dend], [crd])
                        dv(lambda e: e.tensor_tensor(out=q1[:], in0=ai, in1=lre[:], op=ALU.mult), [PWd, lred, q1d], [q1d])
                        dv(lambda e: e.tensor_tensor(out=q2[:], in0=am1[:], in1=lim[:], op=ALU.mult), [am1d, limd, q2d], [q2d])
                        dv(lambda e: e.tensor_tensor(out=q1[:], in0=q1[:], in1=q2[:], op=ALU.subtract), [q1d, q2d], [q1d])
                        dv(lambda e: e.tensor_tensor(out=ci[:], in0=q1[:], in1=den[:], op=ALU.mult), [q1d, dend], [cid])

                        def t3(name):
                            return p.sb(name, [128, GB, 16], F32, sp_), Dep()
                        Br, Brd = t3("Br")
                        Bi, Bid = t3("Bi")
                        Cr, Crd = t3("Cr")
                        Ci, Cid = t3("Ci")
                        for d in range(2):
                            hs = slice(64 * d, 64 * d + 64)
                            p.dma(Br[hs], s5_b_re[j, d, g0:g0 + GB].rearrange("g q h -> q g h"), writes=[Brd])
                            p.dma(Bi[hs], s5_b_im[j, d, g0:g0 + GB].rearrange("g q h -> q g h"), writes=[Bid])
                        ctr = Rot(sp_, "ct", [128, 128], F32, 2)
                        for (Cdst, Cdd, csrc) in ((Cr, Crd, s5_c_re), (Ci, Cid, s5_c_im)):
                            for g8 in range(GB // 8):
                                ct, ctd = ctr.get()
                                for d in range(2):
                                    p.dma(ct[:, 64 * d:64 * d + 64], csrc[j, d, g0 + 8 * g8:g0 + 8 * g8 + 8].rearrange("g h q -> (g h) q"), writes=[ctd])
                                bk, bkd = bankA()
                                p.op("pe", lambda e: e.transpose(bk[:, 0:128], ct[:, :], identf[:, :]), reads=[ctd, cD], writes=[bkd])
                                p.op("act", lambda e: e.activation(out=Cdst[:, 8 * g8:8 * g8 + 8, :], in_=bk[:, 0:128].rearrange("p (g h) -> p g h", h=16),
                                                                   func=AF.Copy), reads=[bkd], writes=[Cdd])
                        T1 = p.sb("T1", [128, GB, 9, 16], F32, sp_)
                        T1d = Dep()

                        def cmul(dR, dI, dd, Pr, Pi, Pd, Xr_, Xi_, Xd, T1v, negI=False):
                            p.op("dve", lambda e: e.tensor_tensor(out=dR, in0=Xr_, in1=Pr, op=ALU.mult), reads=Pd + Xd, writes=dd)
                            p.op("pool", lambda e: e.tensor_tensor(out=T1v, in0=Xi_, in1=Pi, op=ALU.mult), reads=Pd + Xd, writes=[T1d])
                            p.op("dve", lambda e: e.tensor_tensor(out=dI, in0=Xi_, in1=Pr, op=ALU.mult), reads=Pd + Xd, writes=dd)
                            p.op("dve", lambda e: e.tensor_tensor(out=dR, in0=dR, in1=T1v, op=ALU.subtract), reads=dd + [T1d], writes=dd)
                            p.op("pool", lambda e: e.tensor_tensor(out=T1v, in0=Xr_, in1=Pi, op=ALU.mult), reads=Pd + Xd, writes=[T1d])
                            if negI:
                                p.op("dve", lambda e: e.scalar_tensor_tensor(out=dI, in0=dI, scalar=-1.0, in1=T1v, op0=ALU.mult, op1=ALU.subtract),
                                     reads=dd + [T1d], writes=dd)
                            else:
                                p.op("dve", lambda e: e.tensor_tensor(out=dI, in0=dI, in1=T1v, op=ALU.add), reads=dd + [T1d], writes=dd)

                        def slots(tab, S, ri):
                            return tab[:, :, ri, :].rearrange("p s g -> p g s").unsqueeze(3).to_broadcast([128, GB, S, 16])

                        def overs(x3, S):
                            return x3.unsqueeze(2).to_broadcast([128, GB, S, 16])
                        BbR, BbRd = t3("BbR")
                        BbI, BbId = t3("BbI")
                        cmul(BbR[:], BbI[:], [BbRd, BbId], cr[:].unsqueeze(2).to_broadcast([128, GB, 16]), ci[:].unsqueeze(2).to_broadcast([128, GB, 16]),
                             [crd, cid], Br[:], Bi[:], [Brd, Bid], T1[:, :, 0, :])
                        Bbd = [BbRd, BbId]
                        with p.phase() as sq:
                            XR = p.sb("XR", [128, GB, 8, 16], F32, sq)
                            XI = p.sb("XI", [128, GB, 8, 16], F32, sq)
                            Xd_ = Dep()
                            YR = p.sb("YR", [128, GB, 9, 16], F32, sq)
                            YI = p.sb("YI", [128, GB, 9, 16], F32, sq)
                            Yd_ = Dep()
                            cmul(XR[:], XI[:], [Xd_], slots(MWs, 8, 0), slots(MWs, 8, 1), [PWd], overs(BbR[:], 8), overs(BbI[:], 8), Bbd, T1[:, :, 0:8, :])
                            cmul(YR[:], YI[:], [Yd_], slots(PWy, 9, 0), slots(PWy, 9, 1), [PWd], overs(Cr[:], 9), overs(Ci[:], 9), [Crd, Cid], T1[:], negI=True)
                            tmr = Rot(sq, "tm", [128, 128], F32, 3)
                            for d in range(2):
                                hs = slice(64 * d, 64 * d + 64)
                                y0 = 0 if d == 0 else 1
                                for g in range(GB):
                                    bk, bkd = bankA()
                                    p.mm([lambda e: e.matmul(bk[:, 0:128], lhsT=XR[hs, g, :, :].rearrange("q j h -> q (j h)"),
                                                             rhs=YR[hs, g, y0:y0 + 8, :].rearrange("q j h -> q (j h)"), start=True, stop=False),
                                          lambda e: e.matmul(bk[:, 0:128], lhsT=XI[hs, g, :, :].rearrange("q j h -> q (j h)"),
                                                             rhs=YI[hs, g, y0:y0 + 8, :].rearrange("q j h -> q (j h)"), start=False, stop=True)],
                                         reads=[Xd_, Yd_], writes=[bkd])
                                    if d == 0:
                                        tm, tmd = tmr.get()
                                        p.op("dve", lambda e: e.tensor_tensor(out=tm[:], in0=bk[:, 0:128], in1=mask[:, d, :], op=ALU.mult),
                                             reads=[bkd, mkd], writes=[tmd])
                                        p.op("dve", lambda e: e.scalar_tensor_tensor(out=T0b[d][:, g, :], in0=identf[:, :], scalar=dcol[:, g0 + g:g0 + g + 1],
                                                                                     in1=tm[:], op0=ALU.mult, op1=ALU.add),
                                             reads=[tmd, dcd, cD], writes=[T0d[d][g]])
                                    else:
                                        p.op("dve", lambda e: e.tensor_tensor(out=T0b[d][:, g, :], in0=bk[:, 0:128], in1=mask[:, d, :], op=ALU.mult),
                                             reads=[bkd, mkd], writes=[T0d[d][g]])
                            for (hs, o0) in ((slice(0, 64), 1), (slice(64, 128), 0)):
                                p.op("act", lambda e: e.activation(out=OR[hs].rearrange("q g (j h) -> q g j h", h=16), in_=YR[hs, :, o0:o0 + 8, :], func=AF.Copy),
                                     reads=[Yd_], writes=[Od])
                                p.op("act", lambda e: e.activation(out=OI[hs].rearrange("q g (j h) -> q g j h", h=16), in_=YI[hs, :, o0:o0 + 8, :], func=AF.Copy),
                                     reads=[Yd_], writes=[Od])
                        with p.phase() as sq:
                            BR = p.sb("BR", [128, GB, 8, 16], F32, sq)
                            BI = p.sb("BI", [128, GB, 8, 16], F32, sq)
                            Bd_ = Dep()
                            cmul(BR[:], BI[:], [Bd_], slots(PWb, 8, 0), slots(PWb, 8, 1), [PWd], overs(BbR[:], 8), overs(BbI[:], 8), Bbd, T1[:, :, 0:8, :])
                            for d in range(2):
                                hs = slice(64 * d, 64 * d + 64)
                                for g in range(GB):
                                    bk, bkd = bankA()
                                    p.mm([lambda e: e.transpose(bk[:, 0:64], BR[hs, g, :, :].rearrange("q j h -> q (j h)"), identf[hs, hs]),
                                          lambda e: e.transpose(bk[:, 64:128], BI[hs, g, :, :].rearrange("q j h -> q (j h)"), identf[hs, hs])],
                                         reads=[Bd_, cD], writes=[bkd])
                                    p.op("act", lambda e: e.activation(out=Bcb[d][:, g, :], in_=bk[:, 0:128], func=AF.Copy), reads=[bkd], writes=[Bcd[d][g]])
                    XC = p.sb("XC", [128, 2, GB, 288], F32, st)
                    with p.phase() as sq:
                        xcd = Dep()
                        for g in range(GB):
                            for ri in range(2):
                                bk, bkd = bankA()
                                p.mm([lambda e: e.matmul(bk[0:64, 0:288], lhsT=Bcb[0][:, g, ri * 64:(ri + 1) * 64], rhs=Vb[:, g, :], start=True, stop=True),
                                      lambda e: e.matmul(bk[64:128, 0:32], lhsT=Bcb[1][:, g, ri * 64:(ri + 1) * 64], rhs=Vb[:, g, 0:32][:, ::-1],
                                                         start=True, stop=True),
                                      lambda e: e.matmul(bk[64:128, 32:288], lhsT=Bcb[1][:, g, ri * 64:(ri + 1) * 64], rhs=Vb[:, g, 32:288][:, ::-1],
                                                         start=True, stop=True)],
                                     reads=[Bcd[0][g], Bcd[1][g], Vbd[g]], writes=[bkd])
                                if ri == 0:
                                    p.op("act", lambda e: e.activation(out=XC[:, ri, g, :], in_=bk[:, 0:288], func=AF.Copy), reads=[bkd], writes=[xcd])
                                else:
                                    p.op("dve", lambda e: e.tensor_copy(XC[:, ri, g, :], bk[:, 0:288]), reads=[bkd], writes=[xcd])
                    with p.phase() as sq:
                        shp = [128, 2, GB, NS]
                        Tr = Rot(sq, "T", shp, F32, 2)
                        Bb_ = p.sb("Bq", shp, F32, sq)
                        Bqd = Dep()
                        SIN = p.sb("SIN", shp, F32, sq)
                        SINd = [Dep() for _ in range(NS)]
                        _r3 = {}

                        def Rot_get3(st_):
                            if "r" not in _r3:
                                _r3["r"] = Rot(st_, "m6", [128, 2, GB], F32, 4)
                            return _r3["r"].get()
                        slotD = [Dep() for _ in range(8)]

                        def lr(k):
                            return LRP[:, k, :, :].unsqueeze(3).to_broadcast(shp)

                        def li(k):
                            return LIP[:, k, :, :].unsqueeze(3).to_broadcast(shp)

                        def slot(k):
                            return XC[:, :, :, k::8]
                        tp_, tpd_ = None, None
                        for k in range(1, 9):
                            tk, tkd = Tr.get()
                            if k == 1:
                                p.op("act", lambda e: e.activation(out=tk[:], in_=slot(0), func=AF.Copy), reads=[slotD[0]], writes=[tkd])
                                p.op("dve", lambda e: e.memset(slot(0), 0.0), writes=[slotD[0]])
                            else:
                                p.op("pool", lambda e: e.tensor_tensor(out=tk[:], in0=tp_[:], in1=lr(1), op=ALU.mult), reads=[tpd_, Ld], writes=[tkd])
                                p.op("dve", lambda e: e.tensor_tensor(out=Bb_[:], in0=tp_[:, ::-1, :, :], in1=li(1), op=ALU.mult), reads=[tpd_, Ld], writes=[Bqd])
                                p.op("dve", lambda e: e.tensor_tensor(out=Bb_[:], in0=Bb_[:], in1=slot(k - 1), op=ALU.add), reads=[Bqd, slotD[k - 1]], writes=[Bqd])
                                p.op("dve", lambda e: e.tensor_tensor(out=tk[:], in0=tk[:], in1=Bb_[:], op=ALU.add), reads=[tkd, Bqd], writes=[tkd])
                                p.op("act", lambda e: e.activation(out=slot(k - 1), in_=tp_[:], func=AF.Copy), reads=[tpd_], writes=[slotD[k - 1]])
                            tp_, tpd_ = tk, tkd
                        Z, Zd = tp_, tpd_
                        NB6 = 6
                        sh6 = [128, 2, GB, NB6]
                        L8r = LRP[:, 8, :, :].unsqueeze(3).to_broadcast(sh6)
                        L8i = LIP[:, 8, :, :].unsqueeze(3).to_broadcast(sh6)
                        LLR = p.sb("LLR", [128, 7, 2, GB], F32, sq)
                        LLI = p.sb("LLI", [128, 7, 2, GB], F32, sq)
                        LLd = Dep()
                        llst = contextlib.ExitStack()
                        LL = p.sb("LL", [128, 7, 2, GB], F32, llst)
                        u1 = p.sb("u1", [128, GB], F32, llst)
                        u2 = p.sb("u2", [128, GB], F32, llst)
                        u1d, u2d = Dep(), Dep()
                        p.op("dve", lambda e: e.memset(LL[:, 0, 0, :], 1.0), writes=[LLd])
                        p.op("dve", lambda e: e.memset(LL[:, 0, 1, :], 0.0), writes=[LLd])
                        p.op("dve", lambda e: e.tensor_copy(LL[:, 1, 0, :], LRP[:, 8, 0, :]), reads=[Ld], writes=[LLd])
                        p.op("dve", lambda e: e.tensor_copy(LL[:, 1, 1, :], LIP[:, 8, 1, :]), reads=[Ld], writes=[LLd])
                        for k in range(2, 7):
                            xr_, xi_ = LL[:, k - 1, 0, :], LL[:, k - 1, 1, :]
                            br_, bi_ = LL[:, 1, 0, :], LL[:, 1, 1, :]
                            p.op("dve", lambda e: e.tensor_tensor(out=u1[:], in0=xr_, in1=br_, op=ALU.mult), reads=[LLd, u1d], writes=[u1d])
                            p.op("dve", lambda e: e.tensor_tensor(out=u2[:], in0=xi_, in1=bi_, op=ALU.mult), reads=[LLd, u2d], writes=[u2d])
                            p.op("dve", lambda e: e.tensor_tensor(out=LL[:, k, 0, :], in0=u1[:], in1=u2[:], op=ALU.subtract), reads=[u1d, u2d], writes=[LLd])
                            p.op("dve", lambda e: e.tensor_tensor(out=u1[:], in0=xr_, in1=bi_, op=ALU.mult), reads=[LLd, u1d], writes=[u1d])
                            p.op("dve", lambda e: e.tensor_tensor(out=u2[:], in0=xi_, in1=br_, op=ALU.mult), reads=[LLd, u2d], writes=[u2d])
                            p.op("dve", lambda e: e.tensor_tensor(out=LL[:, k, 1, :], in0=u1[:], in1=u2[:], op=ALU.add), reads=[u1d, u2d], writes=[LLd])
                        p.op("dve", lambda e: e.tensor_copy(LLR[:, :, 0, :], LL[:, :, 0, :]), reads=[LLd], writes=[LLd])
                        p.op("dve", lambda e: e.tensor_copy(LLR[:, :, 1, :], LL[:, :, 0, :]), reads=[LLd], writes=[LLd])
                        p.op("dve", lambda e: e.tensor_scalar(out=LLI[:, :, 0, :], in0=LL[:, :, 1, :], scalar1=-1.0, scalar2=None, op0=ALU.mult), reads=[LLd], writes=[LLd])
                        p.op("dve", lambda e: e.tensor_copy(LLI[:, :, 1, :], LL[:, :, 1, :]), reads=[LLd], writes=[LLd])
                        p.barrier()
                        llst.close()
                        SINall = Dep()

                        def sslot(k):
                            return SIN[:, :, :, k::6]

                        def zslot(k):
                            return Z[:, :, :, k::6]
                        w6 = [(p.sb("w6_%d" % i, sh6, F32, sq), Dep()) for i in range(3)]
                        p.op("dve", lambda e: e.memset(sslot(0), 0.0), writes=[SINall])
                        for k in range(1, NB6 + 1):
                            (ta, tad), (tb, tbd) = w6[0], w6[1]
                            if k == 1:
                                p.op("dve", lambda e: e.tensor_copy(ta[:], zslot(0)), reads=[Zd], writes=[tad])
                            else:
                                p.op("dve", lambda e: e.tensor_tensor(out=ta[:], in0=prev6, in1=L8r, op=ALU.mult), reads=[SINall, Ld, w6[2][1]], writes=[tad])
                                p.op("dve", lambda e: e.tensor_tensor(out=tb[:], in0=prev6[:, ::-1, :, :], in1=L8i, op=ALU.mult), reads=[SINall, Ld, w6[2][1]], writes=[tbd])
                                p.op("dve", lambda e: e.tensor_tensor(out=ta[:], in0=ta[:], in1=zslot(k - 1), op=ALU.add), reads=[tad, Zd], writes=[tad])
                                p.op("dve", lambda e: e.tensor_tensor(out=ta[:], in0=ta[:], in1=tb[:], op=ALU.add), reads=[tad, tbd], writes=[tad])
                            if k < NB6:
                                p.op("dve", lambda e: e.tensor_copy(sslot(k), ta[:]), reads=[tad], writes=[SINall])
                                prev6 = sslot(k)
                            else:
                                p.op("dve", lambda e: e.tensor_copy(w6[2][0][:], ta[:]), reads=[tad], writes=[w6[2][1]])
                        tot, totd = w6[2]
                        Eb = p.sb("Eb", sh6, F32, sq)
                        Ebd = Dep()
                        p.op("dve", lambda e: e.memset(Eb[:, :, :, 0], 0.0), writes=[Ebd])
                        for b_ in range(1, NB6):
                            m1, m1d = Rot_get3(sq)
                            m2, m2d = Rot_get3(sq)
                            p.op("dve", lambda e: e.tensor_tensor(out=m1[:], in0=Eb[:, :, :, b_ - 1], in1=LLR[:, 6, :, :], op=ALU.mult), reads=[Ebd, LLd], writes=[m1d])
                            p.op("dve", lambda e: e.tensor_tensor(out=m2[:], in0=Eb[:, ::-1, :, b_ - 1], in1=LLI[:, 6, :, :], op=ALU.mult), reads=[Ebd, LLd], writes=[m2d])
                            p.op("dve", lambda e: e.tensor_tensor(out=m1[:], in0=m1[:], in1=tot[:, :, :, b_ - 1], op=ALU.add), reads=[m1d, totd], writes=[m1d])
                            p.op("dve", lambda e: e.tensor_tensor(out=Eb[:, :, :, b_], in0=m1[:], in1=m2[:], op=ALU.add), reads=[m1d, m2d, Ebd], writes=[Ebd])
                        for k in range(NB6):
                            (ta, tad), (tb, tbd) = w6[0], w6[1]
                            lr6 = LLR[:, k, :, :].unsqueeze(3).to_broadcast(sh6)
                            li6 = LLI[:, k, :, :].unsqueeze(3).to_broadcast(sh6)
                            p.op("dve", lambda e: e.tensor_tensor(out=ta[:], in0=Eb[:], in1=lr6, op=ALU.mult), reads=[Ebd, LLd], writes=[tad])
                            p.op("dve", lambda e: e.tensor_tensor(out=tb[:], in0=Eb[:, ::-1, :, :], in1=li6, op=ALU.mult), reads=[Ebd, LLd], writes=[tbd])
                            p.op("dve", lambda e: e.tensor_tensor(out=ta[:], in0=ta[:], in1=tb[:], op=ALU.add), reads=[tad, tbd], writes=[tad])
                            p.op("dve", lambda e: e.tensor_tensor(out=sslot(k), in0=sslot(k), in1=ta[:], op=ALU.add), reads=[SINall, tad], writes=[SINall])
                        SINd = [SINall]
                        p.op("act", lambda e: e.activation(out=slot(0), in_=SIN[:], func=AF.Copy), reads=SINd, writes=[slotD[0]])
                        Ab_, Aqd = Tr.get()
                        for k in range(1, 8):
                            p.op("pool", lambda e: e.tensor_tensor(out=Ab_[:], in0=SIN[:], in1=lr(k), op=ALU.mult), reads=SINd + [Ld], writes=[Aqd])
                            p.op("dve", lambda e: e.tensor_tensor(out=Bb_[:], in0=SIN[:, ::-1, :, :], in1=li(k), op=ALU.mult), reads=SINd + [Ld], writes=[Bqd])
                            p.op("dve", lambda e: e.tensor_tensor(out=Bb_[:], in0=Bb_[:], in1=slot(k), op=ALU.add), reads=[Bqd, slotD[k]], writes=[Bqd])
                            p.op("dve", lambda e: e.tensor_tensor(out=slot(k), in0=Bb_[:], in1=Ab_[:], op=ALU.add), reads=[slotD[k], Aqd, Bqd], writes=[slotD[k]])
                    with p.phase() as sq:
                        tr_ = Rot(sq, "ty", [128, 288], F32, 3)
                        ur_ = Rot(sq, "tu", [128, 288], F32, 3)
                        for g in range(GB):
                            bk, bkd = bankA()
                            b2_, b2d = bankA()
                            p.mm([lambda e: e.matmul(bk[:, 0:288], lhsT=T0b[0][:, g, :], rhs=Vb[:, g, :], start=True, stop=False),
                                  lambda e: e.matmul(bk[:, 0:288], lhsT=T0b[1][:, g, :], rhs=Vb[:, g, :], start=False, stop=False),
                                  lambda e: e.matmul(bk[:, 0:288], lhsT=OR[0:64, g, :], rhs=XC[0:64, 0, g, :], start=False, stop=False),
                                  lambda e: e.matmul(bk[:, 0:288], lhsT=OI[0:64, g, :], rhs=XC[0:64, 1, g, :], start=False, stop=True)],
                                 reads=[T0d[0][g], T0d[1][g], Vbd[g], Od], writes=[bkd])
                            p.mm([lambda e: e.matmul(b2_[:, 0:32], lhsT=OR[64:128, g, :], rhs=XC[64:128, 0, g, 0:32][:, ::-1], start=True, stop=False),
                                  lambda e: e.matmul(b2_[:, 0:32], lhsT=OI[64:128, g, :], rhs=XC[64:128, 1, g, 0:32][:, ::-1], start=False, stop=False),
                                  lambda e: e.matmul(b2_[:, 32:288], lhsT=OR[64:128, g, :], rhs=XC[64:128, 0, g, 32:288][:, ::-1], start=True, stop=False),
                                  lambda e: e.matmul(b2_[:, 32:288], lhsT=OI[64:128, g, :], rhs=XC[64:128, 1, g, 32:288][:, ::-1], start=False, stop=True)],
                                 reads=[Od], writes=[b2d])
                            tu, tud = ur_.get()
                            ty, tyd = tr_.get()
                            p.op("act", lambda e: e.activation(out=tu[:], in_=b2_[:, 0:288], func=AF.Copy), reads=[b2d], writes=[tud])
                            p.op("dve", lambda e: e.tensor_tensor(out=ty[:], in0=bk[:, 0:288], in1=tu[:], op=ALU.add), reads=[bkd, tud], writes=[tyd])
                            p.op("act", lambda e: e.activation(out=Vb[:, g, 0:32], in_=ty[:, 0:32], func=AF.Gelu_apprx_tanh), reads=[tyd], writes=[Vbd[g]])
                            p.op("act", lambda e: e.activation(out=Vb[:, g, 32:288].rearrange("p (lt q wl) -> p lt wl q", lt=2, q=4),
                                                               in_=ty[:, 32:288].rearrange("p (lt wl q) -> p lt wl q", lt=2, q=4),
                                                               func=AF.Gelu_apprx_tanh), reads=[tyd], writes=[Vbd[g]])
                    p.dma(YD[:, g0:g0 + GB, :], Vb[:], reads=Vbd, writes=[Dep()])
        YDv = YD.rearrange("(j h) g c -> h g j c", h=16)
        for k in range(16):
            for gl in range(8):
                p.dma(ZTD[16 * gl:16 * gl + 16, k, :].rearrange("h (j c) -> h j c", j=8), YDv[:, 8 * k + gl, :, :], reads=[Dep()], writes=[Dep()])
        p.barrier()
        tiles = []
        for jj in range(8):
            def rowfn_c(cs, jj=jj):
                src = XS[0:256, :].rearrange("(c j) d -> j c d", j=8)
                return [(0, 16, src[jj, 0:16, cs], XD[0]), (16, 16, src[jj, 16:32, cs], XD[1])]
            tiles.append((jj * 288, 32, 1, rowfn_c))
        for lt in range(2):
            for jj in range(8):
                def rowfn_l(cs, jj=jj, lt=lt):
                    res = []
                    for q in range(4):
                        r0 = 256 + (8 * q + jj) * 64 + 32 * lt
                        res.append((32 * q, 32, XS[r0:r0 + 32, cs], XD[2 + (8 * q + jj) // 2]))
                    return res
                tiles.append((jj * 288 + 32 + 128 * lt, 128, 0, rowfn_l))
        out_proj(L, last, s5_w_glu[j], True, tiles)

    for L in range(nlayers):
        last = (L == NL - 1)
        if L % 2 == 0:
            lru_layer(L, last)
        else:
            s5_layer(L, last)
        ffn(L, last)

    with p.phase() as st:
        if debug:
            dxr = Rot(st, "dx", [128, D], F32, 2)
            for t in range(18):
                xt, xd = dxr.get()
                p.dma(xt[:], XS[t * 128:(t + 1) * 128, :], reads=[XD[t]], writes=[xd])
                p.dma(dbg[t * 128:(t + 1) * 128, :], xt[:], reads=[xd])
        fg_ = p.sb("fg", [1, D], F32, st)
        fgd = Dep()
        p.dma(fg_[:], final_g.rearrange("(o n) -> o n", o=1), writes=[fgd])
        Gt = p.sb("Gt", [128, D], F32, st)
        Gd = Dep()
        for k in range(4):
            bk, bkd = bankA()
            p.op("pe", lambda e: e.matmul(bk[:], lhsT=ones1[:], rhs=fg_[0:1, k * 512:(k + 1) * 512], start=True, stop=True), reads=[fgd, cD], writes=[bkd])
            p.op("act", lambda e: e.activation(out=Gt[:, k * 512:(k + 1) * 512], in_=bk[:], func=AF.Copy), reads=[bkd], writes=[Gd])
        nb = nm_bufs(st)
        jr, ssr, rsr = nb
        xr = Rot(st, "xt", [128, D], F32, 4)
        orr = Rot(st, "ot", [128, D], F32, 2)
        for ti in range(2, 18):
            junk, jd = jr.get()
            xt, xd = xr.get()
            p.dma(xt[:], XS[ti * 128:(ti + 1) * 128, :], reads=[XD[ti]], writes=[xd])
            ss, ssd = ssr.get()
            rs, rsd = rsr.get()
            ot, otd = orr.get()
            p.op("act", lambda e: e.activation(out=junk[:], in_=xt[:], func=AF.Square, accum_out=ss[:]), reads=[xd], writes=[jd, ssd])
            p.op("dve", lambda e: e.tensor_scalar(out=rs[:], in0=ss[:], scalar1=1.0 / D, scalar2=EPS, op0=ALU.mult, op1=ALU.add), reads=[ssd], writes=[rsd])
            p.op("act", lambda e: e.activation(out=rs[:], in_=rs[:], func=AF.Sqrt), reads=[rsd], writes=[rsd])
            p.op("dve", lambda e: e.reciprocal(out=rs[:], in_=rs[:]), reads=[rsd], writes=[rsd])
            p.op("dve", lambda e: e.scalar_tensor_tensor(out=ot[:], in0=xt[:], scalar=rs[:, 0:1], in1=Gt[:], op0=ALU.mult, op1=ALU.mult),
                 reads=[xd, rsd, Gd], writes=[otd])
            p.dma(out[(ti - 2) * 128:(ti - 1) * 128, :], ot[:], reads=[otd])
    p.finish()
    return p


def make_inputs(inputs):
    ident = np.eye(128, dtype=np.float32)
    jj = np.arange(128) // 16
    maskf = (jj[None, :] >= jj[:, None]).astype(np.float32)
    maskb = (jj[None, :] <= jj[:, None]).astype(np.float32)
    masks = np.stack([maskf, maskb]).astype(np.float32)
    shared = {k: np.ascontiguousarray(np.asarray(v, dtype=np.float32)) for k, v in inputs.items() if k not in ("x", "c", "ctx", "c_ctx")}
    maps = []
    for b in range(4):
        m = dict(shared)
        m["xin"] = np.ascontiguousarray(np.concatenate([inputs["ctx"][b], inputs["x"][b]], axis=0).astype(np.float32))
        m["cvec"] = np.ascontiguousarray(np.stack([inputs["c"][b], inputs["c_ctx"]]).astype(np.float32))
        m["identin"] = ident
        m["maskin"] = masks
        maps.append(m)
    return maps


def kernel(**inputs):
    inputs = {k: np.asarray(v) for k, v in inputs.items()}
    p = build()
    maps = make_inputs(inputs)
    res = run_bass_kernel_spmd(p.nc, maps, core_ids=list(range(4)))
    return np.stack([np.asarray(r["out"], dtype=np.float32) for r in res.results], axis=0)
```

```python
import contextlib
import math
import numpy as np
import concourse.bass as bass
import concourse.mybir as mybir
from concourse.bass_utils import run_bass_kernel_spmd

F32 = mybir.dt.float32
BF16 = mybir.dt.bfloat16
I32 = mybir.dt.int32
AF = mybir.ActivationFunctionType
ALU = mybir.AluOpType

D = 2048
T = 2304
NCTX = 256
DFF = 5632
NL = 4
EPS = 1e-6


class Dep:
    __slots__ = ("w", "r")

    def __init__(self):
        self.w = None
        self.r = []


class P:
    NDMA = 40

    def __init__(self):
        nc = self.nc = bass.Bass("TRN2", target_bir_lowering=False)
        self.es = contextlib.ExitStack()
        self.eng = {"pe": nc.tensor, "act": nc.scalar, "dve": nc.vector, "pool": nc.gpsimd, "sp": nc.sync}
        self.sem = {e: self.es.enter_context(nc.semaphore("s_" + e)) for e in self.eng}
        self.cnt = {e: 0 for e in self.eng}
        self.known = {e: {} for e in self.eng}
        self.dsem = [self.es.enter_context(nc.semaphore("d%d" % i)) for i in range(self.NDMA)]
        self.duse = [0] * self.NDMA
        self.di = 0
        self.ninst = 0
        self.uid = 0

    def sb(self, name, shape, dt, st=None):
        self.uid += 1
        return (st or self.es).enter_context(self.nc.sbuf_tensor("%s_%d" % (name, self.uid), list(shape), dt))

    def ps(self, name, shape, dt):
        return self.es.enter_context(self.nc.psum_tensor(name, list(shape), dt))

    def _wait(self, e, ev):
        if ev is None:
            return
        sem, val = ev
        k = self.known[e]
        if k.get(sem.num, 0) >= val:
            return
        self.eng[e].wait_ge(sem, val)
        k[sem.num] = val

    def _deps(self, e, reads, writes):
        for d in reads:
            self._wait(e, d.w)
        for d in writes:
            self._wait(e, d.w)
            for ev in d.r:
                self._wait(e, ev)

    def _commit(self, ev, reads, writes):
        for d in reads:
            d.r.append(ev)
            if len(d.r) > 16:
                best = {}
                for s, v in d.r:
                    if best.get(s.num, (None, -1))[1] < v:
                        best[s.num] = (s, v)
                d.r = list(best.values())
        for d in writes:
            d.w = ev
            d.r = []

    def op(self, e, make, reads=(), writes=()):
        self._deps(e, reads, writes)
        inst = make(self.eng[e])
        self.cnt[e] += 1
        ev = (self.sem[e], self.cnt[e])
        inst.then_inc(self.sem[e], 1)
        self._commit(ev, reads, writes)
        self.ninst += 1
        return ev

    def mm(self, steps, reads=(), writes=()):
        self._deps("pe", reads, writes)
        inst = None
        for mk in steps:
            inst = mk(self.eng["pe"])
        self.cnt["pe"] += 1
        ev = (self.sem["pe"], self.cnt["pe"])
        inst.then_inc(self.sem["pe"], 1)
        self._commit(ev, reads, writes)
        self.ninst += len(steps)
        return ev

    def dma(self, out, in_, reads=(), writes=(), q="sp", **kw):
        i = self.di
        self.di = (self.di + 1) % self.NDMA
        if self.duse[i] > 0:
            self._wait(q, (self.dsem[i], 16 * self.duse[i]))
        self._deps(q, reads, writes)
        inst = self.eng[q].dma_start(out=out, in_=in_, **kw)
        self.duse[i] += 1
        ev = (self.dsem[i], 16 * self.duse[i])
        inst.then_inc(self.dsem[i], 16)
        self._commit(ev, reads, writes)
        self.ninst += 1
        return ev

    def barrier(self):
        evs = [(self.sem[e], self.cnt[e]) for e in self.eng if self.cnt[e] > 0]
        evs += [(self.dsem[i], 16 * self.duse[i]) for i in range(self.NDMA) if self.duse[i] > 0]
        for e in self.eng:
            for ev in evs:
                self._wait(e, ev)

    @contextlib.contextmanager
    def phase(self):
        st = contextlib.ExitStack()
        try:
            yield st
        finally:
            self.barrier()
            st.close()

    def finish(self):
        self.barrier()
        self.es.close()


def build(debug=None, nlayers=NL):
    p = P()
    nc = p.nc

    def din(name, shape):
        return nc.dram_tensor(name, list(shape), F32, kind="ExternalInput").ap()

    xin = din("xin", [T, D])
    cvec = din("cvec", [2, D])
    ada_w = din("ada_w", [NL, D, 6 * D])
    ada_b = din("ada_b", [NL, 6 * D])
    norm1_g = din("norm1_g", [NL, D])
    norm2_g = din("norm2_g", [NL, D])
    final_g = din("final_g", [D])
    ffn_w13 = din("ffn_w13", [NL, D, 2 * DFF])
    ffn_w2 = din("ffn_w2", [NL, DFF, D])
    lru_w_in = din("lru_w_in", [2, D, 2 * D])
    lru_conv_w = din("lru_conv_w", [2, 4, D])
    lru_conv_b = din("lru_conv_b", [2, D])
    lru_wa = din("lru_wa", [2, 2, 8, 256, 256])
    lru_ba = din("lru_ba", [2, 2, D])
    lru_wx = din("lru_wx", [2, 2, 8, 256, 256])
    lru_bx = din("lru_bx", [2, 2, D])
    lru_lam = din("lru_lam", [2, 2, D])
    lru_w_out = din("lru_w_out", [2, D, D])
    s5_lam_re = din("s5_lam_re", [2, 2, 128, 64])
    s5_lam_im = din("s5_lam_im", [2, 2, 128, 64])
    s5_log_step = din("s5_log_step", [2, 2, 128])
    s5_b_re = din("s5_b_re", [2, 2, 128, 64, 16])
    s5_b_im = din("s5_b_im", [2, 2, 128, 64, 16])
    s5_c_re = din("s5_c_re", [2, 2, 128, 16, 64])
    s5_c_im = din("s5_c_im", [2, 2, 128, 16, 64])
    s5_d = din("s5_d", [2, D])
    s5_w_glu = din("s5_w_glu", [2, D, 2 * D])
    identin = din("identin", [128, 128])
    maskin = din("maskin", [2, 128, 128])
    out = nc.dram_tensor("out", [T - NCTX, D], F32, kind="ExternalOutput").ap()
    dbg = None
    if debug:
        dbg = nc.dram_tensor("dbg", [T, D], F32, kind="ExternalOutput").ap()
        dbgmod = nc.dram_tensor("dbgmod", [NL, 2, 6 * D], F32, kind="ExternalOutput").ap()

    XS = nc.dram_tensor("XS", [T, D], F32, kind="Internal").ap()
    MOD = nc.dram_tensor("MOD", [NL, 2, 6 * D], F32, kind="Internal").ap()
    ZTD = nc.dram_tensor("ZTD", [128, 16, T], BF16, kind="Internal").ap()
    VD = nc.dram_tensor("VD", [128, 128, 288], BF16, kind="Internal").ap()
    YD = nc.dram_tensor("YD", [128, 128, 288], BF16, kind="Internal").ap()
    class _Fresh:
        def __getitem__(self, i):
            return Dep()
    XD = _Fresh()
    MODD = Dep()
    ZTDD = Dep()
    VDD = Dep()
    YDD = Dep()

    ident = p.sb("ident", [128, 128], BF16)
    identf = p.sb("identf", [128, 128], F32)
    ones1 = p.sb("ones1", [1, 128], F32)
    cD = Dep()
    p.dma(ident[:], identin[:, :], writes=[cD], q="pool")
    p.dma(identf[:], identin[:, :], writes=[cD])
    p.op("dve", lambda e: e.memset(ones1[:], 1.0), writes=[cD])
    psA = [(p.ps("psA%d" % i, [128, 512], F32), Dep()) for i in range(6)]
    psT = [(p.ps("psT%d" % i, [128, 8, 128], BF16), Dep()) for i in range(2)]
    rot = {"A": 0, "T": 0}

    def bankA():
        rot["A"] = (rot["A"] + 1) % 6
        return psA[rot["A"]]

    def bankT():
        rot["T"] = (rot["T"] + 1) % 2
        return psT[rot["T"]]

    for t in range(18):
        p.dma(XS[t * 128:(t + 1) * 128, :], xin[t * 128:(t + 1) * 128, :], writes=[XD[t]])

    class Rot:
        def __init__(self, st, name, shape, dt, n):
            self.b = [(p.sb(name, shape, dt, st), Dep()) for _ in range(n)]
            self.i = 0

        def get(self):
            self.i = (self.i + 1) % len(self.b)
            return self.b[self.i]

    def bcast_rows(st, specs):
        res = []
        dsts = [(p.sb("bc", [128, D], F32, st), Dep()) for _ in specs]
        tmp_st = contextlib.ExitStack()
        r0 = Rot(tmp_st, "r0", [1, D], F32, 1)
        r1 = Rot(tmp_st, "r1", [1, D], F32, 1)
        for si, sp in enumerate(specs):
            dst, dd = dsts[si]
            a, ad = r0.get()
            p.dma(a[:], sp[1], reads=[MODD], writes=[ad])
            if sp[0] == "g1p":
                b, bd = r1.get()
                p.dma(b[:], sp[2], reads=[MODD], writes=[bd])
                p.op("dve", lambda e: e.scalar_tensor_tensor(out=a[:], in0=b[:], scalar=1.0, in1=a[:], op0=ALU.add, op1=ALU.mult),
                     reads=[ad, bd], writes=[ad])
            for k in range(4):
                bk, bkd = bankA()
                p.op("pe", lambda e: e.matmul(bk[:], lhsT=ones1[:], rhs=a[0:1, k * 512:(k + 1) * 512], start=True, stop=True),
                     reads=[ad, cD], writes=[bkd])
                p.op("act", lambda e: e.activation(out=dst[:, k * 512:(k + 1) * 512], in_=bk[:], func=AF.Copy), reads=[bkd], writes=[dd])
            res.append((dst, dd))
        p.barrier()
        tmp_st.close()
        return res

    def modrow(L, v, idx):
        return MOD[L, v:v + 1, idx * D:(idx + 1) * D]

    def norm_mod(st_bufs, xt, xd, npart, Mt, Md, SHt, SHd, outap, outd, v3=False):
        jr, ssr, rsr = st_bufs
        junk, jd = jr.get()
        ss, ssd = ssr.get()
        rs, rsd = rsr.get()
        p.op("act", lambda e: e.activation(out=junk[:npart], in_=xt, func=AF.Square, accum_out=ss[:npart]), reads=[xd], writes=[jd, ssd])
        p.op("dve", lambda e: e.tensor_scalar(out=rs[:npart], in0=ss[:npart], scalar1=1.0 / D, scalar2=EPS, op0=ALU.mult, op1=ALU.add),
             reads=[ssd], writes=[rsd])
        p.op("act", lambda e: e.activation(out=rs[:npart], in_=rs[:npart], func=AF.Sqrt), reads=[rsd], writes=[rsd])
        p.op("dve", lambda e: e.reciprocal(out=rs[:npart], in_=rs[:npart]), reads=[rsd], writes=[rsd])
        p.op("dve", lambda e: e.scalar_tensor_tensor(out=junk[:npart], in0=xt, scalar=rs[:npart, 0:1], in1=Mt[:npart],
                                                     op0=ALU.mult, op1=ALU.mult), reads=[xd, rsd, Md], writes=[jd])
        i0, i1 = junk[:npart], SHt[:npart]
        if v3:
            i0 = i0.rearrange("p (g h) -> p g h", h=16)
            i1 = i1.rearrange("p (g h) -> p g h", h=16)
        p.op("dve", lambda e: e.tensor_tensor(out=outap, in0=i0, in1=i1, op=ALU.add), reads=[jd, SHd], writes=[outd])

    def nm_bufs(st, nj=2):
        return (Rot(st, "junk", [128, D], F32, nj), Rot(st, "ss", [128, 1], F32, 3), Rot(st, "rs", [128, 1], F32, 3))

    def to_fm(hb, hbd, ntok, HT, HTd, tok0):
        for kb in range(2):
            bk, bkd = bankT()
            for kk in range(8):
                k = kb * 8 + kk
                p.op("pe", lambda e: e.transpose(bk[:, kk, 0:ntok], hb[0:ntok, k * 128:(k + 1) * 128], ident[0:ntok, 0:ntok]),
                     reads=[hbd, cD], writes=[bkd])
            eng = "act" if kb == 0 else "dve"
            if eng == "act":
                p.op("act", lambda e: e.activation(out=HT[:, kb * 8:(kb + 1) * 8, tok0:tok0 + ntok], in_=bk[:, :, 0:ntok], func=AF.Copy),
                     reads=[bkd], writes=[HTd])
            else:
                p.op("dve", lambda e: e.tensor_copy(HT[:, kb * 8:(kb + 1) * 8, tok0:tok0 + ntok], bk[:, :, 0:ntok]), reads=[bkd], writes=[HTd])

    sT = p.sb("sT", [128, 32], BF16)
    sTd = Dep()
    with p.phase() as st:
        c32 = p.sb("c32", [32, 128], F32, st)
        c32d = Dep()
        p.dma(c32[:], cvec.rearrange("v (k q) -> (v k) q", q=128), writes=[c32d])
        bk, bkd = bankA()
        p.op("pe", lambda e: e.transpose(bk[:, 0:32], c32[:, :], identf[0:32, 0:32]), reads=[c32d, cD], writes=[bkd])
        p.op("act", lambda e: e.activation(out=sT[:], in_=bk[:, 0:32], func=AF.Silu), reads=[bkd], writes=[sTd])

    def adaln_tile(L, nt, w3, wd, br, mr):
        cs = slice(nt * 512, (nt + 1) * 512)
        p.dma(w3, ada_w[L, :, cs].rearrange("(k q) n -> q k n", q=128), writes=[wd], q="pool")
        b_, bd_ = br.get()
        p.dma(b_[:], ada_b[L:L + 1, cs].broadcast_to([2, 512]), writes=[bd_])
        bk, bkd = bankA()
        p.mm([(lambda e, k=k: e.matmul(bk[0:2, :], lhsT=sT[:, k:32:16], rhs=w3[:, k, :], start=(k == 0), stop=(k == 15))) for k in range(16)],
             reads=[sTd, wd], writes=[bkd])
        m_, md_ = mr.get()
        p.op("dve", lambda e: e.tensor_tensor(out=m_[:], in0=bk[0:2, :], in1=b_[:], op=ALU.add), reads=[bkd, bd_], writes=[md_])
        p.dma(MOD[L, :, cs], m_[:], reads=[md_], writes=[MODD])
        if debug:
            p.dma(dbgmod[L, :, cs], m_[:], reads=[md_])

    with p.phase() as st:
        wr = Rot(st, "adaw", [128, 16, 512], BF16, 3)
        br = Rot(st, "adab", [2, 512], F32, 3)
        mr = Rot(st, "adam", [2, 512], F32, 3)
        for nt in range(24):
            w, wd = wr.get()
            adaln_tile(0, nt, w[:], wd, br, mr)

    def ffn(L, last):
        tl = list(range(2, 18)) if last else list(range(18))
        half = len(tl) // 2
        ada_pending = list(range(24)) if (L + 1 < nlayers) else []
        for part in (tl[:half], tl[half:]):
            nt_ = len(part) * 128
            subs = []
            o_ = 0
            while o_ < nt_:
                n_ = min(384, nt_ - o_)
                subs.append((o_, n_))
                o_ += n_
            kinds = sorted(set(1 if ti < 2 else 0 for ti in part))
            with p.phase() as stG:
                GT = p.sb("GT", [128, 44, nt_], BF16, stG)
                GTd = [Dep() for _ in range(44)]
                with p.phase() as stH:
                    HT = p.sb("HT", [128, 16, nt_], BF16, stH)
                    HTd = Dep()
                    with p.phase() as st2:
                        bc = {}
                        for v in kinds:
                            bc[v] = bcast_rows(st2, [("g1p", norm2_g[L:L + 1, :], modrow(L, v, 4)), ("row", modrow(L, v, 3))])
                        nb = nm_bufs(st2, 1)
                        xr = Rot(st2, "xt", [128, D], F32, 2)
                        hr = Rot(st2, "hb", [128, D], BF16, 2)
                        for i, ti in enumerate(part):
                            (Mt, Md), (SHt, SHd) = bc[1 if ti < 2 else 0]
                            xt, xd = xr.get()
                            p.dma(xt[:], XS[ti * 128:(ti + 1) * 128, :], reads=[XD[ti]], writes=[xd])
                            hb, hbd = hr.get()
                            norm_mod(nb, xt[:], xd, 128, Mt, Md, SHt, SHd, hb[:], hbd)
                            to_fm(hb, hbd, 128, HT, HTd, i * 128)
                    with p.phase() as st2:
                        wr = Rot(st2, "w13", [128, 16, 2, 256], BF16, 3)
                        sr = Rot(st2, "sa", [128, 384], F32, 3)
                        abr = Rot(st2, "adab", [2, 512], F32, 2)
                        amr = Rot(st2, "adam", [2, 512], F32, 2)
                        for fg in range(22):
                            if ada_pending:
                                aw, awd = wr.get()
                                adaln_tile(L + 1, ada_pending.pop(0), aw[:, :, :, :].rearrange("p k a n -> p k (a n)"), awd, abr, amr)
                            w, wd = wr.get()
                            p.dma(w[:, :, 0, :], ffn_w13[L, :, fg * 256:(fg + 1) * 256].rearrange("(k q) n -> q k n", q=128), writes=[wd], q="pool")
                            p.dma(w[:, :, 1, :], ffn_w13[L, :, DFF + fg * 256:DFF + (fg + 1) * 256].rearrange("(k q) n -> q k n", q=128),
                                  writes=[wd], q="pool")
                            for c2 in range(2):
                                f = fg * 2 + c2
                                for (o_, n_) in subs:
                                    ba_, bad = bankA()
                                    bb_, bbd = bankA()
                                    p.mm([(lambda e, k=k: e.matmul(ba_[:, 0:n_], lhsT=w[:, k, 0, c2 * 128:(c2 + 1) * 128], rhs=HT[:, k, o_:o_ + n_],
                                                                  start=(k == 0), stop=(k == 15))) for k in range(16)], reads=[wd, HTd], writes=[bad])
                                    p.mm([(lambda e, k=k: e.matmul(bb_[:, 0:n_], lhsT=w[:, k, 1, c2 * 128:(c2 + 1) * 128], rhs=HT[:, k, o_:o_ + n_],
                                                                  start=(k == 0), stop=(k == 15))) for k in range(16)], reads=[wd, HTd], writes=[bbd])
                                    sa, sad = sr.get()
                                    p.op("act", lambda e: e.activation(out=sa[:, 0:n_], in_=ba_[:, 0:n_], func=AF.Silu), reads=[bad], writes=[sad])
                                    p.op("dve", lambda e: e.tensor_tensor(out=GT[:, f, o_:o_ + n_], in0=sa[:, 0:n_], in1=bb_[:, 0:n_], op=ALU.mult),
                                         reads=[sad, bbd], writes=[GTd[f]])
                with p.phase() as st2:
                    g2 = {}
                    for v in kinds:
                        (g2[v],) = bcast_rows(st2, [("row", modrow(L, v, 5))])
                    wr = Rot(st2, "w2", [128, 22, 512], BF16, 3)
                    xr = Rot(st2, "xp", [128, 512], F32, 4)
                    tr = Rot(st2, "tp", [128, 512], F32, 3)
                    for nt in range(4):
                        wh = [wr.get(), wr.get()]
                        for kq in range(4):
                            w_, wd_ = wh[kq // 2]
                            p.dma(w_[:, (kq % 2) * 11:(kq % 2 + 1) * 11, :],
                                  ffn_w2[L, kq * 1408:(kq + 1) * 1408, nt * 512:(nt + 1) * 512].rearrange("(k q) n -> q k n", q=128), writes=[wd_], q="pool")
                        for i, ti in enumerate(part):
                            G2t, G2d = g2[1 if ti < 2 else 0]
                            bk, bkd = bankA()
                            p.mm([(lambda e, k=k: e.matmul(bk[:, :], lhsT=GT[:, k, i * 128:(i + 1) * 128], rhs=wh[k // 22][0][:, k % 22, :],
                                                          start=(k == 0), stop=(k == 43))) for k in range(44)], reads=[wh[0][1], wh[1][1]] + GTd, writes=[bkd])
                            xp, xpd = xr.get()
                            p.dma(xp[:], XS[ti * 128:(ti + 1) * 128, nt * 512:(nt + 1) * 512], reads=[XD[ti]], writes=[xpd], q="act")
                            tp, tpd = tr.get()
                            p.op("dve", lambda e: e.tensor_tensor(out=tp[:], in0=bk[:, :], in1=G2t[:, nt * 512:(nt + 1) * 512], op=ALU.mult),
                                 reads=[bkd, G2d], writes=[tpd])
                            p.op("dve", lambda e: e.tensor_tensor(out=tp[:], in0=tp[:], in1=xp[:], op=ALU.add), reads=[tpd, xpd], writes=[tpd])
                            p.dma(XS[ti * 128:(ti + 1) * 128, nt * 512:(nt + 1) * 512], tp[:], reads=[tpd], writes=[XD[ti]])

    def out_proj(L, last, wsrc, glu, tiles):
        OW = 256 if glu else 512
        with p.phase() as st:
            ZT = p.sb("ZT", [128, 16, T], BF16, st)
            ZTd = Dep()
            for k in range(16):
                p.dma(ZT[:, k, :], ZTD[:, k, :], reads=[Dep()], writes=[ZTd])
            g1 = {}
            for v in ((0, 1) if not last else (0,)):
                (g1[v],) = bcast_rows(st, [("row", modrow(L, v, 2))])
            wr = Rot(st, "wo", [128, 16, 512], BF16, 2)
            xr = Rot(st, "xp", [128, OW], F32, 5)
            tr = Rot(st, "tp", [128, OW], F32, 4)
            sr = Rot(st, "sg", [128, 256], F32, 3)
            def wload(nt):
                w, wd = wr.get()
                if glu:
                    p.dma(w[:, :, 0:256], wsrc[:, nt * 256:(nt + 1) * 256].rearrange("(k q) n -> q k n", q=128), writes=[wd], q="pool")
                    p.dma(w[:, :, 256:512], wsrc[:, D + nt * 256:D + (nt + 1) * 256].rearrange("(k q) n -> q k n", q=128), writes=[wd], q="pool")
                else:
                    p.dma(w[:, :, :], wsrc[:, nt * 512:(nt + 1) * 512].rearrange("(k q) n -> q k n", q=128), writes=[wd], q="pool")
                return w, wd
            nxt = wload(0)
            for nt in range(D // OW):
                w, wd = nxt
                if nt + 1 < D // OW:
                    nxt = wload(nt + 1)
                for (z0, ntok, isctx, rowfn) in tiles:
                    if last and isctx:
                        continue
                    G1t, G1d = g1[isctx]
                    bk, bkd = bankA()
                    p.mm([(lambda e, k=k: e.matmul(bk[0:ntok, :], lhsT=ZT[:, k, z0:z0 + ntok], rhs=w[:, k, :],
                                                  start=(k == 0), stop=(k == 15))) for k in range(16)], reads=[wd, ZTd], writes=[bkd])
                    cs = slice(nt * OW, (nt + 1) * OW)
                    rows = rowfn(cs)
                    xp, xpd = xr.get()
                    for (p0, np_, ap_, dep_) in rows:
                        p.dma(xp[p0:p0 + np_, :], ap_, reads=[dep_], writes=[xpd], q="act")
                    tp, tpd = tr.get()
                    if glu:
                        sg, sgd = sr.get()
                        p.op("act", lambda e: e.activation(out=sg[0:ntok], in_=bk[0:ntok, 256:512], func=AF.Sigmoid), reads=[bkd], writes=[sgd])
                        p.op("dve", lambda e: e.tensor_tensor(out=tp[0:ntok], in0=bk[0:ntok, 0:256], in1=sg[0:ntok], op=ALU.mult),
                             reads=[bkd, sgd], writes=[tpd])
                        p.op("dve", lambda e: e.tensor_tensor(out=tp[0:ntok], in0=tp[0:ntok], in1=G1t[0:ntok, cs], op=ALU.mult),
                             reads=[tpd, G1d], writes=[tpd])
                    else:
                        p.op("dve", lambda e: e.tensor_tensor(out=tp[0:ntok], in0=bk[0:ntok, :], in1=G1t[0:ntok, cs], op=ALU.mult),
                             reads=[bkd, G1d], writes=[tpd])
                    p.op("dve", lambda e: e.tensor_tensor(out=tp[0:ntok], in0=tp[0:ntok], in1=xp[0:ntok], op=ALU.add), reads=[tpd, xpd], writes=[tpd])
                    for (p0, np_, ap_, dep_) in rows:
                        p.dma(ap_, tp[p0:p0 + np_, :], reads=[tpd], writes=[dep_], q="sp")

    def lru_layer(L, last):
        j = L // 2
        with p.phase() as st:
            HT = p.sb("HT", [128, 16, T], BF16, st)
            HTd = Dep()
            with p.phase() as st2:
                nb = nm_bufs(st2)
                xr = Rot(st2, "xt", [128, D], F32, 4)
                hr = Rot(st2, "hb", [128, D], BF16, 2)
                for isctx in (1, 0):
                    with p.phase() as st3:
                        (Mt, Md), (SHt, SHd) = bcast_rows(st3, [("g1p", norm1_g[L:L + 1, :], modrow(L, isctx, 1)), ("row", modrow(L, isctx, 0))])
                        for ti in (range(0, 2) if isctx else range(2, 18)):
                            xt, xd = xr.get()
                            p.dma(xt[:], XS[ti * 128:(ti + 1) * 128, :], reads=[XD[ti]], writes=[xd])
                            hb, hbd = hr.get()
                            norm_mod(nb, xt[:], xd, 128, Mt, Md, SHt, SHd, hb[:], hbd)
                            to_fm(hb, hbd, 128, HT, HTd, ti * 128)
            with p.phase() as st2:
                colT = p.sb("colT", [128, 16 * 16], F32, st2)
                colN = [0]
                stage = Rot(st2, "colstage", [16, 128], F32, 2)

                def colload(name, src_row):
                    t_, td = stage.get()
                    p.dma(t_[:], src_row.rearrange("(k q) -> k q", q=128), writes=[td])
                    bk, bkd = bankA()
                    p.op("pe", lambda e: e.transpose(bk[:, 0:16], t_[:, :], identf[0:16, 0:16]), reads=[td, cD], writes=[bkd])
                    o_ = colT[:, colN[0] * 16:(colN[0] + 1) * 16]
                    colN[0] += 1
                    od = Dep()
                    p.op("act", lambda e: e.activation(out=o_[:], in_=bk[:, 0:16], func=AF.Copy), reads=[bkd], writes=[od])
                    return o_, od
                cw = [colload("cw%d" % k, lru_conv_w[j, k, :]) for k in range(4)]
                cb = colload("cb", lru_conv_b[j, :])
                ba = [colload("ba%d" % d, lru_ba[j, d, :]) for d in range(2)]
                bx = [colload("bx%d" % d, lru_bx[j, d, :]) for d in range(2)]
                cl = []
                for d in range(2):
                    lm, lmd = colload("lam%d" % d, lru_lam[j, d, :])
                    p.op("act", lambda e: e.activation(out=lm[:], in_=lm[:], func=AF.Exp, scale=-1.0), reads=[lmd], writes=[lmd])
                    p.op("dve", lambda e: e.tensor_scalar(out=lm[:], in0=lm[:], scalar1=1.0, scalar2=None, op0=ALU.add), reads=[lmd], writes=[lmd])
                    p.op("act", lambda e: e.activation(out=lm[:], in_=lm[:], func=AF.Ln), reads=[lmd], writes=[lmd])
                    p.op("dve", lambda e: e.tensor_scalar(out=lm[:], in0=lm[:], scalar1=-8.0, scalar2=None, op0=ALU.mult), reads=[lmd], writes=[lmd])
                    cl.append((lm, lmd))
                W = T + 6
                OFFC, OFFL = 1, 260
                XB = p.sb("XB", [128, 2, W], F32, st2)
                XBd = [Dep(), Dep()]
                p.op("dve", lambda e: e.memset(XB[:, :, :], 0.0), writes=XBd)
                XC = p.sb("XC", [128, 2, T], F32, st2)
                XCd = [Dep(), Dep()]
                XCb = p.sb("XCb", [128, 2, T], BF16, st2)
                XCbd = [Dep(), Dep()]
                GG = p.sb("GG", [128, 2, T], BF16, st2)
                GGd = [Dep(), Dep()]
                YA = p.sb("YA", [128, 2, T], F32, st2)
                YAd = [Dep(), Dep()]
                wr = Rot(st2, "win", [128, 16, 128], BF16, 2)
                gwr = Rot(st2, "gw", [128, 2, 256], BF16, 4)
                RFr = Rot(st2, "RF", [128, T], F32, 2)
                IFr = Rot(st2, "IF", [128, T], F32, 2)
                AF2 = p.sb("AF2", [128, T], F32, st2)
                AF2d = Dep()
                subt = [(0, 256)] + [(256 + 512 * i, 512) for i in range(4)]

                def xboff(t0):
                    return (OFFC + t0) if t0 < 256 else (OFFL + t0 - 256)
                for hd in range(8):
                    for cc in range(2):
                        ch = hd * 2 + cc
                        for which in range(2):
                            w, wd = wr.get()
                            p.dma(w[:], lru_w_in[j, :, which * D + ch * 128: which * D + (ch + 1) * 128].rearrange("(k q) n -> q k n", q=128),
                                  writes=[wd], q="pool")
                            for (t0, n_) in subt:
                                bk, bkd = bankA()
                                p.mm([(lambda e, k=k: e.matmul(bk[:, 0:n_], lhsT=w[:, k, :], rhs=HT[:, k, t0:t0 + n_], start=(k == 0), stop=(k == 15)))
                                      for k in range(16)], reads=[wd, HTd], writes=[bkd])
                                if which == 0:
                                    p.op("act", lambda e: e.activation(out=GG[:, cc, t0:t0 + n_], in_=bk[:, 0:n_], func=AF.Gelu_apprx_tanh),
                                         reads=[bkd], writes=[GGd[cc]])
                                else:
                                    o0 = xboff(t0)
                                    p.op("act", lambda e: e.activation(out=XB[:, cc, o0:o0 + n_], in_=bk[:, 0:n_], func=AF.Copy),
                                         reads=[bkd], writes=[XBd[cc]])
                        for (s0, sn, off) in ((0, 256, OFFC), (256, 2048, OFFL)):
                            p.op("dve", lambda e: e.tensor_scalar(out=XC[:, cc, s0:s0 + sn], in0=XB[:, cc, off:off + sn], scalar1=cw[1][0][:, ch:ch + 1],
                                                                  scalar2=cb[0][:, ch:ch + 1], op0=ALU.mult, op1=ALU.add),
                                 reads=[XBd[cc], cw[1][1], cb[1]], writes=[XCd[cc]])
                            for (k, sh) in ((0, -1), (2, 1), (3, 2)):
                                p.op("dve", lambda e: e.scalar_tensor_tensor(out=XC[:, cc, s0:s0 + sn], in0=XB[:, cc, off + sh:off + sh + sn],
                                                                             scalar=cw[k][0][:, ch:ch + 1], in1=XC[:, cc, s0:s0 + sn],
                                                                             op0=ALU.mult, op1=ALU.add),
                                     reads=[XBd[cc], cw[k][1], XCd[cc]], writes=[XCd[cc]])
                        p.op("act", lambda e: e.activation(out=XCb[:, cc, :], in_=XC[:, cc, :], func=AF.Copy), reads=[XCd[cc]], writes=[XCbd[cc]])
                    for d in range(2):
                        gwa, gwad = gwr.get()
                        gwx, gwxd = gwr.get()
                        p.dma(gwa[:], lru_wa[j, d, hd].rearrange("(k q) n -> q k n", q=128), writes=[gwad], q="pool")
                        p.dma(gwx[:], lru_wx[j, d, hd].rearrange("(k q) n -> q k n", q=128), writes=[gwxd], q="pool")
                        for oc in range(2):
                            ch = hd * 2 + oc
                            RF, RFd = RFr.get()
                            IF, IFd = IFr.get()
                            for (t0, n_) in subt:
                                bR, bRd = bankA()
                                bI, bId = bankA()
                                p.mm([(lambda e, k=k: e.matmul(bR[:, 0:n_], lhsT=gwa[:, k, oc * 128:(oc + 1) * 128], rhs=XCb[:, k, t0:t0 + n_],
                                                              start=(k == 0), stop=(k == 1))) for k in range(2)], reads=[gwad, XCbd[0], XCbd[1]], writes=[bRd])
                                p.mm([(lambda e, k=k: e.matmul(bI[:, 0:n_], lhsT=gwx[:, k, oc * 128:(oc + 1) * 128], rhs=XCb[:, k, t0:t0 + n_],
                                                              start=(k == 0), stop=(k == 1))) for k in range(2)], reads=[gwxd, XCbd[0], XCbd[1]], writes=[bId])
                                p.op("act", lambda e: e.activation(out=RF[:, t0:t0 + n_], in_=bR[:, 0:n_], func=AF.Sigmoid, bias=ba[d][0][:, ch:ch + 1]),
                                     reads=[bRd, ba[d][1]], writes=[RFd])
                                p.op("act", lambda e: e.activation(out=IF[:, t0:t0 + n_], in_=bI[:, 0:n_], func=AF.Sigmoid, bias=bx[d][0][:, ch:ch + 1]),
                                     reads=[bId, bx[d][1]], writes=[IFd])
                            p.op("act", lambda e: e.activation(out=RF[:, :], in_=RF[:, :], func=AF.Exp, scale=cl[d][0][:, ch:ch + 1]), reads=[RFd, cl[d][1]], writes=[RFd])
                            p.op("dve", lambda e: e.tensor_tensor(out=AF2[:, :], in0=RF[:, :], in1=RF[:, :], op=ALU.mult), reads=[RFd], writes=[AF2d])
                            p.op("act", lambda e: e.activation(out=AF2[:, :], in_=AF2[:, :], func=AF.Sqrt, scale=-1.0, bias=1.0), reads=[AF2d], writes=[AF2d])
                            p.op("dve", lambda e: e.tensor_tensor(out=IF[:, :], in0=IF[:, :], in1=XC[:, oc, :], op=ALU.mult), reads=[IFd, XCd[oc]], writes=[IFd])
                            p.op("dve", lambda e: e.tensor_tensor(out=IF[:, :], in0=IF[:, :], in1=AF2[:, :], op=ALU.mult), reads=[IFd, AF2d], writes=[IFd])
                            if d == 0:
                                p.op("dve", lambda e: e.tensor_tensor_scan(out=YA[:, oc, :], data0=RF[:, :], data1=IF[:, :], initial=0.0,
                                                                           op0=ALU.mult, op1=ALU.add), reads=[RFd, IFd, YAd[oc]], writes=[YAd[oc]])
                            else:
                                p.op("dve", lambda e: e.tensor_tensor_scan(out=AF2[:, 0:256][:, ::-1], data0=RF[:, 0:256][:, ::-1], data1=IF[:, 0:256][:, ::-1],
                                                                           initial=0.0, op0=ALU.mult, op1=ALU.add), reads=[RFd, IFd, AF2d], writes=[AF2d])
                                p.op("dve", lambda e: e.tensor_tensor_scan(out=AF2[:, 256:T][:, ::-1], data0=RF[:, 256:T][:, ::-1], data1=IF[:, 256:T][:, ::-1],
                                                                           initial=AF2[:, 0:1], op0=ALU.mult, op1=ALU.add), reads=[RFd, IFd, AF2d], writes=[AF2d])
                                p.op("dve", lambda e: e.tensor_tensor(out=YA[:, oc, :], in0=YA[:, oc, :], in1=AF2[:, :], op=ALU.add),
                                     reads=[YAd[oc], AF2d], writes=[YAd[oc]])
                    for oc in range(2):
                        ch = hd * 2 + oc
                        p.op("dve", lambda e: e.tensor_tensor(out=GG[:, oc, :], in0=YA[:, oc, :], in1=GG[:, oc, :], op=ALU.mult),
                             reads=[YAd[oc], GGd[oc]], writes=[GGd[oc]])
                        p.dma(ZTD[:, ch, :], GG[:, oc, :], reads=[GGd[oc]], writes=[Dep()])
        tiles = []
        for ti in range(18):
            def rowfn(cs, ti=ti):
                return [(0, 128, XS[ti * 128:(ti + 1) * 128, cs], XD[ti])]
            tiles.append((ti * 128, 128, 1 if ti < 2 else 0, rowfn))
        out_proj(L, last, lru_w_out[j], False, tiles)

    def s5_col(c):
        w_ = c // 4
        return 32 + 128 * (w_ // 32) + 32 * (c % 4) + (w_ % 32)

    def s5_layer(L, last):
        j = L // 2
        TWO_PI = 2.0 * math.pi
        with p.phase() as st:
            nb = nm_bufs(st)
            xr = Rot(st, "xt", [128, D], F32, 4)
            Ab = p.sb("Ab", [128, 128, 8, 16], BF16, st)
            Abd = Dep()
            vst = Rot(st, "vst", [128, 8, 128], BF16, 3)
            for isctx in (1, 0):
                with p.phase() as st3:
                    (Mt, Md), (SHt, SHd) = bcast_rows(st3, [("g1p", norm1_g[L:L + 1, :], modrow(L, isctx, 1)), ("row", modrow(L, isctx, 0))])
                    for lt in ((0,) if isctx else (0, 1)):
                        npart = 32 if isctx else 128
                        col0 = 0 if isctx else 32 + 128 * lt
                        for jj in range(8):
                            xt, xd = xr.get()
                            if isctx:
                                src = XS[0:256, :].rearrange("(c j) d -> j c d", j=8)
                                p.dma(xt[0:16, :], src[jj, 0:16, :], reads=[XD[0]], writes=[xd])
                                p.dma(xt[16:32, :], src[jj, 16:32, :], reads=[XD[1]], writes=[xd])
                            else:
                                for q in range(4):
                                    r0 = 256 + (8 * q + jj) * 64 + 32 * lt
                                    p.dma(xt[32 * q:32 * q + 32, :], XS[r0:r0 + 32, :], reads=[XD[2 + (8 * q + jj) // 2]], writes=[xd])
                            norm_mod(nb, xt[0:npart], xd, npart, Mt, Md, SHt, SHd, Ab[0:npart, :, jj, :], Abd, v3=True)
                        for g8 in range(16):
                            bk, bkd = bankT()
                            for gg in range(8):
                                g = g8 * 8 + gg
                                p.op("pe", lambda e: e.transpose(bk[:, gg, 0:npart], Ab[0:npart, g, :, :].rearrange("p j h -> p (j h)"),
                                                                 ident[0:npart, 0:npart]), reads=[Abd, cD], writes=[bkd])
                            vs, vsd = vst.get()
                            if isctx:
                                vo, vi = vs[:, :, 0:npart], bk[:, :, 0:npart]
                            else:
                                vo = vs[:, :, :].rearrange("p g (wl q) -> p g q wl", q=4)
                                vi = bk[:, :, :].rearrange("p g (q wl) -> p g q wl", q=4)
                            if g8 % 2 == 0:
                                p.op("act", lambda e: e.activation(out=vo, in_=vi, func=AF.Copy), reads=[bkd], writes=[vsd])
                            else:
                                p.op("dve", lambda e: e.tensor_copy(vo, vi), reads=[bkd], writes=[vsd])
                            p.dma(VD[:, g8 * 8:(g8 + 1) * 8, col0:col0 + npart], vs[:, :, 0:npart], reads=[vsd], writes=[Dep()])

        GB = 32
        NS = 36
        TWO_PI = 2.0 * math.pi
        with p.phase() as st0:
            dcol = p.sb("dcol", [128, 128], F32, st0)
            dcd = Dep()
            for jj in range(8):
                p.dma(dcol[16 * jj:16 * jj + 16, :], s5_d[j, :].rearrange("(g h) -> h g", h=16), writes=[dcd], allow_slow_non_contiguous=True)
            mask = p.sb("mask", [128, 2, 128], F32, st0)
            mkd = Dep()
            for d in range(2):
                p.dma(mask[:, d, :], maskin[d], writes=[mkd])
            for gb in range(128 // GB):
                g0 = gb * GB
                with p.phase() as st:
                    Vb = p.sb("Vb", [128, GB, 288], BF16, st)
                    Vbd = [Dep() for _ in range(GB)]
                    p.dma(Vb[:], VD[:, g0:g0 + GB, :], reads=[Dep()], writes=Vbd)
                    T0b = [p.sb("T0b", [128, GB, 128], BF16, st) for _ in range(2)]
                    T0d = [[Dep() for _ in range(GB)] for _ in range(2)]
                    Bcb = [p.sb("Bcb", [128, GB, 128], BF16, st) for _ in range(2)]
                    Bcd = [[Dep() for _ in range(GB)] for _ in range(2)]
                    OR = p.sb("OR", [128, GB, 128], F32, st)
                    OI = p.sb("OI", [128, GB, 128], F32, st)
                    Od = Dep()
                    LRP = p.sb("LRP", [128, 9, 2, GB], F32, st)
                    LIP = p.sb("LIP", [128, 9, 2, GB], F32, st)
                    Ld = Dep()
                    with p.phase() as sp_:
                        def t2(name):
                            return p.sb(name, [128, GB], F32, sp_), Dep()

                        def dv(fn, reads, writes, eng="dve"):
                            p.op(eng, fn, reads=reads, writes=writes)
                        lre, lred = t2("lre")
                        lim, limd = t2("lim")
                        stp, stpd = t2("stp")
                        for d in range(2):
                            hs = slice(64 * d, 64 * d + 64)
                            p.dma(lre[hs, :], s5_lam_re[j, d, g0:g0 + GB, :].rearrange("g q -> q g"), writes=[lred], allow_slow_non_contiguous=True)
                            p.dma(lim[hs, :], s5_lam_im[j, d, g0:g0 + GB, :].rearrange("g q -> q g"), writes=[limd], allow_slow_non_contiguous=True)
                            p.dma(stp[hs, :], s5_log_step[j, d, g0:g0 + GB].partition_broadcast(64), writes=[stpd])
                        dv(lambda e: e.activation(out=stp[:], in_=stp[:], func=AF.Exp), [stpd], [stpd], "act")
                        dv(lambda e: e.tensor_scalar(out=lre[:], in0=lre[:], scalar1=-1e-4, scalar2=None, op0=ALU.min), [lred], [lred])
                        er, erd = t2("er")
                        ang, angd = t2("ang")
                        dv(lambda e: e.tensor_tensor(out=er[:], in0=lre[:], in1=stp[:], op=ALU.mult), [lred, stpd], [erd])
                        dv(lambda e: e.activation(out=er[:], in_=er[:], func=AF.Exp), [erd], [erd], "act")
                        dv(lambda e: e.tensor_tensor(out=ang[:], in0=lim[:], in1=stp[:], op=ALU.mult), [limd, stpd], [angd])
                        cs = []
                        for shift in (math.pi / 2.0, 0.0):
                            a_, ad_ = t2("a")
                            ki = p.sb("ki", [128, GB], I32, sp_)
                            kid = Dep()
                            kf, kfd = t2("kf")
                            m_, md_ = t2("m")
                            dv(lambda e: e.tensor_scalar(out=a_[:], in0=ang[:], scalar1=shift, scalar2=None, op0=ALU.add), [angd], [ad_])
                            dv(lambda e: e.tensor_scalar(out=ki[:], in0=a_[:], scalar1=1.0 / TWO_PI, scalar2=None, op0=ALU.mult), [ad_], [kid])
                            dv(lambda e: e.tensor_copy(kf[:], ki[:]), [kid], [kfd])
                            dv(lambda e: e.scalar_tensor_tensor(out=a_[:], in0=kf[:], scalar=-TWO_PI, in1=a_[:], op0=ALU.mult, op1=ALU.add), [kfd, ad_], [ad_])
                            dv(lambda e: e.tensor_scalar(out=m_[:], in0=a_[:], scalar1=math.pi, scalar2=None, op0=ALU.is_gt), [ad_], [md_])
                            dv(lambda e: e.scalar_tensor_tensor(out=a_[:], in0=m_[:], scalar=-TWO_PI, in1=a_[:], op0=ALU.mult, op1=ALU.add), [md_, ad_], [ad_])
                            dv(lambda e: e.tensor_scalar(out=m_[:], in0=a_[:], scalar1=-math.pi, scalar2=None, op0=ALU.is_lt), [ad_], [md_])
                            dv(lambda e: e.scalar_tensor_tensor(out=a_[:], in0=m_[:], scalar=TWO_PI, in1=a_[:], op0=ALU.mult, op1=ALU.add), [md_, ad_], [ad_])
                            dv(lambda e: e.activation(out=a_[:], in_=a_[:], func=AF.Sin), [ad_], [ad_], "act")
                            cs.append((a_, ad_))
                        (cosv, cosd), (sinv, sind) = cs
                        PW = p.sb("PW", [128, 9, 2, GB], F32, sp_)
                        MW = p.sb("MW", [128, 8, 2, GB], F32, sp_)
                        PWd = Dep()
                        dv(lambda e: e.memset(PW[:, 0, 0, :], 1.0), [], [PWd])
                        dv(lambda e: e.memset(PW[:, 0, 1, :], 0.0), [], [PWd])
                        dv(lambda e: e.memset(MW[:, 0, 0, :], 1.0), [], [PWd])
                        dv(lambda e: e.memset(MW[:, 0, 1, :], 0.0), [], [PWd])
                        dv(lambda e: e.tensor_tensor(out=PW[:, 1, 0, :], in0=er[:], in1=cosv[:], op=ALU.mult), [erd, cosd], [PWd])
                        dv(lambda e: e.tensor_tensor(out=PW[:, 1, 1, :], in0=er[:], in1=sinv[:], op=ALU.mult), [erd, sind], [PWd])
                        ar, ai = PW[:, 1, 0, :], PW[:, 1, 1, :]
                        q1, q1d = t2("q1")
                        q2, q2d = t2("q2")
                        dv(lambda e: e.tensor_tensor(out=q1[:], in0=ar, in1=ar, op=ALU.mult), [PWd], [q1d])
                        dv(lambda e: e.tensor_tensor(out=q2[:], in0=ai, in1=ai, op=ALU.mult), [PWd], [q2d])
                        dv(lambda e: e.tensor_tensor(out=q1[:], in0=q1[:], in1=q2[:], op=ALU.add), [q1d, q2d], [q1d])
                        dv(lambda e: e.reciprocal(out=q1[:], in_=q1[:]), [q1d], [q1d])
                        dv(lambda e: e.tensor_tensor(out=MW[:, 1, 0, :], in0=ar, in1=q1[:], op=ALU.mult), [PWd, q1d], [PWd])
                        dv(lambda e: e.scalar_tensor_tensor(out=MW[:, 1, 1, :], in0=ai, scalar=-1.0, in1=q1[:], op0=ALU.mult, op1=ALU.mult), [PWd, q1d], [PWd])

                        def cmul_s(dst, k, src, b):
                            xr_, xi_ = src[:, k - 1, 0, :], src[:, k - 1, 1, :]
                            br_, bi_ = b
                            dv(lambda e: e.tensor_tensor(out=q1[:], in0=xr_, in1=br_, op=ALU.mult), [PWd, q1d], [q1d])
                            dv(lambda e: e.tensor_tensor(out=q2[:], in0=xi_, in1=bi_, op=ALU.mult), [PWd, q2d], [q2d])
                            dv(lambda e: e.tensor_tensor(out=dst[:, k, 0, :], in0=q1[:], in1=q2[:], op=ALU.subtract), [q1d, q2d], [PWd])
                            dv(lambda e: e.tensor_tensor(out=q1[:], in0=xr_, in1=bi_, op=ALU.mult), [PWd, q1d], [q1d])
                            dv(lambda e: e.tensor_tensor(out=q2[:], in0=xi_, in1=br_, op=ALU.mult), [PWd, q2d], [q2d])
                            dv(lambda e: e.tensor_tensor(out=dst[:, k, 1, :], in0=q1[:], in1=q2[:], op=ALU.add), [q1d, q2d], [PWd])
                        for k in range(2, 9):
                            cmul_s(PW, k, PW, (ar, ai))
                        for k in range(2, 8):
                            cmul_s(MW, k, MW, (MW[:, 1, 0, :], MW[:, 1, 1, :]))
                        LW = p.sb("LW", [128, 9, 2, GB], F32, sp_)
                        dv(lambda e: e.memset(LW[:, 0, 0, :], 1.0), [PWd], [PWd])
                        dv(lambda e: e.memset(LW[:, 0, 1, :], 0.0), [PWd], [PWd])
                        dv(lambda e: e.tensor_copy(LW[:, 1, :, :], PW[:, 8, :, :]), [PWd], [PWd])
                        for k in range(2, 9):
                            cmul_s(LW, k, LW, (PW[:, 8, 0, :], PW[:, 8, 1, :]))
                        dv(lambda e: e.tensor_copy(LRP[:, :, 0, :], LW[:, :, 0, :]), [PWd], [Ld])
                        dv(lambda e: e.tensor_copy(LRP[:, :, 1, :], LW[:, :, 0, :]), [PWd], [Ld])
                        dv(lambda e: e.tensor_scalar(out=LIP[:, :, 0, :], in0=LW[:, :, 1, :], scalar1=-1.0, scalar2=None, op0=ALU.mult), [PWd], [Ld])
                        dv(lambda e: e.tensor_copy(LIP[:, :, 1, :], LW[:, :, 1, :]), [PWd], [Ld])
                        MWs = p.sb("MWs", [128, 8, 2, GB], F32, sp_)
                        PWy = p.sb("PWy", [128, 9, 2, GB], F32, sp_)
                        PWb = p.sb("PWb", [128, 8, 2, GB], F32, sp_)
                        dv(lambda e: e.tensor_copy(MWs[0:64], MW[0:64]), [PWd], [PWd])
                        dv(lambda e: e.tensor_copy(MWs[64:128], MW[64:128][:, ::-1, :, :]), [PWd], [PWd])
                        dv(lambda e: e.tensor_copy(PWy[0:64], PW[0:64]), [PWd], [PWd])
                        dv(lambda e: e.tensor_copy(PWy[64:128], PW[64:128][:, ::-1, :, :]), [PWd], [PWd])
                        dv(lambda e: e.tensor_copy(PWb[0:64], PW[0:64, 0:8][:, ::-1, :, :]), [PWd], [PWd])
                        dv(lambda e: e.tensor_copy(PWb[64:128], PW[64:128, 0:8]), [PWd], [PWd])
                        den, dend = t2("den")
                        am1, am1d = t2("am1")
                        cr, crd = t2("cr")
                        ci, cid = t2("ci")
                        dv(lambda e: e.tensor_tensor(out=den[:], in0=lre[:], in1=lre[:], op=ALU.mult), [lred], [dend])
                        dv(lambda e: e.tensor_tensor(out=q1[:], in0=lim[:], in1=lim[:], op=ALU.mult), [limd, q1d], [q1d])
                        dv(lambda e: e.tensor_tensor(out=den[:], in0=den[:], in1=q1[:], op=ALU.add), [dend, q1d], [dend])
                        dv(lambda e: e.reciprocal(out=den[:], in_=den[:]), [dend], [dend])
                        dv(lambda e: e.tensor_scalar(out=am1[:], in0=ar, scalar1=-1.0, scalar2=None, op0=ALU.add), [PWd], [am1d])
                        dv(lambda e: e.tensor_tensor(out=q1[:], in0=am1[:], in1=lre[:], op=ALU.mult), [am1d, lred, q1d], [q1d])
                        dv(lambda e: e.tensor_tensor(out=q2[:], in0=ai, in1=lim[:], op=ALU.mult), [PWd, limd, q2d], [q2d])
                        dv(lambda e: e.tensor_tensor(out=q1[:], in0=q1[:], in1=q2[:], op=ALU.add), [q1d, q2d], [q1d])
                        dv(lambda e: e.tensor_tensor(out=cr[:], in0=q1[:], in1=den[:], op=ALU.mult), [q1d, dend], [crd])
                        dv(lambda e: e.tensor_tensor(out=q1[:], in0=ai, in1=lre[:], op=ALU.mult), [PWd, lred, q1d], [q1d])
                        dv(lambda e: e.tensor_tensor(out=q2[:], in0=am1[:], in1=lim[:], op=ALU.mult), [am1d, limd, q2d], [q2d])
                        dv(lambda e: e.tensor_tensor(out=q1[:], in0=q1[:], in1=q2[:], op=ALU.subtract), [q1d, q2d], [q1d])
                        dv(lambda e: e.tensor_tensor(out=ci[:], in0=q1[:], in1=den[:], op=ALU.mult), [q1d, dend], [cid])

                        def t3(name):
                            return p.sb(name, [128, GB, 16], F32, sp_), Dep()
                        Br, Brd = t3("Br")
                        Bi, Bid = t3("Bi")
                        Cr, Crd = t3("Cr")
                        Ci, Cid = t3("Ci")
                        for d in range(2):
                            hs = slice(64 * d, 64 * d + 64)
                            p.dma(Br[hs], s5_b_re[j, d, g0:g0 + GB].rearrange("g q h -> q g h"), writes=[Brd])
                            p.dma(Bi[hs], s5_b_im[j, d, g0:g0 + GB].rearrange("g q h -> q g h"), writes=[Bid])
                        ctr = Rot(sp_, "ct", [128, 128], F32, 2)
                        for (Cdst, Cdd, csrc) in ((Cr, Crd, s5_c_re), (Ci, Cid, s5_c_im)):
                            for g8 in range(GB // 8):
                                ct, ctd = ctr.get()
                                for d in range(2):
                                    p.dma(ct[:, 64 * d:64 * d + 64], csrc[j, d, g0 + 8 * g8:g0 + 8 * g8 + 8].rearrange("g h q -> (g h) q"), writes=[ctd])
                                bk, bkd = bankA()
                                p.op("pe", lambda e: e.transpose(bk[:, 0:128], ct[:, :], identf[:, :]), reads=[ctd, cD], writes=[bkd])
                                p.op("act", lambda e: e.activation(out=Cdst[:, 8 * g8:8 * g8 + 8, :], in_=bk[:, 0:128].rearrange("p (g h) -> p g h", h=16),
                                                                   func=AF.Copy), reads=[bkd], writes=[Cdd])
                        T1 = p.sb("T1", [128, GB, 9, 16], F32, sp_)
                        T1d = Dep()

                        def cmul(dR, dI, dd, Pr, Pi, Pd, Xr_, Xi_, Xd, T1v, negI=False):
                            p.op("dve", lambda e: e.tensor_tensor(out=dR, in0=Xr_, in1=Pr, op=ALU.mult), reads=Pd + Xd, writes=dd)
                            p.op("pool", lambda e: e.tensor_tensor(out=T1v, in0=Xi_, in1=Pi, op=ALU.mult), reads=Pd + Xd, writes=[T1d])
                            p.op("dve", lambda e: e.tensor_tensor(out=dI, in0=Xi_, in1=Pr, op=ALU.mult), reads=Pd + Xd, writes=dd)
                            p.op("dve", lambda e: e.tensor_tensor(out=dR, in0=dR, in1=T1v, op=ALU.subtract), reads=dd + [T1d], writes=dd)
                            p.op("pool", lambda e: e.tensor_tensor(out=T1v, in0=Xr_, in1=Pi, op=ALU.mult), reads=Pd + Xd, writes=[T1d])
                            if negI:
                                p.op("dve", lambda e: e.scalar_tensor_tensor(out=dI, in0=dI, scalar=-1.0, in1=T1v, op0=ALU.mult, op1=ALU.subtract),
                                     reads=dd + [T1d], writes=dd)
                            else:
                                p.op("dve", lambda e: e.tensor_tensor(out=dI, in0=dI, in1=T1v, op=ALU.add), reads=dd + [T1d], writes=dd)

                        def slots(tab, S, ri):
                            return tab[:, :, ri, :].rearrange("p s g -> p g s").unsqueeze(3).to_broadcast([128, GB, S, 16])

                        def overs(x3, S):
                            return x3.unsqueeze(2).to_broadcast([128, GB, S, 16])
                        BbR, BbRd = t3("BbR")
                        BbI, BbId = t3("BbI")
                        cmul(BbR[:], BbI[:], [BbRd, BbId], cr[:].unsqueeze(2).to_broadcast([128, GB, 16]), ci[:].unsqueeze(2).to_broadcast([128, GB, 16]),
                             [crd, cid], Br[:], Bi[:], [Brd, Bid], T1[:, :, 0, :])
                        Bbd = [BbRd, BbId]
                        with p.phase() as sq:
                            XR = p.sb("XR", [128, GB, 8, 16], F32, sq)
                            XI = p.sb("XI", [128, GB, 8, 16], F32, sq)
                            Xd_ = Dep()
                            YR = p.sb("YR", [128, GB, 9, 16], F32, sq)
                            YI = p.sb("YI", [128, GB, 9, 16], F32, sq)
                            Yd_ = Dep()
                            cmul(XR[:], XI[:], [Xd_], slots(MWs, 8, 0), slots(MWs, 8, 1), [PWd], overs(BbR[:], 8), overs(BbI[:], 8), Bbd, T1[:, :, 0:8, :])
                            cmul(YR[:], YI[:], [Yd_], slots(PWy, 9, 0), slots(PWy, 9, 1), [PWd], overs(Cr[:], 9), overs(Ci[:], 9), [Crd, Cid], T1[:], negI=True)
                            tmr = Rot(sq, "tm", [128, 128], F32, 3)
                            for d in range(2):
                                hs = slice(64 * d, 64 * d + 64)
                                y0 = 0 if d == 0 else 1
                                for g in range(GB):
                                    bk, bkd = bankA()
                                    p.mm([lambda e: e.matmul(bk[:, 0:128], lhsT=XR[hs, g, :, :].rearrange("q j h -> q (j h)"),
                                                             rhs=YR[hs, g, y0:y0 + 8, :].rearrange("q j h -> q (j h)"), start=True, stop=False),
                                          lambda e: e.matmul(bk[:, 0:128], lhsT=XI[hs, g, :, :].rearrange("q j h -> q (j h)"),
                                                             rhs=YI[hs, g, y0:y0 + 8, :].rearrange("q j h -> q (j h)"), start=False, stop=True)],
                                         reads=[Xd_, Yd_], writes=[bkd])
                                    if d == 0:
                                        tm, tmd = tmr.get()
                                        p.op("dve", lambda e: e.tensor_tensor(out=tm[:], in0=bk[:, 0:128], in1=mask[:, d, :], op=ALU.mult),
                                             reads=[bkd, mkd], writes=[tmd])
                                        p.op("dve", lambda e: e.scalar_tensor_tensor(out=T0b[d][:, g, :], in0=identf[:, :], scalar=dcol[:, g0 + g:g0 + g + 1],
                                                                                     in1=tm[:], op0=ALU.mult, op1=ALU.add),
                                             reads=[tmd, dcd, cD], writes=[T0d[d][g]])
                                    else:
                                        p.op("dve", lambda e: e.tensor_tensor(out=T0b[d][:, g, :], in0=bk[:, 0:128], in1=mask[:, d, :], op=ALU.mult),
                                             reads=[bkd, mkd], writes=[T0d[d][g]])
                            for (hs, o0) in ((slice(0, 64), 1), (slice(64, 128), 0)):
                                p.op("act", lambda e: e.activation(out=OR[hs].rearrange("q g (j h) -> q g j h", h=16), in_=YR[hs, :, o0:o0 + 8, :], func=AF.Copy),
                                     reads=[Yd_], writes=[Od])
                                p.op("act", lambda e: e.activation(out=OI[hs].rearrange("q g (j h) -> q g j h", h=16), in_=YI[hs, :, o0:o0 + 8, :], func=AF.Copy),
                                     reads=[Yd_], writes=[Od])
                        with p.phase() as sq:
                            BR = p.sb("BR", [128, GB, 8, 16], F32, sq)
                            BI = p.sb("BI", [128, GB, 8, 16], F32, sq)
                            Bd_ = Dep()
                            cmul(BR[:], BI[:], [Bd_], slots(PWb, 8, 0), slots(PWb, 8, 1), [PWd], overs(BbR[:], 8), overs(BbI[:], 8), Bbd, T1[:, :, 0:8, :])
                            for d in range(2):
                                hs = slice(64 * d, 64 * d + 64)
                                for g in range(GB):
                                    bk, bkd = bankA()
                                    p.mm([lambda e: e.transpose(bk[:, 0:64], BR[hs, g, :, :].rearrange("q j h -> q (j h)"), identf[hs, hs]),
                                          lambda e: e.transpose(bk[:, 64:128], BI[hs, g, :, :].rearrange("q j h -> q (j h)"), identf[hs, hs])],
                                         reads=[Bd_, cD], writes=[bkd])
                                    p.op("act", lambda e: e.activation(out=Bcb[d][:, g, :], in_=bk[:, 0:128], func=AF.Copy), reads=[bkd], writes=[Bcd[d][g]])
                    XC = p.sb("XC", [128, 2, GB, 288], F32, st)
                    with p.phase() as sq:
                        xcd = Dep()
                        for g in range(GB):
                            for ri in range(2):
                                bk, bkd = bankA()
                                p.mm([lambda e: e.matmul(bk[0:64, 0:288], lhsT=Bcb[0][:, g, ri * 64:(ri + 1) * 64], rhs=Vb[:, g, :], start=True, stop=True),
                                      lambda e: e.matmul(bk[64:128, 0:32], lhsT=Bcb[1][:, g, ri * 64:(ri + 1) * 64], rhs=Vb[:, g, 0:32][:, ::-1],
                                                         start=True, stop=True),
                                      lambda e: e.matmul(bk[64:128, 32:288], lhsT=Bcb[1][:, g, ri * 64:(ri + 1) * 64], rhs=Vb[:, g, 32:288][:, ::-1],
                                                         start=True, stop=True)],
                                     reads=[Bcd[0][g], Bcd[1][g], Vbd[g]], writes=[bkd])
                                if ri == 0:
                                    p.op("act", lambda e: e.activation(out=XC[:, ri, g, :], in_=bk[:, 0:288], func=AF.Copy), reads=[bkd], writes=[xcd])
                                else:
                                    p.op("dve", lambda e: e.tensor_copy(XC[:, ri, g, :], bk[:, 0:288]), reads=[bkd], writes=[xcd])
                    with p.phase() as sq:
                        shp = [128, 2, GB, NS]
                        Tr = Rot(sq, "T", shp, F32, 2)
                        Bb_ = p.sb("Bq", shp, F32, sq)
                        Bqd = Dep()
                        SIN = p.sb("SIN", shp, F32, sq)
                        SINd = [Dep() for _ in range(NS)]
                        _r3 = {}

                        def Rot_get3(st_):
                            if "r" not in _r3:
                                _r3["r"] = Rot(st_, "m6", [128, 2, GB], F32, 4)
                            return _r3["r"].get()
                        slotD = [Dep() for _ in range(8)]

                        def lr(k):
                            return LRP[:, k, :, :].unsqueeze(3).to_broadcast(shp)

                        def li(k):
                            return LIP[:, k, :, :].unsqueeze(3).to_broadcast(shp)

                        def slot(k):
                            return XC[:, :, :, k::8]
                        tp_, tpd_ = None, None
                        for k in range(1, 9):
                            tk, tkd = Tr.get()
                            if k == 1:
                                p.op("act", lambda e: e.activation(out=tk[:], in_=slot(0), func=AF.Copy), reads=[slotD[0]], writes=[tkd])
                                p.op("dve", lambda e: e.memset(slot(0), 0.0), writes=[slotD[0]])
                            else:
                                p.op("pool", lambda e: e.tensor_tensor(out=tk[:], in0=tp_[:], in1=lr(1), op=ALU.mult), reads=[tpd_, Ld], writes=[tkd])
                                p.op("dve", lambda e: e.tensor_tensor(out=Bb_[:], in0=tp_[:, ::-1, :, :], in1=li(1), op=ALU.mult), reads=[tpd_, Ld], writes=[Bqd])
                                p.op("dve", lambda e: e.tensor_tensor(out=Bb_[:], in0=Bb_[:], in1=slot(k - 1), op=ALU.add), reads=[Bqd, slotD[k - 1]], writes=[Bqd])
                                p.op("dve", lambda e: e.tensor_tensor(out=tk[:], in0=tk[:], in1=Bb_[:], op=ALU.add), reads=[tkd, Bqd], writes=[tkd])
                                p.op("act", lambda e: e.activation(out=slot(k - 1), in_=tp_[:], func=AF.Copy), reads=[tpd_], writes=[slotD[k - 1]])
                            tp_, tpd_ = tk, tkd
                        Z, Zd = tp_, tpd_
                        NB6 = 6
                        sh6 = [128, 2, GB, NB6]
                        L8r = LRP[:, 8, :, :].unsqueeze(3).to_broadcast(sh6)
                        L8i = LIP[:, 8, :, :].unsqueeze(3).to_broadcast(sh6)
                        LLR = p.sb("LLR", [128, 7, 2, GB], F32, sq)
                        LLI = p.sb("LLI", [128, 7, 2, GB], F32, sq)
                        LLd = Dep()
                        llst = contextlib.ExitStack()
                        LL = p.sb("LL", [128, 7, 2, GB], F32, llst)
                        u1 = p.sb("u1", [128, GB], F32, llst)
                        u2 = p.sb("u2", [128, GB], F32, llst)
                        u1d, u2d = Dep(), Dep()
                        p.op("dve", lambda e: e.memset(LL[:, 0, 0, :], 1.0), writes=[LLd])
                        p.op("dve", lambda e: e.memset(LL[:, 0, 1, :], 0.0), writes=[LLd])
                        p.op("dve", lambda e: e.tensor_copy(LL[:, 1, 0, :], LRP[:, 8, 0, :]), reads=[Ld], writes=[LLd])
                        p.op("dve", lambda e: e.tensor_copy(LL[:, 1, 1, :], LIP[:, 8, 1, :]), reads=[Ld], writes=[LLd])
                        for k in range(2, 7):
                            xr_, xi_ = LL[:, k - 1, 0, :], LL[:, k - 1, 1, :]
                            br_, bi_ = LL[:, 1, 0, :], LL[:, 1, 1, :]
                            p.op("dve", lambda e: e.tensor_tensor(out=u1[:], in0=xr_, in1=br_, op=ALU.mult), reads=[LLd, u1d], writes=[u1d])
                            p.op("dve", lambda e: e.tensor_tensor(out=u2[:], in0=xi_, in1=bi_, op=ALU.mult), reads=[LLd, u2d], writes=[u2d])
                            p.op("dve", lambda e: e.tensor_tensor(out=LL[:, k, 0, :], in0=u1[:], in1=u2[:], op=ALU.subtract), reads=[u1d, u2d], writes=[LLd])
                            p.op("dve", lambda e: e.tensor_tensor(out=u1[:], in0=xr_, in1=bi_, op=ALU.mult), reads=[LLd, u1d], writes=[u1d])
                            p.op("dve", lambda e: e.tensor_tensor(out=u2[:], in0=xi_, in1=br_, op=ALU.mult), reads=[LLd, u2d], writes=[u2d])
                            p.op("dve", lambda e: e.tensor_tensor(out=LL[:, k, 1, :], in0=u1[:], in1=u2[:], op=ALU.add), reads=[u1d, u2d], writes=[LLd])
                        p.op("dve", lambda e: e.tensor_copy(LLR[:, :, 0, :], LL[:, :, 0, :]), reads=[LLd], writes=[LLd])
                        p.op("dve", lambda e: e.tensor_copy(LLR[:, :, 1, :], LL[:, :, 0, :]), reads=[LLd], writes=[LLd])
                        p.op("dve", lambda e: e.tensor_scalar(out=LLI[:, :, 0, :], in0=LL[:, :, 1, :], scalar1=-1.0, scalar2=None, op0=ALU.mult), reads=[LLd], writes=[LLd])
                        p.op("dve", lambda e: e.tensor_copy(LLI[:, :, 1, :], LL[:, :, 1, :]), reads=[LLd], writes=[LLd])
                        p.barrier()
                        llst.close()
                        SINall = Dep()

                        def sslot(k):
                            return SIN[:, :, :, k::6]

                        def zslot(k):
                            return Z[:, :, :, k::6]
                        w6 = [(p.sb("w6_%d" % i, sh6, F32, sq), Dep()) for i in range(3)]
                        p.op("dve", lambda e: e.memset(sslot(0), 0.0), writes=[SINall])
                        for k in range(1, NB6 + 1):
                            (ta, tad), (tb, tbd) = w6[0], w6[1]
                            if k == 1:
                                p.op("dve", lambda e: e.tensor_copy(ta[:], zslot(0)), reads=[Zd], writes=[tad])
                            else:
                                p.op("dve", lambda e: e.tensor_tensor(out=ta[:], in0=prev6, in1=L8r, op=ALU.mult), reads=[SINall, Ld, w6[2][1]], writes=[tad])
                                p.op("dve", lambda e: e.tensor_tensor(out=tb[:], in0=prev6[:, ::-1, :, :], in1=L8i, op=ALU.mult), reads=[SINall, Ld, w6[2][1]], writes=[tbd])
                                p.op("dve", lambda e: e.tensor_tensor(out=ta[:], in0=ta[:], in1=zslot(k - 1), op=ALU.add), reads=[tad, Zd], writes=[tad])
                                p.op("dve", lambda e: e.tensor_tensor(out=ta[:], in0=ta[:], in1=tb[:], op=ALU.add), reads=[tad, tbd], writes=[tad])
                            if k < NB6:
                                p.op("dve", lambda e: e.tensor_copy(sslot(k), ta[:]), reads=[tad], writes=[SINall])
                                prev6 = sslot(k)
                            else:
                                p.op("dve", lambda e: e.tensor_copy(w6[2][0][:], ta[:]), reads=[tad], writes=[w6[2][1]])
                        tot, totd = w6[2]
                        Eb = p.sb("Eb", sh6, F32, sq)
                        Ebd = Dep()
                        p.op("dve", lambda e: e.memset(Eb[:, :, :, 0], 0.0), writes=[Ebd])
                        for b_ in range(1, NB6):
                            m1, m1d = Rot_get3(sq)
                            m2, m2d = Rot_get3(sq)
                            p.op("dve", lambda e: e.tensor_tensor(out=m1[:], in0=Eb[:, :, :, b_ - 1], in1=LLR[:, 6, :, :], op=ALU.mult), reads=[Ebd, LLd], writes=[m1d])
                            p.op("dve", lambda e: e.tensor_tensor(out=m2[:], in0=Eb[:, ::-1, :, b_ - 1], in1=LLI[:, 6, :, :], op=ALU.mult), reads=[Ebd, LLd], writes=[m2d])
                            p.op("dve", lambda e: e.tensor_tensor(out=m1[:], in0=m1[:], in1=tot[:, :, :, b_ - 1], op=ALU.add), reads=[m1d, totd], writes=[m1d])
                            p.op("dve", lambda e: e.tensor_tensor(out=Eb[:, :, :, b_], in0=m1[:], in1=m2[:], op=ALU.add), reads=[m1d, m2d, Ebd], writes=[Ebd])
                        for k in range(NB6):
                            (ta, tad), (tb, tbd) = w6[0], w6[1]
                            lr6 = LLR[:, k, :, :].unsqueeze(3).to_broadcast(sh6)
                            li6 = LLI[:, k, :, :].unsqueeze(3).to_broadcast(sh6)
                            p.op("dve", lambda e: e.tensor_tensor(out=ta[:], in0=Eb[:], in1=lr6, op=ALU.mult), reads=[Ebd, LLd], writes=[tad])
                            p.op("dve", lambda e: e.tensor_tensor(out=tb[:], in0=Eb[:, ::-1, :, :], in1=li6, op=ALU.mult), reads=[Ebd, LLd], writes=[tbd])
                            p.op("dve", lambda e: e.tensor_tensor(out=ta[:], in0=ta[:], in1=tb[:], op=ALU.add), reads=[tad, tbd], writes=[tad])
                            p.op("dve", lambda e: e.tensor_tensor(out=sslot(k), in0=sslot(k), in1=ta[:], op=ALU.add), reads=[SINall, tad], writes=[SINall])
                        SINd = [SINall]
                        p.op("act", lambda e: e.activation(out=slot(0), in_=SIN[:], func=AF.Copy), reads=SINd, writes=[slotD[0]])
                        Ab_, Aqd = Tr.get()
                        for k in range(1, 8):
                            p.op("pool", lambda e: e.tensor_tensor(out=Ab_[:], in0=SIN[:], in1=lr(k), op=ALU.mult), reads=SINd + [Ld], writes=[Aqd])
                            p.op("dve", lambda e: e.tensor_tensor(out=Bb_[:], in0=SIN[:, ::-1, :, :], in1=li(k), op=ALU.mult), reads=SINd + [Ld], writes=[Bqd])
                            p.op("dve", lambda e: e.tensor_tensor(out=Bb_[:], in0=Bb_[:], in1=slot(k), op=ALU.add), reads=[Bqd, slotD[k]], writes=[Bqd])
                            p.op("dve", lambda e: e.tensor_tensor(out=slot(k), in0=Bb_[:], in1=Ab_[:], op=ALU.add), reads=[slotD[k], Aqd, Bqd], writes=[slotD[k]])
                    with p.phase() as sq:
                        tr_ = Rot(sq, "ty", [128, 288], F32, 3)
                        ur_ = Rot(sq, "tu", [128, 288], F32, 3)
                        for g in range(GB):
                            bk, bkd = bankA()
                            b2_, b2d = bankA()
                            p.mm([lambda e: e.matmul(bk[:, 0:288], lhsT=T0b[0][:, g, :], rhs=Vb[:, g, :], start=True, stop=False),
                                  lambda e: e.matmul(bk[:, 0:288], lhsT=T0b[1][:, g, :], rhs=Vb[:, g, :], start=False, stop=False),
                                  lambda e: e.matmul(bk[:, 0:288], lhsT=OR[0:64, g, :], rhs=XC[0:64, 0, g, :], start=False, stop=False),
                                  lambda e: e.matmul(bk[:, 0:288], lhsT=OI[0:64, g, :], rhs=XC[0:64, 1, g, :], start=False, stop=True)],
                                 reads=[T0d[0][g], T0d[1][g], Vbd[g], Od], writes=[bkd])
                            p.mm([lambda e: e.matmul(b2_[:, 0:32], lhsT=OR[64:128, g, :], rhs=XC[64:128, 0, g, 0:32][:, ::-1], start=True, stop=False),
                                  lambda e: e.matmul(b2_[:, 0:32], lhsT=OI[64:128, g, :], rhs=XC[64:128, 1, g, 0:32][:, ::-1], start=False, stop=False),
                                  lambda e: e.matmul(b2_[:, 32:288], lhsT=OR[64:128, g, :], rhs=XC[64:128, 0, g, 32:288][:, ::-1], start=True, stop=False),
                                  lambda e: e.matmul(b2_[:, 32:288], lhsT=OI[64:128, g, :], rhs=XC[64:128, 1, g, 32:288][:, ::-1], start=False, stop=True)],
                                 reads=[Od], writes=[b2d])
                            tu, tud = ur_.get()
                            ty, tyd = tr_.get()
                            p.op("act", lambda e: e.activation(out=tu[:], in_=b2_[:, 0:288], func=AF.Copy), reads=[b2d], writes=[tud])
                            p.op("dve", lambda e: e.tensor_tensor(out=ty[:], in0=bk[:, 0:288], in1=tu[:], op=ALU.add), reads=[bkd, tud], writes=[tyd])
                            p.op("act", lambda e: e.activation(out=Vb[:, g, 0:32], in_=ty[:, 0:32], func=AF.Gelu_apprx_tanh), reads=[tyd], writes=[Vbd[g]])
                            p.op("act", lambda e: e.activation(out=Vb[:, g, 32:288].rearrange("p (lt q wl) -> p lt wl q", lt=2, q=4),
                                                               in_=ty[:, 32:288].rearrange("p (lt wl q) -> p lt wl q", lt=2, q=4),
                                                               func=AF.Gelu_apprx_tanh), reads=[tyd], writes=[Vbd[g]])
                    p.dma(YD[:, g0:g0 + GB, :], Vb[:], reads=Vbd, writes=[Dep()])
        YDv = YD.rearrange("(j h) g c -> h g j c", h=16)
        for k in range(16):
            for gl in range(8):
                p.dma(ZTD[16 * gl:16 * gl + 16, k, :].rearrange("h (j c) -> h j c", j=8), YDv[:, 8 * k + gl, :, :], reads=[Dep()], writes=[Dep()])
        p.barrier()
        tiles = []
        for jj in range(8):
            def rowfn_c(cs, jj=jj):
                src = XS[0:256, :].rearrange("(c j) d -> j c d", j=8)
                return [(0, 16, src[jj, 0:16, cs], XD[0]), (16, 16, src[jj, 16:32, cs], XD[1])]
            tiles.append((jj * 288, 32, 1, rowfn_c))
        for lt in range(2):
            for jj in range(8):
                def rowfn_l(cs, jj=jj, lt=lt):
                    res = []
                    for q in range(4):
                        r0 = 256 + (8 * q + jj) * 64 + 32 * lt
                        res.append((32 * q, 32, XS[r0:r0 + 32, cs], XD[2 + (8 * q + jj) // 2]))
                    return res
                tiles.append((jj * 288 + 32 + 128 * lt, 128, 0, rowfn_l))
        out_proj(L, last, s5_w_glu[j], True, tiles)

    for L in range(nlayers):
        last = (L == NL - 1)
        if L % 2 == 0:
            lru_layer(L, last)
        else:
            s5_layer(L, last)
        ffn(L, last)

    with p.phase() as st:
        if debug:
            dxr = Rot(st, "dx", [128, D], F32, 2)
            for t in range(18):
                xt, xd = dxr.get()
                p.dma(xt[:], XS[t * 128:(t + 1) * 128, :], reads=[XD[t]], writes=[xd])
                p.dma(dbg[t * 128:(t + 1) * 128, :], xt[:], reads=[xd])
        fg_ = p.sb("fg", [1, D], F32, st)
        fgd = Dep()
        p.dma(fg_[:], final_g.rearrange("(o n) -> o n", o=1), writes=[fgd])
        Gt = p.sb("Gt", [128, D], F32, st)
        Gd = Dep()
        for k in range(4):
            bk, bkd = bankA()
            p.op("pe", lambda e: e.matmul(bk[:], lhsT=ones1[:], rhs=fg_[0:1, k * 512:(k + 1) * 512], start=True, stop=True), reads=[fgd, cD], writes=[bkd])
            p.op("act", lambda e: e.activation(out=Gt[:, k * 512:(k + 1) * 512], in_=bk[:], func=AF.Copy), reads=[bkd], writes=[Gd])
        nb = nm_bufs(st)
        jr, ssr, rsr = nb
        xr = Rot(st, "xt", [128, D], F32, 4)
        orr = Rot(st, "ot", [128, D], F32, 2)
        for ti in range(2, 18):
            junk, jd = jr.get()
            xt, xd = xr.get()
            p.dma(xt[:], XS[ti * 128:(ti + 1) * 128, :], reads=[XD[ti]], writes=[xd])
            ss, ssd = ssr.get()
            rs, rsd = rsr.get()
            ot, otd = orr.get()
            p.op("act", lambda e: e.activation(out=junk[:], in_=xt[:], func=AF.Square, accum_out=ss[:]), reads=[xd], writes=[jd, ssd])
            p.op("dve", lambda e: e.tensor_scalar(out=rs[:], in0=ss[:], scalar1=1.0 / D, scalar2=EPS, op0=ALU.mult, op1=ALU.add), reads=[ssd], writes=[rsd])
            p.op("act", lambda e: e.activation(out=rs[:], in_=rs[:], func=AF.Sqrt), reads=[rsd], writes=[rsd])
            p.op("dve", lambda e: e.reciprocal(out=rs[:], in_=rs[:]), reads=[rsd], writes=[rsd])
            p.op("dve", lambda e: e.scalar_tensor_tensor(out=ot[:], in0=xt[:], scalar=rs[:, 0:1], in1=Gt[:], op0=ALU.mult, op1=ALU.mult),
                 reads=[xd, rsd, Gd], writes=[otd])
            p.dma(out[(ti - 2) * 128:(ti - 1) * 128, :], ot[:], reads=[otd])
    p.finish()
    return p


def make_inputs(inputs):
    ident = np.eye(128, dtype=np.float32)
    jj = np.arange(128) // 16
    maskf = (jj[None, :] >= jj[:, None]).astype(np.float32)
    maskb = (jj[None, :] <= jj[:, None]).astype(np.float32)
    masks = np.stack([maskf, maskb]).astype(np.float32)
    shared = {k: np.ascontiguousarray(np.asarray(v, dtype=np.float32)) for k, v in inputs.items() if k not in ("x", "c", "ctx", "c_ctx")}
    maps = []
    for b in range(4):
        m = dict(shared)
        m["xin"] = np.ascontiguousarray(np.concatenate([inputs["ctx"][b], inputs["x"][b]], axis=0).astype(np.float32))
        m["cvec"] = np.ascontiguousarray(np.stack([inputs["c"][b], inputs["c_ctx"]]).astype(np.float32))
        m["identin"] = ident
        m["maskin"] = masks
        maps.append(m)
    return maps


def kernel(**inputs):
    inputs = {k: np.asarray(v) for k, v in inputs.items()}
    p = build()
    maps = make_inputs(inputs)
    res = run_bass_kernel_spmd(p.nc, maps, core_ids=list(range(4)))
    return np.stack([np.asarray(r["out"], dtype=np.float32) for r in res.results], axis=0)
```

```python
import contextlib
import math
import numpy as np
import concourse.bass as bass
import concourse.mybir as mybir
from concourse.bass_utils import run_bass_kernel_spmd

F32 = mybir.dt.float32
BF16 = mybir.dt.bfloat16
I32 = mybir.dt.int32
AF = mybir.ActivationFunctionType
ALU = mybir.AluOpType

D = 2048
T = 2304
NCTX = 256
DFF = 5632
NL = 4
EPS = 1e-6


class Dep:
    __slots__ = ("w", "r")

    def __init__(self):
        self.w = None
        self.r = []


class P:
    NDMA = 40

    def __init__(self):
        nc = self.nc = bass.Bass("TRN2", target_bir_lowering=False)
        self.es = contextlib.ExitStack()
        self.eng = {"pe": nc.tensor, "act": nc.scalar, "dve": nc.vector, "pool": nc.gpsimd, "sp": nc.sync}
        self.sem = {e: self.es.enter_context(nc.semaphore("s_" + e)) for e in self.eng}
        self.cnt = {e: 0 for e in self.eng}
        self.known = {e: {} for e in self.eng}
        self.dsem = [self.es.enter_context(nc.semaphore("d%d" % i)) for i in range(self.NDMA)]
        self.duse = [0] * self.NDMA
        self.di = 0
        self.ninst = 0
        self.uid = 0

    def sb(self, name, shape, dt, st=None):
        self.uid += 1
        return (st or self.es).enter_context(self.nc.sbuf_tensor("%s_%d" % (name, self.uid), list(shape), dt))

    def ps(self, name, shape, dt):
        return self.es.enter_context(self.nc.psum_tensor(name, list(shape), dt))

    def _wait(self, e, ev):
        if ev is None:
            return
        sem, val = ev
        k = self.known[e]
        if k.get(sem.num, 0) >= val:
            return
        self.eng[e].wait_ge(sem, val)
        k[sem.num] = val

    def _deps(self, e, reads, writes):
        for d in reads:
            self._wait(e, d.w)
        for d in writes:
            self._wait(e, d.w)
            for ev in d.r:
                self._wait(e, ev)

    def _commit(self, ev, reads, writes):
        for d in reads:
            d.r.append(ev)
            if len(d.r) > 16:
                best = {}
                for s, v in d.r:
                    if best.get(s.num, (None, -1))[1] < v:
                        best[s.num] = (s, v)
                d.r = list(best.values())
        for d in writes:
            d.w = ev
            d.r = []

    def op(self, e, make, reads=(), writes=()):
        self._deps(e, reads, writes)
        inst = make(self.eng[e])
        self.cnt[e] += 1
        ev = (self.sem[e], self.cnt[e])
        inst.then_inc(self.sem[e], 1)
        self._commit(ev, reads, writes)
        self.ninst += 1
        return ev

    def mm(self, steps, reads=(), writes=()):
        self._deps("pe", reads, writes)
        inst = None
        for mk in steps:
            inst = mk(self.eng["pe"])
        self.cnt["pe"] += 1
        ev = (self.sem["pe"], self.cnt["pe"])
        inst.then_inc(self.sem["pe"], 1)
        self._commit(ev, reads, writes)
        self.ninst += len(steps)
        return ev

    def dma(self, out, in_, reads=(), writes=(), q="sp", **kw):
        i = self.di
        self.di = (self.di + 1) % self.NDMA
        if self.duse[i] > 0:
            self._wait(q, (self.dsem[i], 16 * self.duse[i]))
        self._deps(q, reads, writes)
        inst = self.eng[q].dma_start(out=out, in_=in_, **kw)
        self.duse[i] += 1
        ev = (self.dsem[i], 16 * self.duse[i])
        inst.then_inc(self.dsem[i], 16)
        self._commit(ev, reads, writes)
        self.ninst += 1
        return ev

    def barrier(self):
        evs = [(self.sem[e], self.cnt[e]) for e in self.eng if self.cnt[e] > 0]
        evs += [(self.dsem[i], 16 * self.duse[i]) for i in range(self.NDMA) if self.duse[i] > 0]
        for e in self.eng:
            for ev in evs:
                self._wait(e, ev)

    @contextlib.contextmanager
    def phase(self):
        st = contextlib.ExitStack()
        try:
            yield st
        finally:
            self.barrier()
            st.close()

    def finish(self):
        self.barrier()
        self.es.close()


def build(debug=None, nlayers=NL):
    p = P()
    nc = p.nc

    def din(name, shape):
        return nc.dram_tensor(name, list(shape), F32, kind="ExternalInput").ap()

    xin = din("xin", [T, D])
    cvec = din("cvec", [2, D])
    ada_w = din("ada_w", [NL, D, 6 * D])
    ada_b = din("ada_b", [NL, 6 * D])
    norm1_g = din("norm1_g", [NL, D])
    norm2_g = din("norm2_g", [NL, D])
    final_g = din("final_g", [D])
    ffn_w13 = din("ffn_w13", [NL, D, 2 * DFF])
    ffn_w2 = din("ffn_w2", [NL, DFF, D])
    lru_w_in = din("lru_w_in", [2, D, 2 * D])
    lru_conv_w = din("lru_conv_w", [2, 4, D])
    lru_conv_b = din("lru_conv_b", [2, D])
    lru_wa = din("lru_wa", [2, 2, 8, 256, 256])
    lru_ba = din("lru_ba", [2, 2, D])
    lru_wx = din("lru_wx", [2, 2, 8, 256, 256])
    lru_bx = din("lru_bx", [2, 2, D])
    lru_lam = din("lru_lam", [2, 2, D])
    lru_w_out = din("lru_w_out", [2, D, D])
    s5_lam_re = din("s5_lam_re", [2, 2, 128, 64])
    s5_lam_im = din("s5_lam_im", [2, 2, 128, 64])
    s5_log_step = din("s5_log_step", [2, 2, 128])
    s5_b_re = din("s5_b_re", [2, 2, 128, 64, 16])
    s5_b_im = din("s5_b_im", [2, 2, 128, 64, 16])
    s5_c_re = din("s5_c_re", [2, 2, 128, 16, 64])
    s5_c_im = din("s5_c_im", [2, 2, 128, 16, 64])
    s5_d = din("s5_d", [2, D])
    s5_w_glu = din("s5_w_glu", [2, D, 2 * D])
    identin = din("identin", [128, 128])
    maskin = din("maskin", [2, 128, 128])
    out = nc.dram_tensor("out", [T - NCTX, D], F32, kind="ExternalOutput").ap()
    dbg = None
    if debug:
        dbg = nc.dram_tensor("dbg", [T, D], F32, kind="ExternalOutput").ap()
        dbgmod = nc.dram_tensor("dbgmod", [NL, 2, 6 * D], F32, kind="ExternalOutput").ap()

    XS = nc.dram_tensor("XS", [T, D], F32, kind="Internal").ap()
    MOD = nc.dram_tensor("MOD", [NL, 2, 6 * D], F32, kind="Internal").ap()
    ZTD = nc.dram_tensor("ZTD", [128, 16, T], BF16, kind="Internal").ap()
    VD = nc.dram_tensor("VD", [128, 128, 288], BF16, kind="Internal").ap()
    YD = nc.dram_tensor("YD", [128, 128, 288], BF16, kind="Internal").ap()
    class _Fresh:
        def __getitem__(self, i):
            return Dep()
    XD = _Fresh()
    MODD = Dep()
    ZTDD = Dep()
    VDD = Dep()
    YDD = Dep()

    ident = p.sb("ident", [128, 128], BF16)
    identf = p.sb("identf", [128, 128], F32)
    ones1 = p.sb("ones1", [1, 128], F32)
    cD = Dep()
    p.dma(ident[:], identin[:, :], writes=[cD], q="pool")
    p.dma(identf[:], identin[:, :], writes=[cD])
    p.op("dve", lambda e: e.memset(ones1[:], 1.0), writes=[cD])
    psA = [(p.ps("psA%d" % i, [128, 512], F32), Dep()) for i in range(6)]
    psT = [(p.ps("psT%d" % i, [128, 8, 128], BF16), Dep()) for i in range(2)]
    rot = {"A": 0, "T": 0}

    def bankA():
        rot["A"] = (rot["A"] + 1) % 6
        return psA[rot["A"]]

    def bankT():
        rot["T"] = (rot["T"] + 1) % 2
        return psT[rot["T"]]

    for t in range(18):
        p.dma(XS[t * 128:(t + 1) * 128, :], xin[t * 128:(t + 1) * 128, :], writes=[XD[t]])

    class Rot:
        def __init__(self, st, name, shape, dt, n):
            self.b = [(p.sb(name, shape, dt, st), Dep()) for _ in range(n)]
            self.i = 0

        def get(self):
            self.i = (self.i + 1) % len(self.b)
            return self.b[self.i]

    def bcast_rows(st, specs):
        res = []
        dsts = [(p.sb("bc", [128, D], F32, st), Dep()) for _ in specs]
        tmp_st = contextlib.ExitStack()
        r0 = Rot(tmp_st, "r0", [1, D], F32, 1)
        r1 = Rot(tmp_st, "r1", [1, D], F32, 1)
        for si, sp in enumerate(specs):
            dst, dd = dsts[si]
            a, ad = r0.get()
            p.dma(a[:], sp[1], reads=[MODD], writes=[ad])
            if sp[0] == "g1p":
                b, bd = r1.get()
                p.dma(b[:], sp[2], reads=[MODD], writes=[bd])
                p.op("dve", lambda e: e.scalar_tensor_tensor(out=a[:], in0=b[:], scalar=1.0, in1=a[:], op0=ALU.add, op1=ALU.mult),
                     reads=[ad, bd], writes=[ad])
            for k in range(4):
                bk, bkd = bankA()
                p.op("pe", lambda e: e.matmul(bk[:], lhsT=ones1[:], rhs=a[0:1, k * 512:(k + 1) * 512], start=True, stop=True),
                     reads=[ad, cD], writes=[bkd])
                p.op("act", lambda e: e.activation(out=dst[:, k * 512:(k + 1) * 512], in_=bk[:], func=AF.Copy), reads=[bkd], writes=[dd])
            res.append((dst, dd))
        p.barrier()
        tmp_st.close()
        return res

    def modrow(L, v, idx):
        return MOD[L, v:v + 1, idx * D:(idx + 1) * D]

    def norm_mod(st_bufs, xt, xd, npart, Mt, Md, SHt, SHd, outap, outd, v3=False):
        jr, ssr, rsr = st_bufs
        junk, jd = jr.get()
        ss, ssd = ssr.get()
        rs, rsd = rsr.get()
        p.op("act", lambda e: e.activation(out=junk[:npart], in_=xt, func=AF.Square, accum_out=ss[:npart]), reads=[xd], writes=[jd, ssd])
        p.op("dve", lambda e: e.tensor_scalar(out=rs[:npart], in0=ss[:npart], scalar1=1.0 / D, scalar2=EPS, op0=ALU.mult, op1=ALU.add),
             reads=[ssd], writes=[rsd])
        p.op("act", lambda e: e.activation(out=rs[:npart], in_=rs[:npart], func=AF.Sqrt), reads=[rsd], writes=[rsd])
        p.op("dve", lambda e: e.reciprocal(out=rs[:npart], in_=rs[:npart]), reads=[rsd], writes=[rsd])
        p.op("dve", lambda e: e.scalar_tensor_tensor(out=junk[:npart], in0=xt, scalar=rs[:npart, 0:1], in1=Mt[:npart],
                                                     op0=ALU.mult, op1=ALU.mult), reads=[xd, rsd, Md], writes=[jd])
        i0, i1 = junk[:npart], SHt[:npart]
        if v3:
            i0 = i0.rearrange("p (g h) -> p g h", h=16)
            i1 = i1.rearrange("p (g h) -> p g h", h=16)
        p.op("dve", lambda e: e.tensor_tensor(out=outap, in0=i0, in1=i1, op=ALU.add), reads=[jd, SHd], writes=[outd])

    def nm_bufs(st, nj=2):
        return (Rot(st, "junk", [128, D], F32, nj), Rot(st, "ss", [128, 1], F32, 3), Rot(st, "rs", [128, 1], F32, 3))

    def to_fm(hb, hbd, ntok, HT, HTd, tok0):
        for kb in range(2):
            bk, bkd = bankT()
            for kk in range(8):
                k = kb * 8 + kk
                p.op("pe", lambda e: e.transpose(bk[:, kk, 0:ntok], hb[0:ntok, k * 128:(k + 1) * 128], ident[0:ntok, 0:ntok]),
                     reads=[hbd, cD], writes=[bkd])
            eng = "act" if kb == 0 else "dve"
            if eng == "act":
                p.op("act", lambda e: e.activation(out=HT[:, kb * 8:(kb + 1) * 8, tok0:tok0 + ntok], in_=bk[:, :, 0:ntok], func=AF.Copy),
                     reads=[bkd], writes=[HTd])
            else:
                p.op("dve", lambda e: e.tensor_copy(HT[:, kb * 8:(kb + 1) * 8, tok0:tok0 + ntok], bk[:, :, 0:ntok]), reads=[bkd], writes=[HTd])

    sT = p.sb("sT", [128, 32], BF16)
    sTd = Dep()
    with p.phase() as st:
        c32 = p.sb("c32", [32, 128], F32, st)
        c32d = Dep()
        p.dma(c32[:], cvec.rearrange("v (k q) -> (v k) q", q=128), writes=[c32d])
        bk, bkd = bankA()
        p.op("pe", lambda e: e.transpose(bk[:, 0:32], c32[:, :], identf[0:32, 0:32]), reads=[c32d, cD], writes=[bkd])
        p.op("act", lambda e: e.activation(out=sT[:], in_=bk[:, 0:32], func=AF.Silu), reads=[bkd], writes=[sTd])

    def adaln_tile(L, nt, w3, wd, br, mr):
        cs = slice(nt * 512, (nt + 1) * 512)
        p.dma(w3, ada_w[L, :, cs].rearrange("(k q) n -> q k n", q=128), writes=[wd], q="pool")
        b_, bd_ = br.get()
        p.dma(b_[:], ada_b[L:L + 1, cs].broadcast_to([2, 512]), writes=[bd_])
        bk, bkd = bankA()
        p.mm([(lambda e, k=k: e.matmul(bk[0:2, :], lhsT=sT[:, k:32:16], rhs=w3[:, k, :], start=(k == 0), stop=(k == 15))) for k in range(16)],
             reads=[sTd, wd], writes=[bkd])
        m_, md_ = mr.get()
        p.op("dve", lambda e: e.tensor_tensor(out=m_[:], in0=bk[0:2, :], in1=b_[:], op=ALU.add), reads=[bkd, bd_], writes=[md_])
        p.dma(MOD[L, :, cs], m_[:], reads=[md_], writes=[MODD])
        if debug:
            p.dma(dbgmod[L, :, cs], m_[:], reads=[md_])

    with p.phase() as st:
        wr = Rot(st, "adaw", [128, 16, 512], BF16, 3)
        br = Rot(st, "adab", [2, 512], F32, 3)
        mr = Rot(st, "adam", [2, 512], F32, 3)
        for nt in range(24):
            w, wd = wr.get()
            adaln_tile(0, nt, w[:], wd, br, mr)

    def ffn(L, last):
        tl = list(range(2, 18)) if last else list(range(18))
        half = len(tl) // 2
        ada_pending = list(range(24)) if (L + 1 < nlayers) else []
        for part in (tl[:half], tl[half:]):
            nt_ = len(part) * 128
            subs = []
            o_ = 0
            while o_ < nt_:
                n_ = min(384, nt_ - o_)
                subs.append((o_, n_))
                o_ += n_
            kinds = sorted(set(1 if ti < 2 else 0 for ti in part))
            with p.phase() as stG:
                GT = p.sb("GT", [128, 44, nt_], BF16, stG)
                GTd = [Dep() for _ in range(44)]
                with p.phase() as stH:
                    HT = p.sb("HT", [128, 16, nt_], BF16, stH)
                    HTd = Dep()
                    with p.phase() as st2:
                        bc = {}
                        for v in kinds:
                            bc[v] = bcast_rows(st2, [("g1p", norm2_g[L:L + 1, :], modrow(L, v, 4)), ("row", modrow(L, v, 3))])
                        nb = nm_bufs(st2, 1)
                        xr = Rot(st2, "xt", [128, D], F32, 2)
                        hr = Rot(st2, "hb", [128, D], BF16, 2)
                        for i, ti in enumerate(part):
                            (Mt, Md), (SHt, SHd) = bc[1 if ti < 2 else 0]
                            xt, xd = xr.get()
                            p.dma(xt[:], XS[ti * 128:(ti + 1) * 128, :], reads=[XD[ti]], writes=[xd])
                            hb, hbd = hr.get()
                            norm_mod(nb, xt[:], xd, 128, Mt, Md, SHt, SHd, hb[:], hbd)
                            to_fm(hb, hbd, 128, HT, HTd, i * 128)
                    with p.phase() as st2:
                        wr = Rot(st2, "w13", [128, 16, 2, 256], BF16, 3)
                        sr = Rot(st2, "sa", [128, 384], F32, 3)
                        abr = Rot(st2, "adab", [2, 512], F32, 2)
                        amr = Rot(st2, "adam", [2, 512], F32, 2)
                        for fg in range(22):
                            if ada_pending:
                                aw, awd = wr.get()
                                adaln_tile(L + 1, ada_pending.pop(0), aw[:, :, :, :].rearrange("p k a n -> p k (a n)"), awd, abr, amr)
                            w, wd = wr.get()
                            p.dma(w[:, :, 0, :], ffn_w13[L, :, fg * 256:(fg + 1) * 256].rearrange("(k q) n -> q k n", q=128), writes=[wd], q="pool")
                            p.dma(w[:, :, 1, :], ffn_w13[L, :, DFF + fg * 256:DFF + (fg + 1) * 256].rearrange("(k q) n -> q k n", q=128),
                                  writes=[wd], q="pool")
                            for c2 in range(2):
                                f = fg * 2 + c2
                                for (o_, n_) in subs:
                                    ba_, bad = bankA()
                                    bb_, bbd = bankA()
                                    p.mm([(lambda e, k=k: e.matmul(ba_[:, 0:n_], lhsT=w[:, k, 0, c2 * 128:(c2 + 1) * 128], rhs=HT[:, k, o_:o_ + n_],
                                                                  start=(k == 0), stop=(k == 15))) for k in range(16)], reads=[wd, HTd], writes=[bad])
                                    p.mm([(lambda e, k=k: e.matmul(bb_[:, 0:n_], lhsT=w[:, k, 1, c2 * 128:(c2 + 1) * 128], rhs=HT[:, k, o_:o_ + n_],
                                                                  start=(k == 0), stop=(k == 15))) for k in range(16)], reads=[wd, HTd], writes=[bbd])
                                    sa, sad = sr.get()
                                    p.op("act", lambda e: e.activation(out=sa[:, 0:n_], in_=ba_[:, 0:n_], func=AF.Silu), reads=[bad], writes=[sad])
                                    p.op("dve", lambda e: e.tensor_tensor(out=GT[:, f, o_:o_ + n_], in0=sa[:, 0:n_], in1=bb_[:, 0:n_], op=ALU.mult),
                                         reads=[sad, bbd], writes=[GTd[f]])
                with p.phase() as st2:
                    g2 = {}
                    for v in kinds:
                        (g2[v],) = bcast_rows(st2, [("row", modrow(L, v, 5))])
                    wr = Rot(st2, "w2", [128, 22, 512], BF16, 4 if len(kinds) == 1 else 3)
                    xr = Rot(st2, "xp", [128, 512], F32, 2)
                    tr = Rot(st2, "tp", [128, 512], F32, 2)
                    for nt in range(4):
                        wh = [wr.get(), wr.get()]
                        for kq in range(4):
                            w_, wd_ = wh[kq // 2]
                            p.dma(w_[:, (kq % 2) * 11:(kq % 2 + 1) * 11, :],
                                  ffn_w2[L, kq * 1408:(kq + 1) * 1408, nt * 512:(nt + 1) * 512].rearrange("(k q) n -> q k n", q=128), writes=[wd_], q="pool")
                        for i, ti in enumerate(part):
                            G2t, G2d = g2[1 if ti < 2 else 0]
                            bk, bkd = bankA()
                            p.mm([(lambda e, k=k: e.matmul(bk[:, :], lhsT=GT[:, k, i * 128:(i + 1) * 128], rhs=wh[k // 22][0][:, k % 22, :],
                                                          start=(k == 0), stop=(k == 43))) for k in range(44)], reads=[wh[0][1], wh[1][1]] + GTd, writes=[bkd])
                            xp, xpd = xr.get()
                            p.dma(xp[:], XS[ti * 128:(ti + 1) * 128, nt * 512:(nt + 1) * 512], reads=[XD[ti]], writes=[xpd])
                            tp, tpd = tr.get()
                            p.op("dve", lambda e: e.tensor_tensor(out=tp[:], in0=bk[:, :], in1=G2t[:, nt * 512:(nt + 1) * 512], op=ALU.mult),
                                 reads=[bkd, G2d], writes=[tpd])
                            p.op("dve", lambda e: e.tensor_tensor(out=tp[:], in0=tp[:], in1=xp[:], op=ALU.add), reads=[tpd, xpd], writes=[tpd])
                            p.dma(XS[ti * 128:(ti + 1) * 128, nt * 512:(nt + 1) * 512], tp[:], reads=[tpd], writes=[XD[ti]])

    def out_proj(L, last, wsrc, glu, tiles):
        OW = 256 if glu else 512
        with p.phase() as st:
            ZT = p.sb("ZT", [128, 16, T], BF16, st)
            ZTd = Dep()
            for k in range(16):
                p.dma(ZT[:, k, :], ZTD[:, k, :], reads=[Dep()], writes=[ZTd])
            g1 = {}
            for v in ((0, 1) if not last else (0,)):
                (g1[v],) = bcast_rows(st, [("row", modrow(L, v, 2))])
            wr = Rot(st, "wo", [128, 16, 512], BF16, 2)
            xr = Rot(st, "xp", [128, OW], F32, 3)
            tr = Rot(st, "tp", [128, OW], F32, 3)
            sr = Rot(st, "sg", [128, 256], F32, 3)
            def wload(nt):
                w, wd = wr.get()
                if glu:
                    p.dma(w[:, :, 0:256], wsrc[:, nt * 256:(nt + 1) * 256].rearrange("(k q) n -> q k n", q=128), writes=[wd], q="pool")
                    p.dma(w[:, :, 256:512], wsrc[:, D + nt * 256:D + (nt + 1) * 256].rearrange("(k q) n -> q k n", q=128), writes=[wd], q="pool")
                else:
                    p.dma(w[:, :, :], wsrc[:, nt * 512:(nt + 1) * 512].rearrange("(k q) n -> q k n", q=128), writes=[wd], q="pool")
                return w, wd
            nxt = wload(0)
            for nt in range(D // OW):
                w, wd = nxt
                if nt + 1 < D // OW:
                    nxt = wload(nt + 1)
                for (z0, ntok, isctx, rowfn) in tiles:
                    if last and isctx:
                        continue
                    G1t, G1d = g1[isctx]
                    bk, bkd = bankA()
                    p.mm([(lambda e, k=k: e.matmul(bk[0:ntok, :], lhsT=ZT[:, k, z0:z0 + ntok], rhs=w[:, k, :],
                                                  start=(k == 0), stop=(k == 15))) for k in range(16)], reads=[wd, ZTd], writes=[bkd])
                    cs = slice(nt * OW, (nt + 1) * OW)
                    rows = rowfn(cs)
                    xp, xpd = xr.get()
                    for (p0, np_, ap_, dep_) in rows:
                        p.dma(xp[p0:p0 + np_, :], ap_, reads=[dep_], writes=[xpd])
                    tp, tpd = tr.get()
                    if glu:
                        sg, sgd = sr.get()
                        p.op("act", lambda e: e.activation(out=sg[0:ntok], in_=bk[0:ntok, 256:512], func=AF.Sigmoid), reads=[bkd], writes=[sgd])
                        p.op("dve", lambda e: e.tensor_tensor(out=tp[0:ntok], in0=bk[0:ntok, 0:256], in1=sg[0:ntok], op=ALU.mult),
                             reads=[bkd, sgd], writes=[tpd])
                        p.op("dve", lambda e: e.tensor_tensor(out=tp[0:ntok], in0=tp[0:ntok], in1=G1t[0:ntok, cs], op=ALU.mult),
                             reads=[tpd, G1d], writes=[tpd])
                    else:
                        p.op("dve", lambda e: e.tensor_tensor(out=tp[0:ntok], in0=bk[0:ntok, :], in1=G1t[0:ntok, cs], op=ALU.mult),
                             reads=[bkd, G1d], writes=[tpd])
                    p.op("dve", lambda e: e.tensor_tensor(out=tp[0:ntok], in0=tp[0:ntok], in1=xp[0:ntok], op=ALU.add), reads=[tpd, xpd], writes=[tpd])
                    for (p0, np_, ap_, dep_) in rows:
                        p.dma(ap_, tp[p0:p0 + np_, :], reads=[tpd], writes=[dep_], q="pool")

    def lru_layer(L, last):
        j = L // 2
        with p.phase() as st:
            HT = p.sb("HT", [128, 16, T], BF16, st)
            HTd = Dep()
            with p.phase() as st2:
                nb = nm_bufs(st2)
                xr = Rot(st2, "xt", [128, D], F32, 4)
                hr = Rot(st2, "hb", [128, D], BF16, 2)
                for isctx in (1, 0):
                    with p.phase() as st3:
                        (Mt, Md), (SHt, SHd) = bcast_rows(st3, [("g1p", norm1_g[L:L + 1, :], modrow(L, isctx, 1)), ("row", modrow(L, isctx, 0))])
                        for ti in (range(0, 2) if isctx else range(2, 18)):
                            xt, xd = xr.get()
                            p.dma(xt[:], XS[ti * 128:(ti + 1) * 128, :], reads=[XD[ti]], writes=[xd])
                            hb, hbd = hr.get()
                            norm_mod(nb, xt[:], xd, 128, Mt, Md, SHt, SHd, hb[:], hbd)
                            to_fm(hb, hbd, 128, HT, HTd, ti * 128)
            with p.phase() as st2:
                colT = p.sb("colT", [128, 16 * 16], F32, st2)
                colN = [0]
                stage = Rot(st2, "colstage", [16, 128], F32, 2)

                def colload(name, src_row):
                    t_, td = stage.get()
                    p.dma(t_[:], src_row.rearrange("(k q) -> k q", q=128), writes=[td])
                    bk, bkd = bankA()
                    p.op("pe", lambda e: e.transpose(bk[:, 0:16], t_[:, :], identf[0:16, 0:16]), reads=[td, cD], writes=[bkd])
                    o_ = colT[:, colN[0] * 16:(colN[0] + 1) * 16]
                    colN[0] += 1
                    od = Dep()
                    p.op("act", lambda e: e.activation(out=o_[:], in_=bk[:, 0:16], func=AF.Copy), reads=[bkd], writes=[od])
                    return o_, od
                cw = [colload("cw%d" % k, lru_conv_w[j, k, :]) for k in range(4)]
                cb = colload("cb", lru_conv_b[j, :])
                ba = [colload("ba%d" % d, lru_ba[j, d, :]) for d in range(2)]
                bx = [colload("bx%d" % d, lru_bx[j, d, :]) for d in range(2)]
                cl = []
                for d in range(2):
                    lm, lmd = colload("lam%d" % d, lru_lam[j, d, :])
                    p.op("act", lambda e: e.activation(out=lm[:], in_=lm[:], func=AF.Exp, scale=-1.0), reads=[lmd], writes=[lmd])
                    p.op("dve", lambda e: e.tensor_scalar(out=lm[:], in0=lm[:], scalar1=1.0, scalar2=None, op0=ALU.add), reads=[lmd], writes=[lmd])
                    p.op("act", lambda e: e.activation(out=lm[:], in_=lm[:], func=AF.Ln), reads=[lmd], writes=[lmd])
                    p.op("dve", lambda e: e.tensor_scalar(out=lm[:], in0=lm[:], scalar1=-8.0, scalar2=None, op0=ALU.mult), reads=[lmd], writes=[lmd])
                    cl.append((lm, lmd))
                W = T + 6
                OFFC, OFFL = 1, 260
                XB = p.sb("XB", [128, 2, W], F32, st2)
                XBd = [Dep(), Dep()]
                p.op("dve", lambda e: e.memset(XB[:, :, :], 0.0), writes=XBd)
                XC = p.sb("XC", [128, 2, T], F32, st2)
                XCd = [Dep(), Dep()]
                XCb = p.sb("XCb", [128, 2, T], BF16, st2)
                XCbd = [Dep(), Dep()]
                GG = p.sb("GG", [128, 2, T], BF16, st2)
                GGd = [Dep(), Dep()]
                YA = p.sb("YA", [128, 2, T], F32, st2)
                YAd = [Dep(), Dep()]
                wr = Rot(st2, "win", [128, 16, 128], BF16, 2)
                gwr = Rot(st2, "gw", [128, 2, 256], BF16, 4)
                RFr = Rot(st2, "RF", [128, T], F32, 2)
                IFr = Rot(st2, "IF", [128, T], F32, 2)
                AF2 = p.sb("AF2", [128, T], F32, st2)
                AF2d = Dep()
                subt = [(0, 256)] + [(256 + 512 * i, 512) for i in range(4)]

                def xboff(t0):
                    return (OFFC + t0) if t0 < 256 else (OFFL + t0 - 256)
                for hd in range(8):
                    for cc in range(2):
                        ch = hd * 2 + cc
                        for which in range(2):
                            w, wd = wr.get()
                            p.dma(w[:], lru_w_in[j, :, which * D + ch * 128: which * D + (ch + 1) * 128].rearrange("(k q) n -> q k n", q=128),
                                  writes=[wd], q="pool")
                            for (t0, n_) in subt:
                                bk, bkd = bankA()
                                p.mm([(lambda e, k=k: e.matmul(bk[:, 0:n_], lhsT=w[:, k, :], rhs=HT[:, k, t0:t0 + n_], start=(k == 0), stop=(k == 15)))
                                      for k in range(16)], reads=[wd, HTd], writes=[bkd])
                                if which == 0:
                                    p.op("act", lambda e: e.activation(out=GG[:, cc, t0:t0 + n_], in_=bk[:, 0:n_], func=AF.Gelu_apprx_tanh),
                                         reads=[bkd], writes=[GGd[cc]])
                                else:
                                    o0 = xboff(t0)
                                    p.op("act", lambda e: e.activation(out=XB[:, cc, o0:o0 + n_], in_=bk[:, 0:n_], func=AF.Copy),
                                         reads=[bkd], writes=[XBd[cc]])
                        for (s0, sn, off) in ((0, 256, OFFC), (256, 2048, OFFL)):
                            p.op("dve", lambda e: e.tensor_scalar(out=XC[:, cc, s0:s0 + sn], in0=XB[:, cc, off:off + sn], scalar1=cw[1][0][:, ch:ch + 1],
                                                                  scalar2=cb[0][:, ch:ch + 1], op0=ALU.mult, op1=ALU.add),
                                 reads=[XBd[cc], cw[1][1], cb[1]], writes=[XCd[cc]])
                            for (k, sh) in ((0, -1), (2, 1), (3, 2)):
                                p.op("dve", lambda e: e.scalar_tensor_tensor(out=XC[:, cc, s0:s0 + sn], in0=XB[:, cc, off + sh:off + sh + sn],
                                                                             scalar=cw[k][0][:, ch:ch + 1], in1=XC[:, cc, s0:s0 + sn],
                                                                             op0=ALU.mult, op1=ALU.add),
                                     reads=[XBd[cc], cw[k][1], XCd[cc]], writes=[XCd[cc]])
                        p.op("act", lambda e: e.activation(out=XCb[:, cc, :], in_=XC[:, cc, :], func=AF.Copy), reads=[XCd[cc]], writes=[XCbd[cc]])
                    for d in range(2):
                        gwa, gwad = gwr.get()
                        gwx, gwxd = gwr.get()
                        p.dma(gwa[:], lru_wa[j, d, hd].rearrange("(k q) n -> q k n", q=128), writes=[gwad], q="pool")
                        p.dma(gwx[:], lru_wx[j, d, hd].rearrange("(k q) n -> q k n", q=128), writes=[gwxd], q="pool")
                        for oc in range(2):
                            ch = hd * 2 + oc
                            RF, RFd = RFr.get()
                            IF, IFd = IFr.get()
                            for (t0, n_) in subt:
                                bR, bRd = bankA()
                                bI, bId = bankA()
                                p.mm([(lambda e, k=k: e.matmul(bR[:, 0:n_], lhsT=gwa[:, k, oc * 128:(oc + 1) * 128], rhs=XCb[:, k, t0:t0 + n_],
                                                              start=(k == 0), stop=(k == 1))) for k in range(2)], reads=[gwad, XCbd[0], XCbd[1]], writes=[bRd])
                                p.mm([(lambda e, k=k: e.matmul(bI[:, 0:n_], lhsT=gwx[:, k, oc * 128:(oc + 1) * 128], rhs=XCb[:, k, t0:t0 + n_],
                                                              start=(k == 0), stop=(k == 1))) for k in range(2)], reads=[gwxd, XCbd[0], XCbd[1]], writes=[bId])
                                p.op("act", lambda e: e.activation(out=RF[:, t0:t0 + n_], in_=bR[:, 0:n_], func=AF.Sigmoid, bias=ba[d][0][:, ch:ch + 1]),
                                     reads=[bRd, ba[d][1]], writes=[RFd])
                                p.op("act", lambda e: e.activation(out=IF[:, t0:t0 + n_], in_=bI[:, 0:n_], func=AF.Sigmoid, bias=bx[d][0][:, ch:ch + 1]),
                                     reads=[bId, bx[d][1]], writes=[IFd])
                            p.op("act", lambda e: e.activation(out=RF[:, :], in_=RF[:, :], func=AF.Exp, scale=cl[d][0][:, ch:ch + 1]), reads=[RFd, cl[d][1]], writes=[RFd])
                            p.op("dve", lambda e: e.tensor_tensor(out=AF2[:, :], in0=RF[:, :], in1=RF[:, :], op=ALU.mult), reads=[RFd], writes=[AF2d])
                            p.op("act", lambda e: e.activation(out=AF2[:, :], in_=AF2[:, :], func=AF.Sqrt, scale=-1.0, bias=1.0), reads=[AF2d], writes=[AF2d])
                            p.op("dve", lambda e: e.tensor_tensor(out=IF[:, :], in0=IF[:, :], in1=XC[:, oc, :], op=ALU.mult), reads=[IFd, XCd[oc]], writes=[IFd])
                            p.op("dve", lambda e: e.tensor_tensor(out=IF[:, :], in0=IF[:, :], in1=AF2[:, :], op=ALU.mult), reads=[IFd, AF2d], writes=[IFd])
                            if d == 0:
                                p.op("dve", lambda e: e.tensor_tensor_scan(out=YA[:, oc, :], data0=RF[:, :], data1=IF[:, :], initial=0.0,
                                                                           op0=ALU.mult, op1=ALU.add), reads=[RFd, IFd, YAd[oc]], writes=[YAd[oc]])
                            else:
                                p.op("dve", lambda e: e.tensor_tensor_scan(out=AF2[:, 0:256][:, ::-1], data0=RF[:, 0:256][:, ::-1], data1=IF[:, 0:256][:, ::-1],
                                                                           initial=0.0, op0=ALU.mult, op1=ALU.add), reads=[RFd, IFd, AF2d], writes=[AF2d])
                                p.op("dve", lambda e: e.tensor_tensor_scan(out=AF2[:, 256:T][:, ::-1], data0=RF[:, 256:T][:, ::-1], data1=IF[:, 256:T][:, ::-1],
                                                                           initial=AF2[:, 0:1], op0=ALU.mult, op1=ALU.add), reads=[RFd, IFd, AF2d], writes=[AF2d])
                                p.op("dve", lambda e: e.tensor_tensor(out=YA[:, oc, :], in0=YA[:, oc, :], in1=AF2[:, :], op=ALU.add),
                                     reads=[YAd[oc], AF2d], writes=[YAd[oc]])
                    for oc in range(2):
                        ch = hd * 2 + oc
                        p.op("dve", lambda e: e.tensor_tensor(out=GG[:, oc, :], in0=YA[:, oc, :], in1=GG[:, oc, :], op=ALU.mult),
                             reads=[YAd[oc], GGd[oc]], writes=[GGd[oc]])
                        p.dma(ZTD[:, ch, :], GG[:, oc, :], reads=[GGd[oc]], writes=[Dep()])
        tiles = []
        for ti in range(18):
            def rowfn(cs, ti=ti):
                return [(0, 128, XS[ti * 128:(ti + 1) * 128, cs], XD[ti])]
            tiles.append((ti * 128, 128, 1 if ti < 2 else 0, rowfn))
        out_proj(L, last, lru_w_out[j], False, tiles)

    def s5_col(c):
        w_ = c // 4
        return 32 + 128 * (w_ // 32) + 32 * (c % 4) + (w_ % 32)

    def s5_layer(L, last):
        j = L // 2
        TWO_PI = 2.0 * math.pi
        with p.phase() as st:
            nb = nm_bufs(st)
            xr = Rot(st, "xt", [128, D], F32, 4)
            Ab = p.sb("Ab", [128, 128, 8, 16], BF16, st)
            Abd = Dep()
            vst = Rot(st, "vst", [128, 8, 128], BF16, 3)
            for isctx in (1, 0):
                with p.phase() as st3:
                    (Mt, Md), (SHt, SHd) = bcast_rows(st3, [("g1p", norm1_g[L:L + 1, :], modrow(L, isctx, 1)), ("row", modrow(L, isctx, 0))])
                    for lt in ((0,) if isctx else (0, 1)):
                        npart = 32 if isctx else 128
                        col0 = 0 if isctx else 32 + 128 * lt
                        for jj in range(8):
                            xt, xd = xr.get()
                            if isctx:
                                src = XS[0:256, :].rearrange("(c j) d -> j c d", j=8)
                                p.dma(xt[0:16, :], src[jj, 0:16, :], reads=[XD[0]], writes=[xd])
                                p.dma(xt[16:32, :], src[jj, 16:32, :], reads=[XD[1]], writes=[xd])
                            else:
                                for q in range(4):
                                    r0 = 256 + (8 * q + jj) * 64 + 32 * lt
                                    p.dma(xt[32 * q:32 * q + 32, :], XS[r0:r0 + 32, :], reads=[XD[2 + (8 * q + jj) // 2]], writes=[xd])
                            norm_mod(nb, xt[0:npart], xd, npart, Mt, Md, SHt, SHd, Ab[0:npart, :, jj, :], Abd, v3=True)
                        for g8 in range(16):
                            bk, bkd = bankT()
                            for gg in range(8):
                                g = g8 * 8 + gg
                                p.op("pe", lambda e: e.transpose(bk[:, gg, 0:npart], Ab[0:npart, g, :, :].rearrange("p j h -> p (j h)"),
                                                                 ident[0:npart, 0:npart]), reads=[Abd, cD], writes=[bkd])
                            vs, vsd = vst.get()
                            if isctx:
                                vo, vi = vs[:, :, 0:npart], bk[:, :, 0:npart]
                            else:
                                vo = vs[:, :, :].rearrange("p g (wl q) -> p g q wl", q=4)
                                vi = bk[:, :, :].rearrange("p g (q wl) -> p g q wl", q=4)
                            if g8 % 2 == 0:
                                p.op("act", lambda e: e.activation(out=vo, in_=vi, func=AF.Copy), reads=[bkd], writes=[vsd])
                            else:
                                p.op("dve", lambda e: e.tensor_copy(vo, vi), reads=[bkd], writes=[vsd])
                            p.dma(VD[:, g8 * 8:(g8 + 1) * 8, col0:col0 + npart], vs[:, :, 0:npart], reads=[vsd], writes=[Dep()])

        GB = 32
        NS = 36
        TWO_PI = 2.0 * math.pi
        with p.phase() as st0:
            dcol = p.sb("dcol", [128, 128], F32, st0)
            dcd = Dep()
            for jj in range(8):
                p.dma(dcol[16 * jj:16 * jj + 16, :], s5_d[j, :].rearrange("(g h) -> h g", h=16), writes=[dcd], allow_slow_non_contiguous=True)
            mask = p.sb("mask", [128, 2, 128], F32, st0)
            mkd = Dep()
            for d in range(2):
                p.dma(mask[:, d, :], maskin[d], writes=[mkd])
            for gb in range(128 // GB):
                g0 = gb * GB
                with p.phase() as st:
                    Vb = p.sb("Vb", [128, GB, 288], BF16, st)
                    Vbd = [Dep() for _ in range(GB)]
                    p.dma(Vb[:], VD[:, g0:g0 + GB, :], reads=[Dep()], writes=Vbd)
                    T0b = [p.sb("T0b", [128, GB, 128], BF16, st) for _ in range(2)]
                    T0d = [[Dep() for _ in range(GB)] for _ in range(2)]
                    Bcb = [p.sb("Bcb", [128, GB, 128], BF16, st) for _ in range(2)]
                    Bcd = [[Dep() for _ in range(GB)] for _ in range(2)]
                    OR = p.sb("OR", [128, GB, 128], F32, st)
                    OI = p.sb("OI", [128, GB, 128], F32, st)
                    Od = Dep()
                    LRP = p.sb("LRP", [128, 9, 2, GB], F32, st)
                    LIP = p.sb("LIP", [128, 9, 2, GB], F32, st)
                    Ld = Dep()
                    with p.phase() as sp_:
                        def t2(name):
                            return p.sb(name, [128, GB], F32, sp_), Dep()

                        def dv(fn, reads, writes, eng="dve"):
                            p.op(eng, fn, reads=reads, writes=writes)
                        lre, lred = t2("lre")
                        lim, limd = t2("lim")
                        stp, stpd = t2("stp")
                        for d in range(2):
                            hs = slice(64 * d, 64 * d + 64)
                            p.dma(lre[hs, :], s5_lam_re[j, d, g0:g0 + GB, :].rearrange("g q -> q g"), writes=[lred], allow_slow_non_contiguous=True)
                            p.dma(lim[hs, :], s5_lam_im[j, d, g0:g0 + GB, :].rearrange("g q -> q g"), writes=[limd], allow_slow_non_contiguous=True)
                            p.dma(stp[hs, :], s5_log_step[j, d, g0:g0 + GB].partition_broadcast(64), writes=[stpd])
                        dv(lambda e: e.activation(out=stp[:], in_=stp[:], func=AF.Exp), [stpd], [stpd], "act")
                        dv(lambda e: e.tensor_scalar(out=lre[:], in0=lre[:], scalar1=-1e-4, scalar2=None, op0=ALU.min), [lred], [lred])
                        er, erd = t2("er")
                        ang, angd = t2("ang")
                        dv(lambda e: e.tensor_tensor(out=er[:], in0=lre[:], in1=stp[:], op=ALU.mult), [lred, stpd], [erd])
                        dv(lambda e: e.activation(out=er[:], in_=er[:], func=AF.Exp), [erd], [erd], "act")
                        dv(lambda e: e.tensor_tensor(out=ang[:], in0=lim[:], in1=stp[:], op=ALU.mult), [limd, stpd], [angd])
                        cs = []
                        for shift in (math.pi / 2.0, 0.0):
                            a_, ad_ = t2("a")
                            ki = p.sb("ki", [128, GB], I32, sp_)
                            kid = Dep()
                            kf, kfd = t2("kf")
                            m_, md_ = t2("m")
                            dv(lambda e: e.tensor_scalar(out=a_[:], in0=ang[:], scalar1=shift, scalar2=None, op0=ALU.add), [angd], [ad_])
                            dv(lambda e: e.tensor_scalar(out=ki[:], in0=a_[:], scalar1=1.0 / TWO_PI, scalar2=None, op0=ALU.mult), [ad_], [kid])
                            dv(lambda e: e.tensor_copy(kf[:], ki[:]), [kid], [kfd])
                            dv(lambda e: e.scalar_tensor_tensor(out=a_[:], in0=kf[:], scalar=-TWO_PI, in1=a_[:], op0=ALU.mult, op1=ALU.add), [kfd, ad_], [ad_])
                            dv(lambda e: e.tensor_scalar(out=m_[:], in0=a_[:], scalar1=math.pi, scalar2=None, op0=ALU.is_gt), [ad_], [md_])
                            dv(lambda e: e.scalar_tensor_tensor(out=a_[:], in0=m_[:], scalar=-TWO_PI, in1=a_[:], op0=ALU.mult, op1=ALU.add), [md_, ad_], [ad_])
                            dv(lambda e: e.tensor_scalar(out=m_[:], in0=a_[:], scalar1=-math.pi, scalar2=None, op0=ALU.is_lt), [ad_], [md_])
                            dv(lambda e: e.scalar_tensor_tensor(out=a_[:], in0=m_[:], scalar=TWO_PI, in1=a_[:], op0=ALU.mult, op1=ALU.add), [md_, ad_], [ad_])
                            dv(lambda e: e.activation(out=a_[:], in_=a_[:], func=AF.Sin), [ad_], [ad_], "act")
                            cs.append((a_, ad_))
                        (cosv, cosd), (sinv, sind) = cs
                        PW = p.sb("PW", [128, 9, 2, GB], F32, sp_)
                        MW = p.sb("MW", [128, 8, 2, GB], F32, sp_)
                        PWd = Dep()
                        dv(lambda e: e.memset(PW[:, 0, 0, :], 1.0), [], [PWd])
                        dv(lambda e: e.memset(PW[:, 0, 1, :], 0.0), [], [PWd])
                        dv(lambda e: e.memset(MW[:, 0, 0, :], 1.0), [], [PWd])
                        dv(lambda e: e.memset(MW[:, 0, 1, :], 0.0), [], [PWd])
                        dv(lambda e: e.tensor_tensor(out=PW[:, 1, 0, :], in0=er[:], in1=cosv[:], op=ALU.mult), [erd, cosd], [PWd])
                        dv(lambda e: e.tensor_tensor(out=PW[:, 1, 1, :], in0=er[:], in1=sinv[:], op=ALU.mult), [erd, sind], [PWd])
                        ar, ai = PW[:, 1, 0, :], PW[:, 1, 1, :]
                        q1, q1d = t2("q1")
                        q2, q2d = t2("q2")
                        dv(lambda e: e.tensor_tensor(out=q1[:], in0=ar, in1=ar, op=ALU.mult), [PWd], [q1d])
                        dv(lambda e: e.tensor_tensor(out=q2[:], in0=ai, in1=ai, op=ALU.mult), [PWd], [q2d])
                        dv(lambda e: e.tensor_tensor(out=q1[:], in0=q1[:], in1=q2[:], op=ALU.add), [q1d, q2d], [q1d])
                        dv(lambda e: e.reciprocal(out=q1[:], in_=q1[:]), [q1d], [q1d])
                        dv(lambda e: e.tensor_tensor(out=MW[:, 1, 0, :], in0=ar, in1=q1[:], op=ALU.mult), [PWd, q1d], [PWd])
                        dv(lambda e: e.scalar_tensor_tensor(out=MW[:, 1, 1, :], in0=ai, scalar=-1.0, in1=q1[:], op0=ALU.mult, op1=ALU.mult), [PWd, q1d], [PWd])

                        def cmul_s(dst, k, src, b):
                            xr_, xi_ = src[:, k - 1, 0, :], src[:, k - 1, 1, :]
                            br_, bi_ = b
                            dv(lambda e: e.tensor_tensor(out=q1[:], in0=xr_, in1=br_, op=ALU.mult), [PWd, q1d], [q1d])
                            dv(lambda e: e.tensor_tensor(out=q2[:], in0=xi_, in1=bi_, op=ALU.mult), [PWd, q2d], [q2d])
                            dv(lambda e: e.tensor_tensor(out=dst[:, k, 0, :], in0=q1[:], in1=q2[:], op=ALU.subtract), [q1d, q2d], [PWd])
                            dv(lambda e: e.tensor_tensor(out=q1[:], in0=xr_, in1=bi_, op=ALU.mult), [PWd, q1d], [q1d])
                            dv(lambda e: e.tensor_tensor(out=q2[:], in0=xi_, in1=br_, op=ALU.mult), [PWd, q2d], [q2d])
                            dv(lambda e: e.tensor_tensor(out=dst[:, k, 1, :], in0=q1[:], in1=q2[:], op=ALU.add), [q1d, q2d], [PWd])
                        for k in range(2, 9):
                            cmul_s(PW, k, PW, (ar, ai))
                        for k in range(2, 8):
                            cmul_s(MW, k, MW, (MW[:, 1, 0, :], MW[:, 1, 1, :]))
                        LW = p.sb("LW", [128, 9, 2, GB], F32, sp_)
                        dv(lambda e: e.memset(LW[:, 0, 0, :], 1.0), [PWd], [PWd])
                        dv(lambda e: e.memset(LW[:, 0, 1, :], 0.0), [PWd], [PWd])
                        dv(lambda e: e.tensor_copy(LW[:, 1, :, :], PW[:, 8, :, :]), [PWd], [PWd])
                        for k in range(2, 9):
                            cmul_s(LW, k, LW, (PW[:, 8, 0, :], PW[:, 8, 1, :]))
                        dv(lambda e: e.tensor_copy(LRP[:, :, 0, :], LW[:, :, 0, :]), [PWd], [Ld])
                        dv(lambda e: e.tensor_copy(LRP[:, :, 1, :], LW[:, :, 0, :]), [PWd], [Ld])
                        dv(lambda e: e.tensor_scalar(out=LIP[:, :, 0, :], in0=LW[:, :, 1, :], scalar1=-1.0, scalar2=None, op0=ALU.mult), [PWd], [Ld])
                        dv(lambda e: e.tensor_copy(LIP[:, :, 1, :], LW[:, :, 1, :]), [PWd], [Ld])
                        MWs = p.sb("MWs", [128, 8, 2, GB], F32, sp_)
                        PWy = p.sb("PWy", [128, 9, 2, GB], F32, sp_)
                        PWb = p.sb("PWb", [128, 8, 2, GB], F32, sp_)
                        dv(lambda e: e.tensor_copy(MWs[0:64], MW[0:64]), [PWd], [PWd])
                        dv(lambda e: e.tensor_copy(MWs[64:128], MW[64:128][:, ::-1, :, :]), [PWd], [PWd])
                        dv(lambda e: e.tensor_copy(PWy[0:64], PW[0:64]), [PWd], [PWd])
                        dv(lambda e: e.tensor_copy(PWy[64:128], PW[64:128][:, ::-1, :, :]), [PWd], [PWd])
                        dv(lambda e: e.tensor_copy(PWb[0:64], PW[0:64, 0:8][:, ::-1, :, :]), [PWd], [PWd])
                        dv(lambda e: e.tensor_copy(PWb[64:128], PW[64:128, 0:8]), [PWd], [PWd])
                        den, dend = t2("den")
                        am1, am1d = t2("am1")
                        cr, crd = t2("cr")
                        ci, cid = t2("ci")
                        dv(lambda e: e.tensor_tensor(out=den[:], in0=lre[:], in1=lre[:], op=ALU.mult), [lred], [dend])
                        dv(lambda e: e.tensor_tensor(out=q1[:], in0=lim[:], in1=lim[:], op=ALU.mult), [limd, q1d], [q1d])
                        dv(lambda e: e.tensor_tensor(out=den[:], in0=den[:], in1=q1[:], op=ALU.add), [dend, q1d], [dend])
                        dv(lambda e: e.reciprocal(out=den[:], in_=den[:]), [dend], [dend])
                        dv(lambda e: e.tensor_scalar(out=am1[:], in0=ar, scalar1=-1.0, scalar2=None, op0=ALU.add), [PWd], [am1d])
                        dv(lambda e: e.tensor_tensor(out=q1[:], in0=am1[:], in1=lre[:], op=ALU.mult), [am1d, lred, q1d], [q1d])
                        dv(lambda e: e.tensor_tensor(out=q2[:], in0=ai, in1=lim[:], op=ALU.mult), [PWd, limd, q2d], [q2d])
                        dv(lambda e: e.tensor_tensor(out=q1[:], in0=q1[:], in1=q2[:], op=ALU.add), [q1d, q2d], [q1d])
                        dv(lambda e: e.tensor_tensor(out=cr[:], in0=q1[:], in1=den[:], op=ALU.mult), [q1d, dend], [crd])
                        dv(lambda e: e.tensor_tensor(out=q1[:], in0=ai, in1=lre[:], op=ALU.mult), [PWd, lred, q1d], [q1d])
                        dv(lambda e: e.tensor_tensor(out=q2[:], in0=am1[:], in1=lim[:], op=ALU.mult), [am1d, limd, q2d], [q2d])
                        dv(lambda e: e.tensor_tensor(out=q1[:], in0=q1[:], in1=q2[:], op=ALU.subtract), [q1d, q2d], [q1d])
                        dv(lambda e: e.tensor_tensor(out=ci[:], in0=q1[:], in1=den[:], op=ALU.mult), [q1d, dend], [cid])

                        def t3(name):
                            return p.sb(name, [128, GB, 16], F32, sp_), Dep()
                        Br, Brd = t3("Br")
                        Bi, Bid = t3("Bi")
                        Cr, Crd = t3("Cr")
                        Ci, Cid = t3("Ci")
                        for d in range(2):
                            hs = slice(64 * d, 64 * d + 64)
                            p.dma(Br[hs], s5_b_re[j, d, g0:g0 + GB].rearrange("g q h -> q g h"), writes=[Brd])
                            p.dma(Bi[hs], s5_b_im[j, d, g0:g0 + GB].rearrange("g q h -> q g h"), writes=[Bid])
                        ctr = Rot(sp_, "ct", [128, 128], F32, 2)
                        for (Cdst, Cdd, csrc) in ((Cr, Crd, s5_c_re), (Ci, Cid, s5_c_im)):
                            for g8 in range(GB // 8):
                                ct, ctd = ctr.get()
                                for d in range(2):
                                    p.dma(ct[:, 64 * d:64 * d + 64], csrc[j, d, g0 + 8 * g8:g0 + 8 * g8 + 8].rearrange("g h q -> (g h) q"), writes=[ctd])
                                bk, bkd = bankA()
                                p.op("pe", lambda e: e.transpose(bk[:, 0:128], ct[:, :], identf[:, :]), reads=[ctd, cD], writes=[bkd])
                                p.op("act", lambda e: e.activation(out=Cdst[:, 8 * g8:8 * g8 + 8, :], in_=bk[:, 0:128].rearrange("p (g h) -> p g h", h=16),
                                                                   func=AF.Copy), reads=[bkd], writes=[Cdd])
                        T1 = p.sb("T1", [128, GB, 9, 16], F32, sp_)
                        T1d = Dep()

                        def cmul(dR, dI, dd, Pr, Pi, Pd, Xr_, Xi_, Xd, T1v, negI=False):
                            p.op("dve", lambda e: e.tensor_tensor(out=dR, in0=Xr_, in1=Pr, op=ALU.mult), reads=Pd + Xd, writes=dd)
                            p.op("pool", lambda e: e.tensor_tensor(out=T1v, in0=Xi_, in1=Pi, op=ALU.mult), reads=Pd + Xd, writes=[T1d])
                            p.op("dve", lambda e: e.tensor_tensor(out=dI, in0=Xi_, in1=Pr, op=ALU.mult), reads=Pd + Xd, writes=dd)
                            p.op("dve", lambda e: e.tensor_tensor(out=dR, in0=dR, in1=T1v, op=ALU.subtract), reads=dd + [T1d], writes=dd)
                            p.op("pool", lambda e: e.tensor_tensor(out=T1v, in0=Xr_, in1=Pi, op=ALU.mult), reads=Pd + Xd, writes=[T1d])
                            if negI:
                                p.op("dve", lambda e: e.scalar_tensor_tensor(out=dI, in0=dI, scalar=-1.0, in1=T1v, op0=ALU.mult, op1=ALU.subtract),
                                     reads=dd + [T1d], writes=dd)
                            else:
                                p.op("dve", lambda e: e.tensor_tensor(out=dI, in0=dI, in1=T1v, op=ALU.add), reads=dd + [T1d], writes=dd)

                        def slots(tab, S, ri):
                            return tab[:, :, ri, :].rearrange("p s g -> p g s").unsqueeze(3).to_broadcast([128, GB, S, 16])

                        def overs(x3, S):
                            return x3.unsqueeze(2).to_broadcast([128, GB, S, 16])
                        BbR, BbRd = t3("BbR")
                        BbI, BbId = t3("BbI")
                        cmul(BbR[:], BbI[:], [BbRd, BbId], cr[:].unsqueeze(2).to_broadcast([128, GB, 16]), ci[:].unsqueeze(2).to_broadcast([128, GB, 16]),
                             [crd, cid], Br[:], Bi[:], [Brd, Bid], T1[:, :, 0, :])
                        Bbd = [BbRd, BbId]
                        with p.phase() as sq:
                            XR = p.sb("XR", [128, GB, 8, 16], F32, sq)
                            XI = p.sb("XI", [128, GB, 8, 16], F32, sq)
                            Xd_ = Dep()
                            YR = p.sb("YR", [128, GB, 9, 16], F32, sq)
                            YI = p.sb("YI", [128, GB, 9, 16], F32, sq)
                            Yd_ = Dep()
                            cmul(XR[:], XI[:], [Xd_], slots(MWs, 8, 0), slots(MWs, 8, 1), [PWd], overs(BbR[:], 8), overs(BbI[:], 8), Bbd, T1[:, :, 0:8, :])
                            cmul(YR[:], YI[:], [Yd_], slots(PWy, 9, 0), slots(PWy, 9, 1), [PWd], overs(Cr[:], 9), overs(Ci[:], 9), [Crd, Cid], T1[:], negI=True)
                            tmr = Rot(sq, "tm", [128, 128], F32, 3)
                            for d in range(2):
                                hs = slice(64 * d, 64 * d + 64)
                                y0 = 0 if d == 0 else 1
                                for g in range(GB):
                                    bk, bkd = bankA()
                                    p.mm([lambda e: e.matmul(bk[:, 0:128], lhsT=XR[hs, g, :, :].rearrange("q j h -> q (j h)"),
                                                             rhs=YR[hs, g, y0:y0 + 8, :].rearrange("q j h -> q (j h)"), start=True, stop=False),
                                          lambda e: e.matmul(bk[:, 0:128], lhsT=XI[hs, g, :, :].rearrange("q j h -> q (j h)"),
                                                             rhs=YI[hs, g, y0:y0 + 8, :].rearrange("q j h -> q (j h)"), start=False, stop=True)],
                                         reads=[Xd_, Yd_], writes=[bkd])
                                    if d == 0:
                                        tm, tmd = tmr.get()
                                        p.op("dve", lambda e: e.tensor_tensor(out=tm[:], in0=bk[:, 0:128], in1=mask[:, d, :], op=ALU.mult),
                                             reads=[bkd, mkd], writes=[tmd])
                                        p.op("dve", lambda e: e.scalar_tensor_tensor(out=T0b[d][:, g, :], in0=identf[:, :], scalar=dcol[:, g0 + g:g0 + g + 1],
                                                                                     in1=tm[:], op0=ALU.mult, op1=ALU.add),
                                             reads=[tmd, dcd, cD], writes=[T0d[d][g]])
                                    else:
                                        p.op("dve", lambda e: e.tensor_tensor(out=T0b[d][:, g, :], in0=bk[:, 0:128], in1=mask[:, d, :], op=ALU.mult),
                                             reads=[bkd, mkd], writes=[T0d[d][g]])
                            for (hs, o0) in ((slice(0, 64), 1), (slice(64, 128), 0)):
                                p.op("act", lambda e: e.activation(out=OR[hs].rearrange("q g (j h) -> q g j h", h=16), in_=YR[hs, :, o0:o0 + 8, :], func=AF.Copy),
                                     reads=[Yd_], writes=[Od])
                                p.op("act", lambda e: e.activation(out=OI[hs].rearrange("q g (j h) -> q g j h", h=16), in_=YI[hs, :, o0:o0 + 8, :], func=AF.Copy),
                                     reads=[Yd_], writes=[Od])
                        with p.phase() as sq:
                            BR = p.sb("BR", [128, GB, 8, 16], F32, sq)
                            BI = p.sb("BI", [128, GB, 8, 16], F32, sq)
                            Bd_ = Dep()
                            cmul(BR[:], BI[:], [Bd_], slots(PWb, 8, 0), slots(PWb, 8, 1), [PWd], overs(BbR[:], 8), overs(BbI[:], 8), Bbd, T1[:, :, 0:8, :])
                            for d in range(2):
                                hs = slice(64 * d, 64 * d + 64)
                                for g in range(GB):
                                    bk, bkd = bankA()
                                    p.mm([lambda e: e.transpose(bk[:, 0:64], BR[hs, g, :, :].rearrange("q j h -> q (j h)"), identf[hs, hs]),
                                          lambda e: e.transpose(bk[:, 64:128], BI[hs, g, :, :].rearrange("q j h -> q (j h)"), identf[hs, hs])],
                                         reads=[Bd_, cD], writes=[bkd])
                                    p.op("act", lambda e: e.activation(out=Bcb[d][:, g, :], in_=bk[:, 0:128], func=AF.Copy), reads=[bkd], writes=[Bcd[d][g]])
                    XC = p.sb("XC", [128, 2, GB, 288], F32, st)
                    with p.phase() as sq:
                        xcd = Dep()
                        for g in range(GB):
                            for ri in range(2):
                                bk, bkd = bankA()
                                p.mm([lambda e: e.matmul(bk[0:64, 0:288], lhsT=Bcb[0][:, g, ri * 64:(ri + 1) * 64], rhs=Vb[:, g, :], start=True, stop=True),
                                      lambda e: e.matmul(bk[64:128, 0:32], lhsT=Bcb[1][:, g, ri * 64:(ri + 1) * 64], rhs=Vb[:, g, 0:32][:, ::-1],
                                                         start=True, stop=True),
                                      lambda e: e.matmul(bk[64:128, 32:288], lhsT=Bcb[1][:, g, ri * 64:(ri + 1) * 64], rhs=Vb[:, g, 32:288][:, ::-1],
                                                         start=True, stop=True)],
                                     reads=[Bcd[0][g], Bcd[1][g], Vbd[g]], writes=[bkd])
                                if ri == 0:
                                    p.op("act", lambda e: e.activation(out=XC[:, ri, g, :], in_=bk[:, 0:288], func=AF.Copy), reads=[bkd], writes=[xcd])
                                else:
                                    p.op("dve", lambda e: e.tensor_copy(XC[:, ri, g, :], bk[:, 0:288]), reads=[bkd], writes=[xcd])
                    with p.phase() as sq:
                        shp = [128, 2, GB, NS]
                        Tr = Rot(sq, "T", shp, F32, 2)
                        Bb_ = p.sb("Bq", shp, F32, sq)
                        Bqd = Dep()
                        SIN = p.sb("SIN", shp, F32, sq)
                        SINd = [Dep() for _ in range(NS)]
                        _r3 = {}

                        def Rot_get3(st_):
                            if "r" not in _r3:
                                _r3["r"] = Rot(st_, "m6", [128, 2, GB], F32, 4)
                            return _r3["r"].get()
                        slotD = [Dep() for _ in range(8)]

                        def lr(k):
                            return LRP[:, k, :, :].unsqueeze(3).to_broadcast(shp)

                        def li(k):
                            return LIP[:, k, :, :].unsqueeze(3).to_broadcast(shp)

                        def slot(k):
                            return XC[:, :, :, k::8]
                        tp_, tpd_ = None, None
                        for k in range(1, 9):
                            tk, tkd = Tr.get()
                            if k == 1:
                                p.op("act", lambda e: e.activation(out=tk[:], in_=slot(0), func=AF.Copy), reads=[slotD[0]], writes=[tkd])
                                p.op("dve", lambda e: e.memset(slot(0), 0.0), writes=[slotD[0]])
                            else:
                                p.op("pool", lambda e: e.tensor_tensor(out=tk[:], in0=tp_[:], in1=lr(1), op=ALU.mult), reads=[tpd_, Ld], writes=[tkd])
                                p.op("dve", lambda e: e.tensor_tensor(out=Bb_[:], in0=tp_[:, ::-1, :, :], in1=li(1), op=ALU.mult), reads=[tpd_, Ld], writes=[Bqd])
                                p.op("dve", lambda e: e.tensor_tensor(out=Bb_[:], in0=Bb_[:], in1=slot(k - 1), op=ALU.add), reads=[Bqd, slotD[k - 1]], writes=[Bqd])
                                p.op("dve", lambda e: e.tensor_tensor(out=tk[:], in0=tk[:], in1=Bb_[:], op=ALU.add), reads=[tkd, Bqd], writes=[tkd])
                                p.op("act", lambda e: e.activation(out=slot(k - 1), in_=tp_[:], func=AF.Copy), reads=[tpd_], writes=[slotD[k - 1]])
                            tp_, tpd_ = tk, tkd
                        Z, Zd = tp_, tpd_
                        NB6 = 6
                        sh6 = [128, 2, GB, NB6]
                        L8r = LRP[:, 8, :, :].unsqueeze(3).to_broadcast(sh6)
                        L8i = LIP[:, 8, :, :].unsqueeze(3).to_broadcast(sh6)
                        LLR = p.sb("LLR", [128, 7, 2, GB], F32, sq)
                        LLI = p.sb("LLI", [128, 7, 2, GB], F32, sq)
                        LLd = Dep()
                        llst = contextlib.ExitStack()
                        LL = p.sb("LL", [128, 7, 2, GB], F32, llst)
                        u1 = p.sb("u1", [128, GB], F32, llst)
                        u2 = p.sb("u2", [128, GB], F32, llst)
                        u1d, u2d = Dep(), Dep()
                        p.op("dve", lambda e: e.memset(LL[:, 0, 0, :], 1.0), writes=[LLd])
                        p.op("dve", lambda e: e.memset(LL[:, 0, 1, :], 0.0), writes=[LLd])
                        p.op("dve", lambda e: e.tensor_copy(LL[:, 1, 0, :], LRP[:, 8, 0, :]), reads=[Ld], writes=[LLd])
                        p.op("dve", lambda e: e.tensor_copy(LL[:, 1, 1, :], LIP[:, 8, 1, :]), reads=[Ld], writes=[LLd])
                        for k in range(2, 7):
                            xr_, xi_ = LL[:, k - 1, 0, :], LL[:, k - 1, 1, :]
                            br_, bi_ = LL[:, 1, 0, :], LL[:, 1, 1, :]
                            p.op("dve", lambda e: e.tensor_tensor(out=u1[:], in0=xr_, in1=br_, op=ALU.mult), reads=[LLd, u1d], writes=[u1d])
                            p.op("dve", lambda e: e.tensor_tensor(out=u2[:], in0=xi_, in1=bi_, op=ALU.mult), reads=[LLd, u2d], writes=[u2d])
                            p.op("dve", lambda e: e.tensor_tensor(out=LL[:, k, 0, :], in0=u1[:], in1=u2[:], op=ALU.subtract), reads=[u1d, u2d], writes=[LLd])
                            p.op("dve", lambda e: e.tensor_tensor(out=u1[:], in0=xr_, in1=bi_, op=ALU.mult), reads=[LLd, u1d], writes=[u1d])
                            p.op("dve", lambda e: e.tensor_tensor(out=u2[:], in0=xi_, in1=br_, op=ALU.mult), reads=[LLd, u2d], writes=[u2d])
                            p.op("dve", lambda e: e.tensor_tensor(out=LL[:, k, 1, :], in0=u1[:], in1=u2[:], op=ALU.add), reads=[u1d, u2d], writes=[LLd])
                        p.op("dve", lambda e: e.tensor_copy(LLR[:, :, 0, :], LL[:, :, 0, :]), reads=[LLd], writes=[LLd])
                        p.op("dve", lambda e: e.tensor_copy(LLR[:, :, 1, :], LL[:, :, 0, :]), reads=[LLd], writes=[LLd])
                        p.op("dve", lambda e: e.tensor_scalar(out=LLI[:, :, 0, :], in0=LL[:, :, 1, :], scalar1=-1.0, scalar2=None, op0=ALU.mult), reads=[LLd], writes=[LLd])
                        p.op("dve", lambda e: e.tensor_copy(LLI[:, :, 1, :], LL[:, :, 1, :]), reads=[LLd], writes=[LLd])
                        p.barrier()
                        llst.close()
                        SINall = Dep()

                        def sslot(k):
                            return SIN[:, :, :, k::6]

                        def zslot(k):
                            return Z[:, :, :, k::6]
                        w6 = [(p.sb("w6_%d" % i, sh6, F32, sq), Dep()) for i in range(3)]
                        p.op("dve", lambda e: e.memset(sslot(0), 0.0), writes=[SINall])
                        for k in range(1, NB6 + 1):
                            (ta, tad), (tb, tbd) = w6[0], w6[1]
                            if k == 1:
                                p.op("dve", lambda e: e.tensor_copy(ta[:], zslot(0)), reads=[Zd], writes=[tad])
                            else:
                                p.op("dve", lambda e: e.tensor_tensor(out=ta[:], in0=prev6, in1=L8r, op=ALU.mult), reads=[SINall, Ld, w6[2][1]], writes=[tad])
                                p.op("dve", lambda e: e.tensor_tensor(out=tb[:], in0=prev6[:, ::-1, :, :], in1=L8i, op=ALU.mult), reads=[SINall, Ld, w6[2][1]], writes=[tbd])
                                p.op("dve", lambda e: e.tensor_tensor(out=ta[:], in0=ta[:], in1=zslot(k - 1), op=ALU.add), reads=[tad, Zd], writes=[tad])
                                p.op("dve", lambda e: e.tensor_tensor(out=ta[:], in0=ta[:], in1=tb[:], op=ALU.add), reads=[tad, tbd], writes=[tad])
                            if k < NB6:
                                p.op("dve", lambda e: e.tensor_copy(sslot(k), ta[:]), reads=[tad], writes=[SINall])
                                prev6 = sslot(k)
                            else:
                                p.op("dve", lambda e: e.tensor_copy(w6[2][0][:], ta[:]), reads=[tad], writes=[w6[2][1]])
                        tot, totd = w6[2]
                        Eb = p.sb("Eb", sh6, F32, sq)
                        Ebd = Dep()
                        p.op("dve", lambda e: e.memset(Eb[:, :, :, 0], 0.0), writes=[Ebd])
                        for b_ in range(1, NB6):
                            m1, m1d = Rot_get3(sq)
                            m2, m2d = Rot_get3(sq)
                            p.op("dve", lambda e: e.tensor_tensor(out=m1[:], in0=Eb[:, :, :, b_ - 1], in1=LLR[:, 6, :, :], op=ALU.mult), reads=[Ebd, LLd], writes=[m1d])
                            p.op("dve", lambda e: e.tensor_tensor(out=m2[:], in0=Eb[:, ::-1, :, b_ - 1], in1=LLI[:, 6, :, :], op=ALU.mult), reads=[Ebd, LLd], writes=[m2d])
                            p.op("dve", lambda e: e.tensor_tensor(out=m1[:], in0=m1[:], in1=tot[:, :, :, b_ - 1], op=ALU.add), reads=[m1d, totd], writes=[m1d])
                            p.op("dve", lambda e: e.tensor_tensor(out=Eb[:, :, :, b_], in0=m1[:], in1=m2[:], op=ALU.add), reads=[m1d, m2d, Ebd], writes=[Ebd])
                        for k in range(NB6):
                            (ta, tad), (tb, tbd) = w6[0], w6[1]
                            lr6 = LLR[:, k, :, :].unsqueeze(3).to_broadcast(sh6)
                            li6 = LLI[:, k, :, :].unsqueeze(3).to_broadcast(sh6)
                            p.op("dve", lambda e: e.tensor_tensor(out=ta[:], in0=Eb[:], in1=lr6, op=ALU.mult), reads=[Ebd, LLd], writes=[tad])
                            p.op("dve", lambda e: e.tensor_tensor(out=tb[:], in0=Eb[:, ::-1, :, :], in1=li6, op=ALU.mult), reads=[Ebd, LLd], writes=[tbd])
                            p.op("dve", lambda e: e.tensor_tensor(out=ta[:], in0=ta[:], in1=tb[:], op=ALU.add), reads=[tad, tbd], writes=[tad])
                            p.op("dve", lambda e: e.tensor_tensor(out=sslot(k), in0=sslot(k), in1=ta[:], op=ALU.add), reads=[SINall, tad], writes=[SINall])
                        SINd = [SINall]
                        p.op("act", lambda e: e.activation(out=slot(0), in_=SIN[:], func=AF.Copy), reads=SINd, writes=[slotD[0]])
                        Ab_, Aqd = Tr.get()
                        for k in range(1, 8):
                            p.op("pool", lambda e: e.tensor_tensor(out=Ab_[:], in0=SIN[:], in1=lr(k), op=ALU.mult), reads=SINd + [Ld], writes=[Aqd])
                            p.op("dve", lambda e: e.tensor_tensor(out=Bb_[:], in0=SIN[:, ::-1, :, :], in1=li(k), op=ALU.mult), reads=SINd + [Ld], writes=[Bqd])
                            p.op("dve", lambda e: e.tensor_tensor(out=Bb_[:], in0=Bb_[:], in1=slot(k), op=ALU.add), reads=[Bqd, slotD[k]], writes=[Bqd])
                            p.op("dve", lambda e: e.tensor_tensor(out=slot(k), in0=Bb_[:], in1=Ab_[:], op=ALU.add), reads=[slotD[k], Aqd, Bqd], writes=[slotD[k]])
                    with p.phase() as sq:
                        tr_ = Rot(sq, "ty", [128, 288], F32, 3)
                        ur_ = Rot(sq, "tu", [128, 288], F32, 3)
                        for g in range(GB):
                            bk, bkd = bankA()
                            b2_, b2d = bankA()
                            p.mm([lambda e: e.matmul(bk[:, 0:288], lhsT=T0b[0][:, g, :], rhs=Vb[:, g, :], start=True, stop=False),
                                  lambda e: e.matmul(bk[:, 0:288], lhsT=T0b[1][:, g, :], rhs=Vb[:, g, :], start=False, stop=False),
                                  lambda e: e.matmul(bk[:, 0:288], lhsT=OR[0:64, g, :], rhs=XC[0:64, 0, g, :], start=False, stop=False),
                                  lambda e: e.matmul(bk[:, 0:288], lhsT=OI[0:64, g, :], rhs=XC[0:64, 1, g, :], start=False, stop=True)],
                                 reads=[T0d[0][g], T0d[1][g], Vbd[g], Od], writes=[bkd])
                            p.mm([lambda e: e.matmul(b2_[:, 0:32], lhsT=OR[64:128, g, :], rhs=XC[64:128, 0, g, 0:32][:, ::-1], start=True, stop=False),
                                  lambda e: e.matmul(b2_[:, 0:32], lhsT=OI[64:128, g, :], rhs=XC[64:128, 1, g, 0:32][:, ::-1], start=False, stop=False),
                                  lambda e: e.matmul(b2_[:, 32:288], lhsT=OR[64:128, g, :], rhs=XC[64:128, 0, g, 32:288][:, ::-1], start=True, stop=False),
                                  lambda e: e.matmul(b2_[:, 32:288], lhsT=OI[64:128, g, :], rhs=XC[64:128, 1, g, 32:288][:, ::-1], start=False, stop=True)],
                                 reads=[Od], writes=[b2d])
                            tu, tud = ur_.get()
                            ty, tyd = tr_.get()
                            p.op("act", lambda e: e.activation(out=tu[:], in_=b2_[:, 0:288], func=AF.Copy), reads=[b2d], writes=[tud])
                            p.op("dve", lambda e: e.tensor_tensor(out=ty[:], in0=bk[:, 0:288], in1=tu[:], op=ALU.add), reads=[bkd, tud], writes=[tyd])
                            p.op("act", lambda e: e.activation(out=Vb[:, g, 0:32], in_=ty[:, 0:32], func=AF.Gelu_apprx_tanh), reads=[tyd], writes=[Vbd[g]])
                            p.op("act", lambda e: e.activation(out=Vb[:, g, 32:288].rearrange("p (lt q wl) -> p lt wl q", lt=2, q=4),
                                                               in_=ty[:, 32:288].rearrange("p (lt wl q) -> p lt wl q", lt=2, q=4),
                                                               func=AF.Gelu_apprx_tanh), reads=[tyd], writes=[Vbd[g]])
                    p.dma(YD[:, g0:g0 + GB, :], Vb[:], reads=Vbd, writes=[Dep()])
        YDv = YD.rearrange("(j h) g c -> h g j c", h=16)
        for k in range(16):
            for gl in range(8):
                p.dma(ZTD[16 * gl:16 * gl + 16, k, :].rearrange("h (j c) -> h j c", j=8), YDv[:, 8 * k + gl, :, :], reads=[Dep()], writes=[Dep()])
        p.barrier()
        tiles = []
        for jj in range(8):
            def rowfn_c(cs, jj=jj):
                src = XS[0:256, :].rearrange("(c j) d -> j c d", j=8)
                return [(0, 16, src[jj, 0:16, cs], XD[0]), (16, 16, src[jj, 16:32, cs], XD[1])]
            tiles.append((jj * 288, 32, 1, rowfn_c))
        for lt in range(2):
            for jj in range(8):
                def rowfn_l(cs, jj=jj, lt=lt):
                    res = []
                    for q in range(4):
                        r0 = 256 + (8 * q + jj) * 64 + 32 * lt
                        res.append((32 * q, 32, XS[r0:r0 + 32, cs], XD[2 + (8 * q + jj) // 2]))
                    return res
                tiles.append((jj * 288 + 32 + 128 * lt, 128, 0, rowfn_l))
        out_proj(L, last, s5_w_glu[j], True, tiles)

    for L in range(nlayers):
        last = (L == NL - 1)
        if L % 2 == 0:
            lru_layer(L, last)
        else:
            s5_layer(L, last)
        ffn(L, last)

    with p.phase() as st:
        if debug:
            dxr = Rot(st, "dx", [128, D], F32, 2)
            for t in range(18):
                xt, xd = dxr.get()
                p.dma(xt[:], XS[t * 128:(t + 1) * 128, :], reads=[XD[t]], writes=[xd])
                p.dma(dbg[t * 128:(t + 1) * 128, :], xt[:], reads=[xd])
        fg_ = p.sb("fg", [1, D], F32, st)
        fgd = Dep()
        p.dma(fg_[:], final_g.rearrange("(o n) -> o n", o=1), writes=[fgd])
        Gt = p.sb("Gt", [128, D], F32, st)
        Gd = Dep()
        for k in range(4):
            bk, bkd = bankA()
            p.op("pe", lambda e: e.matmul(bk[:], lhsT=ones1[:], rhs=fg_[0:1, k * 512:(k + 1) * 512], start=True, stop=True), reads=[fgd, cD], writes=[bkd])
            p.op("act", lambda e: e.activation(out=Gt[:, k * 512:(k + 1) * 512], in_=bk[:], func=AF.Copy), reads=[bkd], writes=[Gd])
        nb = nm_bufs(st)
        jr, ssr, rsr = nb
        xr = Rot(st, "xt", [128, D], F32, 4)
        orr = Rot(st, "ot", [128, D], F32, 2)
        for ti in range(2, 18):
            junk, jd = jr.get()
            xt, xd = xr.get()
            p.dma(xt[:], XS[ti * 128:(ti + 1) * 128, :], reads=[XD[ti]], writes=[xd])
            ss, ssd = ssr.get()
            rs, rsd = rsr.get()
            ot, otd = orr.get()
            p.op("act", lambda e: e.activation(out=junk[:], in_=xt[:], func=AF.Square, accum_out=ss[:]), reads=[xd], writes=[jd, ssd])
            p.op("dve", lambda e: e.tensor_scalar(out=rs[:], in0=ss[:], scalar1=1.0 / D, scalar2=EPS, op0=ALU.mult, op1=ALU.add), reads=[ssd], writes=[rsd])
            p.op("act", lambda e: e.activation(out=rs[:], in_=rs[:], func=AF.Sqrt), reads=[rsd], writes=[rsd])
            p.op("dve", lambda e: e.reciprocal(out=rs[:], in_=rs[:]), reads=[rsd], writes=[rsd])
            p.op("dve", lambda e: e.scalar_tensor_tensor(out=ot[:], in0=xt[:], scalar=rs[:, 0:1], in1=Gt[:], op0=ALU.mult, op1=ALU.mult),
                 reads=[xd, rsd, Gd], writes=[otd])
            p.dma(out[(ti - 2) * 128:(ti - 1) * 128, :], ot[:], reads=[otd])
    p.finish()
    return p


def make_inputs(inputs):
    ident = np.eye(128, dtype=np.float32)
    jj = np.arange(128) // 16
    maskf = (jj[None, :] >= jj[:, None]).astype(np.float32)
    maskb = (jj[None, :] <= jj[:, None]).astype(np.float32)
    masks = np.stack([maskf, maskb]).astype(np.float32)
    shared = {k: np.ascontiguousarray(np.asarray(v, dtype=np.float32)) for k, v in inputs.items() if k not in ("x", "c", "ctx", "c_ctx")}
    maps = []
    for b in range(4):
        m = dict(shared)
        m["xin"] = np.ascontiguousarray(np.concatenate([inputs["ctx"][b], inputs["x"][b]], axis=0).astype(np.float32))
        m["cvec"] = np.ascontiguousarray(np.stack([inputs["c"][b], inputs["c_ctx"]]).astype(np.float32))
        m["identin"] = ident
        m["maskin"] = masks
        maps.append(m)
    return maps


def kernel(**inputs):
    inputs = {k: np.asarray(v) for k, v in inputs.items()}
    p = build()
    maps = make_inputs(inputs)
    res = run_bass_kernel_spmd(p.nc, maps, core_ids=list(range(4)))
    return np.stack([np.asarray(r["out"], dtype=np.float32) for r in res.results], axis=0)
```
